# Optimizing a Trainium2 kernel written in Bass

```python
import math
import jax
import jax.numpy as jnp
from jax import lax
import numpy as np

D_MODEL = 1024
BATCH = 8
SEQ = 4096
DEPTH = 4

CTX_LEN = 256
GRID_W = 64
N_MIXERS = 4
MIX_W = D_MODEL
GROUP_W = MIX_W // N_MIXERS
HG_DK = 64
HG_DV = 64
HG_HEADS = GROUP_W // HG_DV
HG_QK = HG_HEADS * HG_DK
HG_V = HG_HEADS * HG_DV
HG_CHUNK = 16
S5_GROUP_CH = 16
S5_GROUPS = GROUP_W // S5_GROUP_CH
S5_STATE = 64
S5_DT_MIN = 1e-3
S5_DT_MAX = 1e-1
HY_CH = GROUP_W
HY_ORDER = 2
HY_EMB = 33
HY_HID = 64
HY_TARGET = 1e-2
HY_FAST = 0.3
HY_SLOW = 1.5
HY_MAX_DECAY = math.log(HY_TARGET) / HY_FAST
HY_MIN_DECAY = math.log(HY_TARGET) / HY_SLOW
ATT_HD = 64
ATT_HEADS = GROUP_W // ATT_HD
ATT_KV = 2
ATT_GROUP = ATT_HEADS // ATT_KV
WINDOW = 128
ATT_BLOCK = 128
ATT_SCALE = 1.0 / math.sqrt(ATT_HD)
ROPE_BASE = 10000.0
D_FF = 2816
EPS = 1e-6
IN_SIZES = (HG_QK, HG_QK, HG_QK, HG_V, HG_V, GROUP_W, 3 * HY_CH,
            ATT_HEADS * ATT_HD, ATT_KV * ATT_HD, ATT_KV * ATT_HD)
IN_COLS = sum(IN_SIZES)

kernel_name = 'hybrid_parallel_groups_diffusion_trunk'


def rms_norm(x, w):
    xf = x.astype(jnp.float32)
    y = xf * lax.rsqrt(jnp.mean(xf * xf, axis=-1, keepdims=True) + EPS)
    return (y * w.astype(jnp.float32)).astype(x.dtype)


def ada_params(cond, w, b):
    m = jnp.dot(jax.nn.silu(cond), w) + b
    return jnp.split(m[:, None, :], 6, axis=-1)


def dwconv3(x, w, b):
    xp = jnp.pad(x, ((0, 0), (1, 1), (0, 0)))
    return xp[:, :-2] * w[0] + xp[:, 1:-1] * w[1] + xp[:, 2:] * w[2] + b


def flip(t):
    return jnp.flip(t, axis=1)


def split_projection(p):
    offsets = []
    acc = 0
    for s in IN_SIZES[:-1]:
        acc += s
        offsets.append(acc)
    return jnp.split(p, offsets, axis=-1)


def gla_chunk_scan(q, k, v, log_f, s0):
    b_, L, h, _ = q.shape
    dv = v.shape[-1]
    n = L // HG_CHUNK

    def chunks(t):
        return t.reshape(b_, n, HG_CHUNK, h, t.shape[-1]).transpose(0, 1, 3, 2, 4)

    q, k, v, g = chunks(q), chunks(k), chunks(v), chunks(log_f)
    cum = jnp.cumsum(g, axis=3)
    ref = cum[:, :, :, HG_CHUNK // 2 - 1:HG_CHUNK // 2]
    last = cum[:, :, :, -1:]
    scores = jnp.einsum('bnhtk,bnhsk->bnhts', q * jnp.exp(cum - ref), k * jnp.exp(ref - cum))
    lower_tri = jnp.tril(jnp.ones((HG_CHUNK, HG_CHUNK), dtype=bool))
    scores = jnp.where(lower_tri, scores, 0.0)
    o_intra = jnp.einsum('bnhts,bnhsv->bnhtv', scores, v)
    ds = jnp.einsum('bnhsk,bnhsv->nbhkv', k * jnp.exp(last - cum), v)
    decay = jnp.exp(last[:, :, :, 0]).transpose(1, 0, 2, 3)

    def step(s, inp):
        ds_c, dec_c = inp
        return dec_c[..., None] * s + ds_c, s

    s_last, s_in = lax.scan(step, s0, (ds, decay))
    o_inter = jnp.einsum('bnhtk,nbhkv->bnhtv', q * jnp.exp(cum), s_in)
    o = (o_intra + o_inter).transpose(0, 1, 3, 2, 4).reshape(b_, L, h, dv)
    return o, s_last


def hgrn2_direction(q, f_logit, v, lb, s0):
    z = f_logit.astype(jnp.float32)
    f = lb + (1.0 - lb) * jax.nn.sigmoid(z)
    k = (1.0 - lb) * jax.nn.sigmoid(-z)
    return gla_chunk_scan(q, k, v, jnp.log(f), s0)


def hgrn2_inputs(q, ff, fb, v):
    def heads(t, d):
        return t.astype(jnp.float32).reshape(t.shape[0], t.shape[1], HG_HEADS, d)
    return jax.nn.silu(heads(q, HG_DK)), heads(ff, HG_DK), heads(fb, HG_DK), heads(v, HG_DV)


def hgrn2_bidir(q, ff, fb, v, lb_f, lb_b, s0_f, s0_b):
    o_f, s_f = hgrn2_direction(q, ff, v, lb_f, s0_f)
    o_b, s_b = hgrn2_direction(flip(q), flip(fb), flip(v), lb_b, s0_b)
    return o_f + flip(o_b), s_f, s_b


def s5_discretize(a_re, a_im, log_step, b_re, b_im):
    a_re = a_re.astype(jnp.float32)
    a_im = a_im.astype(jnp.float32)
    b_re = b_re.astype(jnp.float32)
    b_im = b_im.astype(jnp.float32)
    dt = jnp.exp(log_step.astype(jnp.float32))[:, None]
    mag = jnp.exp(a_re * dt)
    ang = a_im * dt
    ab_re, ab_im = mag * jnp.cos(ang), mag * jnp.sin(ang)
    den = a_re * a_re + a_im * a_im
    nr, ni = ab_re - 1.0, ab_im
    fr = (nr * a_re + ni * a_im) / den
    fi = (ni * a_re - nr * a_im) / den
    bb_re = fr[..., None] * b_re - fi[..., None] * b_im
    bb_im = fr[..., None] * b_im + fi[..., None] * b_re
    return ab_re, ab_im, bb_re, bb_im


def complex_affine_combine(e1, e2):
    a1r, a1i, b1r, b1i = e1
    a2r, a2i, b2r, b2i = e2
    return (a2r * a1r - a2i * a1i, a2r * a1i + a2i * a1r,
            a2r * b1r - a2i * b1i + b2r, a2r * b1i + a2i * b1r + b2i)


def s5_direction(u, disc, x0_re, x0_im):
    ab_re, ab_im, bb_re, bb_im = disc
    L = u.shape[1]
    bu_re = jnp.einsum('blgh,gph->lbgp', u, bb_re)
    bu_im = jnp.einsum('blgh,gph->lbgp', u, bb_im)
    bu_re = bu_re.at[0].add(ab_re * x0_re - ab_im * x0_im)
    bu_im = bu_im.at[0].add(ab_re * x0_im + ab_im * x0_re)
    a_re = jnp.broadcast_to(ab_re, (L, 1) + ab_re.shape)
    a_im = jnp.broadcast_to(ab_im, (L, 1) + ab_im.shape)
    _, _, xr, xi = lax.associative_scan(complex_affine_combine, (a_re, a_im, bu_re, bu_im), axis=0)
    return xr, xi


def s5_readout(xr, xi, c_re, c_im):
    return jnp.einsum('lbgp,ghp->blgh', xr, c_re) - jnp.einsum('lbgp,ghp->blgh', xi, c_im)


def s5_groups(t):
    return t.astype(jnp.float32).reshape(t.shape[0], t.shape[1], S5_GROUPS, S5_GROUP_CH)


def s5_output(u, xf, xb, c_re, c_im, d, glu_w):
    b_, L = u.shape[:2]
    c_re = c_re.astype(jnp.float32)
    c_im = c_im.astype(jnp.float32)
    y = s5_readout(xf[0], xf[1], c_re[0], c_im[0]) + flip(s5_readout(xb[0], xb[1], c_re[1], c_im[1]))
    y = y.reshape(b_, L, GROUP_W) + d.astype(jnp.float32) * u.reshape(b_, L, GROUP_W)
    z = jax.nn.gelu(y)
    return z * jax.nn.sigmoid(jnp.dot(z, glu_w.astype(jnp.float32)))


def hyena_filters(L, w1, b1, freq, w2, b2, w3):
    f32 = jnp.float32
    t01 = jnp.linspace(0.0, 1.0, L, dtype=f32)[:, None]
    w = 2.0 * math.pi * jnp.arange(L, dtype=f32)[:, None] / L
    bands = (HY_EMB - 1) // 2
    fr = jnp.linspace(1e-4, bands - 1, bands, dtype=f32)[None, :]
    z = jnp.concatenate([t01, jnp.cos(fr * w), -jnp.sin(fr * w)], axis=-1)
    h = jnp.sin(freq[0].astype(f32) * (jnp.dot(z, w1.astype(f32)) + b1.astype(f32)))
    h = jnp.sin(freq[1].astype(f32) * (jnp.dot(h, w2.astype(f32)) + b2.astype(f32)))
    h = jnp.dot(h, w3.astype(f32)).reshape(L, HY_ORDER, 2, HY_CH)
    deltas = jnp.linspace(HY_MIN_DECAY, HY_MAX_DECAY, HY_CH, dtype=f32)
    h = h * jnp.exp(-t01[:, :, None, None] * jnp.abs(deltas))
    h_f, h_b = h[:, :, 0], h[:, :, 1]
    k2 = jnp.concatenate([h_f[:1] + h_b[:1], h_f[1:], jnp.zeros_like(h_f[:1]),
                          jnp.flip(h_b[1:], axis=0)], axis=0)
    return jnp.fft.rfft(k2, axis=0)


def hyena_mixer(p, conv_w, conv_b, k_fft, bias):
    L = p.shape[1]
    p = dwconv3(p, conv_w, conv_b).astype(jnp.float32)
    x1, x2, z = jnp.split(p, 3, axis=-1)
    bias = bias.astype(jnp.float32)
    for o, gate in enumerate((x1, x2)):
        zf = jnp.fft.rfft(z, n=2 * L, axis=1)
        conv = jnp.fft.irfft(zf * k_fft[None, :, o], n=2 * L, axis=1)[:, :L]
        z = gate * (conv + bias[o] * z)
    return z


def axial_rope(L):
    rows = L // GRID_W
    row = jnp.repeat(jnp.arange(rows), GRID_W).astype(jnp.float32)
    col = jnp.tile(jnp.arange(GRID_W), rows).astype(jnp.float32)
    axis_dim = ATT_HD // 2
    inv = 1.0 / (ROPE_BASE ** (jnp.arange(0, axis_dim, 2, dtype=jnp.float32) / axis_dim))
    ang = jnp.concatenate([row[:, None] * inv, col[:, None] * inv], axis=-1)
    return jnp.cos(ang), jnp.sin(ang)


def apply_rope(x, cos, sin):
    xr = x.reshape(x.shape[:-1] + (ATT_HD // 2, 2))
    x0, x1 = xr[..., 0], xr[..., 1]
    c, s = cos[None, :, None, :], sin[None, :, None, :]
    return jnp.stack([x0 * c - x1 * s, x0 * s + x1 * c], axis=-1).reshape(x.shape)


def att_heads(t, n):
    return t.astype(jnp.float32).reshape(t.shape[0], t.shape[1], n, ATT_HD)


def context_attention(q, k, v, sink):
    b_, lc = q.shape[:2]
    qg = q.reshape(b_, lc, ATT_KV, ATT_GROUP, ATT_HD)
    s = jnp.einsum('bqkgd,bskd->bkgqs', qg, k) * ATT_SCALE
    sink_l = jnp.broadcast_to(sink.astype(jnp.float32).reshape(1, ATT_KV, ATT_GROUP, 1, 1), s.shape[:-1] + (1,))
    p = jax.nn.softmax(jnp.concatenate([sink_l, s], axis=-1), axis=-1)[..., 1:]
    o = jnp.einsum('bkgqs,bskd->bqkgd', p, v)
    return o.reshape(b_, lc, ATT_HEADS * ATT_HD)


def latent_window_attention(q, k, v, k_ctx, v_ctx, sink):
    b_, L = q.shape[:2]
    nb = L // ATT_BLOCK
    qb = q.reshape(b_, nb, ATT_BLOCK, ATT_KV, ATT_GROUP, ATT_HD)

    def band_blocks(t):
        tp = jnp.pad(t, ((0, 0), (ATT_BLOCK, ATT_BLOCK), (0, 0), (0, 0)))
        tp = tp.reshape(b_, nb + 2, ATT_BLOCK, ATT_KV, ATT_HD)
        return jnp.concatenate([tp[:, :-2], tp[:, 1:-1], tp[:, 2:]], axis=2)

    kw, vw = band_blocks(k), band_blocks(v)
    s_win = jnp.einsum('bnqkgd,bnskd->bnkgqs', qb, kw) * ATT_SCALE
    s_ctx = jnp.einsum('bnqkgd,bckd->bnkgqc', qb, k_ctx) * ATT_SCALE
    blk = jnp.arange(nb)[:, None, None] * ATT_BLOCK
    qpos = blk + jnp.arange(ATT_BLOCK)[None, :, None]
    kpos = blk - ATT_BLOCK + jnp.arange(3 * ATT_BLOCK)[None, None, :]
    valid = (jnp.abs(kpos - qpos) <= WINDOW) & (kpos >= 0) & (kpos < L)
    s_win = jnp.where(valid[None, :, None, None], s_win, -jnp.inf)
    sink_l = jnp.broadcast_to(sink.astype(jnp.float32).reshape(1, 1, ATT_KV, ATT_GROUP, 1, 1),
                              s_ctx.shape[:-1] + (1,))
    p = jax.nn.softmax(jnp.concatenate([sink_l, s_ctx, s_win], axis=-1), axis=-1)
    lc = k_ctx.shape[1]
    o = (jnp.einsum('bnkgqc,bckd->bnqkgd', p[..., 1:1 + lc], v_ctx)
         + jnp.einsum('bnkgqs,bnskd->bnqkgd', p[..., 1 + lc:], vw))
    return o.reshape(b_, L, ATT_HEADS * ATT_HD)


def merge_groups(hg, s5, hy, at, gate, norm_w):
    b_, L = s5.shape[:2]
    m = jnp.stack([hg.reshape(b_, L, HG_V), s5, hy, at], axis=2)
    m = rms_norm(m, norm_w.reshape(N_MIXERS, GROUP_W))
    hg_part = m[:, :, 0] * jax.nn.silu(gate.astype(jnp.float32))
    return jnp.concatenate([hg_part, m[:, :, 1:].reshape(b_, L, (N_MIXERS - 1) * GROUP_W)], axis=-1)


def conv_ffn(h, w_up, conv_w, conv_b, w_down):
    a, v = jnp.split(jnp.dot(h, w_up), 2, axis=-1)
    a = dwconv3(a, conv_w, conv_b)
    return jnp.dot(jax.nn.silu(a) * v, w_down)


def setup_inputs(seed: int = 0) -> dict:
    key = jax.random.key(seed)
    keys = iter(jax.random.split(key, 48))

    def normal(shape, scale):
        return jax.random.normal(next(keys), shape, jnp.float32) * scale

    def gain(shape):
        return 1.0 + normal(shape, 0.05)

    D = D_MODEL
    G, P, HS = S5_GROUPS, S5_STATE, S5_GROUP_CH
    n_idx = jnp.arange(P, dtype=jnp.float32)
    return {
        'x': normal((BATCH, SEQ, D), 1.0),
        'c': normal((BATCH, D), 1.0),
        'ctx': normal((BATCH, CTX_LEN, D), 1.0),
        'c_ctx': normal((D,), 1.0),
        'ada_w': normal((DEPTH, D, 6 * D), 0.5 * D ** -0.5),
        'ada_b': normal((DEPTH, 6 * D), 0.02),
        'norm_mix_w': gain((DEPTH, D)),
        'norm_ffn_w': gain((DEPTH, D)),
        'w_in': normal((DEPTH, D, IN_COLS), D ** -0.5),
        'hg_lb_logits': normal((DEPTH, 2, HG_QK), 0.5),
        's5_a_re': -0.5 + normal((DEPTH, 2, G, P), 0.01),
        's5_a_im': math.pi * n_idx + normal((DEPTH, 2, G, P), 0.01),
        's5_log_step': jax.random.uniform(next(keys), (DEPTH, 2, G), jnp.float32,
                                          math.log(S5_DT_MIN), math.log(S5_DT_MAX)),
        's5_b_re': normal((DEPTH, G, P, HS), (2 * HS) ** -0.5),
        's5_b_im': normal((DEPTH, G, P, HS), (2 * HS) ** -0.5),
        's5_c_re': normal((DEPTH, 2, G, HS, P), (2 * P) ** -0.5),
        's5_c_im': normal((DEPTH, 2, G, HS, P), (2 * P) ** -0.5),
        's5_d': normal((DEPTH, GROUP_W), 1.0),
        's5_glu_w': normal((DEPTH, GROUP_W, GROUP_W), GROUP_W ** -0.5),
        'hy_conv_w': normal((DEPTH, 3, 3 * HY_CH), 3 ** -0.5),
        'hy_conv_b': normal((DEPTH, 3 * HY_CH), 0.02),
        'hy_w1': normal((DEPTH, HY_EMB, HY_HID), HY_EMB ** -0.5),
        'hy_b1': normal((DEPTH, HY_HID), 0.1),
        'hy_freq': 1.0 + normal((DEPTH, 2, HY_HID), 0.05),
        'hy_w2': normal((DEPTH, HY_HID, HY_HID), HY_HID ** -0.5),
        'hy_b2': normal((DEPTH, HY_HID), 0.1),
        'hy_w3': normal((DEPTH, HY_HID, HY_ORDER * 2 * HY_CH), 0.1 * HY_HID ** -0.5),
        'hy_bias': normal((DEPTH, HY_ORDER, HY_CH), 0.5),
        'att_sink': normal((DEPTH, ATT_HEADS), 1.0),
        'merge_norm_w': gain((DEPTH, MIX_W)),
        'w_out': normal((DEPTH, MIX_W, D), MIX_W ** -0.5),
        'ffn_w_up': normal((DEPTH, D, 2 * D_FF), D ** -0.5),
        'ffn_conv_w': normal((DEPTH, 3, D_FF), 3 ** -0.5),
        'ffn_conv_b': normal((DEPTH, D_FF), 0.02),
        'ffn_w_down': normal((DEPTH, D_FF, D), D_FF ** -0.5),
        'final_norm_w': gain((D,)),
    }


def reference(x, c, ctx, c_ctx, ada_w, ada_b, norm_mix_w, norm_ffn_w, w_in, hg_lb_logits,
              s5_a_re, s5_a_im, s5_log_step, s5_b_re, s5_b_im, s5_c_re, s5_c_im, s5_d, s5_glu_w,
              hy_conv_w, hy_conv_b, hy_w1, hy_b1, hy_freq, hy_w2, hy_b2, hy_w3, hy_bias,
              att_sink, merge_norm_w, w_out, ffn_w_up, ffn_conv_w, ffn_conv_b, ffn_w_down,
              final_norm_w):
    f32 = jnp.float32
    bsz, seq_len, _ = x.shape
    ctx_len = ctx.shape[1]
    sm = jax.nn.softmax(hg_lb_logits.astype(f32), axis=0)
    lower_bounds = jnp.cumsum(sm, axis=0) - sm[0:1]
    rope_cos, rope_sin = axial_rope(seq_len)
    hg_zero = jnp.zeros((bsz, HG_HEADS, HG_DK, HG_DV), f32)
    s5_zero = jnp.zeros((bsz, S5_GROUPS, S5_STATE), f32)
    xc = ctx
    for l in range(DEPTH):
        last = l == DEPTH - 1
        sh1, sc1, g1, sh2, sc2, g2 = ada_params(c, ada_w[l], ada_b[l])
        csh1, csc1, cg1, csh2, csc2, cg2 = ada_params(c_ctx[None, :], ada_w[l], ada_b[l])
        p_lat = jnp.dot(rms_norm(x, norm_mix_w[l]) * (1.0 + sc1) + sh1, w_in[l])
        p_ctx = jnp.dot(rms_norm(xc, norm_mix_w[l]) * (1.0 + csc1) + csh1, w_in[l])
        (lq, lff, lfb, li, lg, lu, lhy, ltq, ltk, ltv) = split_projection(p_lat)
        (cq, cff, cfb, ci, cgate, cu, chy, ctq, ctk, ctv) = split_projection(p_ctx)

        lb_f = lower_bounds[l, 0].reshape(HG_HEADS, HG_DK)
        lb_b = lower_bounds[l, 1].reshape(HG_HEADS, HG_DK)
        hg_ctx, st_f, st_b = hgrn2_bidir(*hgrn2_inputs(cq, cff, cfb, ci), lb_f, lb_b, hg_zero, hg_zero)
        hg_lat, _, _ = hgrn2_bidir(*hgrn2_inputs(lq, lff, lfb, li), lb_f, lb_b, st_f, st_b)

        disc_f = s5_discretize(s5_a_re[l, 0], s5_a_im[l, 0], s5_log_step[l, 0], s5_b_re[l], s5_b_im[l])
        disc_b = s5_discretize(s5_a_re[l, 1], s5_a_im[l, 1], s5_log_step[l, 1], s5_b_re[l], s5_b_im[l])
        uc, ul = s5_groups(cu), s5_groups(lu)
        xcf = s5_direction(uc, disc_f, s5_zero, s5_zero)
        xcb = s5_direction(flip(uc), disc_b, s5_zero, s5_zero)
        xlf = s5_direction(ul, disc_f, xcf[0][-1], xcf[1][-1])
        xlb = s5_direction(flip(ul), disc_b, xcb[0][-1], xcb[1][-1])
        s5_lat = s5_output(ul, xlf, xlb, s5_c_re[l], s5_c_im[l], s5_d[l], s5_glu_w[l])

        hy_params = (hy_w1[l], hy_b1[l], hy_freq[l], hy_w2[l], hy_b2[l], hy_w3[l])
        hy_lat = hyena_mixer(lhy, hy_conv_w[l], hy_conv_b[l], hyena_filters(seq_len, *hy_params), hy_bias[l])

        tq = apply_rope(att_heads(ltq, ATT_HEADS), rope_cos, rope_sin)
        tk = apply_rope(att_heads(ltk, ATT_KV), rope_cos, rope_sin)
        tv = att_heads(ltv, ATT_KV)
        ck, cv = att_heads(ctk, ATT_KV), att_heads(ctv, ATT_KV)
        at_lat = latent_window_attention(tq, tk, tv, ck, cv, att_sink[l])

        mix_lat = merge_groups(hg_lat, s5_lat, hy_lat, at_lat, lg, merge_norm_w[l]).astype(x.dtype)
        x_new = x + g1 * jnp.dot(mix_lat, w_out[l])
        x_new = x_new + g2 * conv_ffn(rms_norm(x_new, norm_ffn_w[l]) * (1.0 + sc2) + sh2,
                                      ffn_w_up[l], ffn_conv_w[l], ffn_conv_b[l], ffn_w_down[l])
        if not last:
            s5_ctx = s5_output(uc, xcf, xcb, s5_c_re[l], s5_c_im[l], s5_d[l], s5_glu_w[l])
            hy_ctx = hyena_mixer(chy, hy_conv_w[l], hy_conv_b[l], hyena_filters(ctx_len, *hy_params), hy_bias[l])
            at_ctx = context_attention(att_heads(ctq, ATT_HEADS), ck, cv, att_sink[l])
            mix_ctx = merge_groups(hg_ctx, s5_ctx, hy_ctx, at_ctx, cgate, merge_norm_w[l]).astype(xc.dtype)
            xc = xc + cg1 * jnp.dot(mix_ctx, w_out[l])
            xc = xc + cg2 * conv_ffn(rms_norm(xc, norm_ffn_w[l]) * (1.0 + csc2) + csh2,
                                     ffn_w_up[l], ffn_conv_w[l], ffn_conv_b[l], ffn_w_down[l])
        x = x_new
    return rms_norm(x, final_norm_w)
```

```python
import contextlib
import math
import numpy as np
import concourse.bass as bass
import concourse.mybir as mybir
from concourse.bass_utils import run_bass_kernel_spmd

F32 = mybir.dt.float32
BF16 = mybir.dt.bfloat16
I32 = mybir.dt.int32
ALU = mybir.AluOpType
AF = mybir.ActivationFunctionType

D = 1024
DEPTH = 4
LC = 256
LL = 4096
T = LC + LL
DFF = 2816
EPS = 1e-6
NCORES = 8
TILES = [(0, 256, 1)] + [(256 + 512 * i, 512, 0) for i in range(8)]
TP = T + 3


def pcol(c):
    return c + 1 if c < LC else c + 2


ENGS = ("pe", "act", "dve", "pool", "sp")
NDMA = 12


class Prog:
    def __init__(self, nc):
        self.nc = nc
        self.ops = {e: [] for e in ENGS}
        self.cnt = {e: 0 for e in ENGS}
        self.seen = {e: {} for e in ENGS}
        self.lastw = {}
        self.readers = {}
        self.dma_i = {e: 0 for e in ENGS}
        self.dma_val = {}
        self.sems = {}

    def _wait(self, eng, tok):
        sk, v = tok
        if sk == "pe" and eng == "pe":
            return
        if self.seen[eng].get(sk, 0) >= v:
            return
        self.seen[eng][sk] = v
        self.ops[eng].append(("wait", sk, v))

    def _deps(self, eng, reads, writes):
        for k in reads:
            if k in self.lastw:
                self._wait(eng, self.lastw[k])
        for k in writes:
            if k in self.lastw:
                self._wait(eng, self.lastw[k])
            for t in self.readers.get(k, ()):
                self._wait(eng, t)

    def _commit(self, tok, reads, writes):
        for k in reads:
            lst = self.readers.setdefault(k, [])
            lst.append(tok)
            if len(lst) > 32:
                d = {}
                for sk, v in lst:
                    d[sk] = max(d.get(sk, 0), v)
                self.readers[k] = list(d.items())
        for k in writes:
            self.lastw[k] = tok
            self.readers[k] = []

    def op(self, eng, fn, reads=(), writes=()):
        self._deps(eng, reads, writes)
        self.cnt[eng] += 1
        tok = (eng, self.cnt[eng])
        self.ops[eng].append(("op", fn))
        self._commit(tok, reads, writes)
        return tok

    def dma(self, eng, out, in_, reads=(), writes=(), **kw):
        self._deps(eng, reads, writes)
        i = self.dma_i[eng]
        self.dma_i[eng] += 1
        slot = (eng, i % NDMA)
        prev = self.dma_val.get(slot, 0)
        if prev:
            self._wait(eng, (slot, prev))
        val = prev + 16
        self.dma_val[slot] = val
        self.ops[eng].append(("dma", out, in_, slot, kw))
        tok = (slot, val)
        self._commit(tok, reads, writes)
        return tok

    def finish(self, toks):
        for t in toks:
            self._wait("sp", t)

    def emit(self):
        nc = self.nc
        with contextlib.ExitStack() as st:
            semkeys = list(ENGS)
            for e in ENGS:
                for s in range(NDMA):
                    if (e, s) in self.dma_val:
                        semkeys.append((e, s))
            for sk in semkeys:
                nm = sk if isinstance(sk, str) else "d_%s_%d" % sk
                self.sems[sk] = st.enter_context(nc.semaphore("s_" + nm))
            block = st.enter_context(nc.Block())

            def run(eng_obj, lst, ename):
                sem_own = self.sems[ename]
                for o in lst:
                    if o[0] == "wait":
                        eng_obj.wait_ge(self.sems[o[1]], o[2])
                    elif o[0] == "op":
                        o[1](eng_obj).then_inc(sem_own, 1)
                    else:
                        _, out, in_, slot, kw = o
                        eng_obj.dma_start(out=out, in_=in_, **kw).then_inc(self.sems[slot], 16)

            @block.tensor
            def _(e):
                run(e, self.ops["pe"], "pe")

            @block.scalar
            def _(e):
                run(e, self.ops["act"], "act")

            @block.vector
            def _(e):
                run(e, self.ops["dve"], "dve")

            @block.gpsimd
            def _(e):
                run(e, self.ops["pool"], "pool")

            @block.sync
            def _(e):
                run(e, self.ops["sp"], "sp")


def _barrier(P):
    toks = [(e, P.cnt[e]) for e in ENGS if P.cnt[e] > 0]
    toks += [(slot, v) for slot, v in P.dma_val.items()]
    for e in ENGS:
        for t in toks:
            P._wait(e, t)


ARENA_WORDS = 51200
XN_WORDS = 8 * TP // 2 + 4


class Builder:
    def __init__(self, depth=DEPTH, mixers=("hg", "s5", "hy", "at"), debug=()):
        self.depth = depth
        self.mixers = mixers
        self.debug = debug
        self.nc = bass.Bass("TRN2", target_bir_lowering=False)
        self.P = Prog(self.nc)
        self.st = contextlib.ExitStack()
        self.din = {}
        self.out_toks = []

    def inp(self, name, shape, dt=F32):
        self.din[name] = self.nc.dram_tensor(name, list(shape), dt, kind="ExternalInput").ap()
        return self.din[name]

    def dram(self, name, shape, dt=F32, kind="Internal"):
        return self.nc.dram_tensor(name, list(shape), dt, kind=kind).ap()

    def phase(self):
        _barrier(self.P)
        self.ptr = self.base
        self.nphase += 1

    def alloc(self, words, name):
        a = self.ptr
        self.ptr += (words + 1) // 2 * 2
        assert self.ptr <= ARENA_WORDS, (name, self.ptr)
        key = "%s@%d" % (name, self.nphase)
        return self.arena[:, a:a + words], key

    def f32(self, shape, name):
        n = int(np.prod(shape))
        v, k = self.alloc(n, name)
        if len(shape) == 2:
            v = v.rearrange("p (a b) -> p a b", a=shape[0])
        elif len(shape) == 3:
            v = v.rearrange("p (a b c) -> p a b c", a=shape[0], b=shape[1])
        return v, k

    def bf(self, shape, name):
        n = int(np.prod(shape))
        v, k = self.alloc((n + 1) // 2, name)
        v = v.bitcast(BF16)[:, 0:n]
        if len(shape) == 2:
            v = v.rearrange("p (a b) -> p a b", a=shape[0])
        elif len(shape) == 3:
            v = v.rearrange("p (a b c) -> p a b c", a=shape[0], b=shape[1])
        return v, k

    def act(self, out, in_, func, reads, writes, **kw):
        return self.P.op("act", lambda e: e.activation(out, in_, func, **kw), reads, writes)

    def tt(self, eng, out, a, b, op, reads, writes):
        return self.P.op(eng, lambda e: e.tensor_tensor(out, a, b, op), reads, writes)

    def ts(self, eng, out, a, s1, s2, op0, op1, reads, writes):
        if s2 is None:
            return self.P.op(eng, lambda e: e.tensor_scalar(out, a, s1, None, op0), reads, writes)
        return self.P.op(eng, lambda e: e.tensor_scalar(out, a, s1, s2, op0, op1), reads, writes)

    def stt(self, eng, out, a, s, b, op0, op1, reads, writes):
        return self.P.op(eng, lambda e: e.scalar_tensor_tensor(out, a, s, b, op0, op1), reads, writes)

    def cp(self, eng, out, in_, reads, writes):
        return self.P.op(eng, lambda e: e.tensor_copy(out, in_), reads, writes)

    def mm(self, out, pairs, reads, writes):
        n = len(pairs)
        for i, (l, r) in enumerate(pairs):
            self.P.op("pe", lambda e, l=l, r=r, i=i: e.matmul(out, l, r, start=(i == 0), stop=(i == n - 1)),
                      reads, writes)

    def dma(self, out, in_, reads, writes, eng="sp"):
        return self.P.dma(eng, out, in_, reads, writes)

    def build(self):
        nc, P = self.nc, self.P
        inp = self.inp
        xin = inp("xin", [D, T])
        cvec = inp("cvec", [128, 8, 2])
        ada_w = inp("ada_w", [DEPTH, D, 6 * D])
        ada_b = inp("ada_b", [DEPTH, 128, 48])
        nmw = inp("nmw", [DEPTH, 128, 8])
        nfw = inp("nfw", [DEPTH, 128, 8])
        fnw = inp("fnw", [128, 8])
        self.w_in = inp("w_in", [DEPTH, D, 2816])
        self.w_out = inp("w_out", [DEPTH, D, D])
        self.w_up = inp("w_up", [DEPTH, D, 2 * DFF])
        self.w_dn = inp("w_dn", [DEPTH, DFF, D])
        fcw = inp("fcw", [DEPTH, 128, 3, 22])
        fcb = inp("fcb", [DEPTH, 128, 22])
        mnw = inp("mnw", [DEPTH, 128, 8])
        self.declare_mixer_inputs()
        out = self.dram("out", [D, LL], kind="ExternalOutput")
        X = self.dram("Xs", [D, T])
        self.X = X
        self.MIX = self.dram("MIXs", [1536, T], kind=("ExternalOutput" if getattr(self, "mix_debug", False) else "Internal"))
        self.H = self.dram("Hs", [DFF, T], BF16)
        self.dbg = {}
        for name, shape in self.debug:
            self.dbg[name] = self.dram("dbg_" + name, shape, kind="ExternalOutput")

        self.arena = self.st.enter_context(nc.sbuf_tensor("arena", [128, ARENA_WORDS], F32))
        self.q = [self.st.enter_context(nc.psum_tensor("q%d" % i, [128, 1024], F32)) for i in range(4)]
        self.qk = ["q%d" % i for i in range(4)]
        self.ptr = 0
        self.nphase = 0
        xnv, _ = self.alloc(XN_WORDS, "xn")
        self.xn = xnv.bitcast(BF16)[:, 0:8 * TP].rearrange("p (k n) -> p k n", k=8)
        ob, _ = self.alloc(64, "ones")
        self.ones_bf = ob.bitcast(BF16)
        self.mods, _ = self.f32([48, 2], "mods")
        self.gain1, _ = self.f32([8, 2], "gain1")
        self.gain2, _ = self.f32([8, 2], "gain2")
        cc, _ = self.f32([8, 2], "cc")
        scc, _ = self.f32([8, 2], "scc")
        adab, _ = self.alloc(48, "adab")
        nmw_t, _ = self.alloc(8, "nmw_t")
        nfw_t, _ = self.alloc(8, "nfw_t")
        fnw_t, _ = self.alloc(8, "fnw_t")
        self.mnw_t, _ = self.alloc(8, "mnw_t")
        self.fcw_t, _ = self.f32([3, 22], "fcw_t")
        self.fcb_t, _ = self.alloc(22, "fcb_t")
        self.alloc_mixer_persistent()
        self.base = self.ptr
        mods, gain1, gain2 = self.mods, self.gain1, self.gain2
        q, qk = self.q, self.qk

        P.op("pool", lambda e: e.memset(self.ones_bf, 1.0), [], ["ones_bf"])
        P.op("pool", lambda e: e.memset(self.xn, 0.0), [], ["xn"])
        self.dma(cc, cvec, [], ["cc"])
        self.dma(fnw_t, fnw, [], ["fnw_t"])
        self.act(scc, cc, AF.Silu, ["cc"], ["scc"])

        src = xin
        for l in range(self.depth):
            self.phase()
            self.dma(adab, ada_b[l], [], ["adab"])
            self.dma(nmw_t, nmw[l], [], ["nmw_t"])
            self.dma(nfw_t, nfw[l], [], ["nfw_t"])
            self.dma(self.mnw_t, mnw[l], [], ["mnw_t"])
            self.dma(self.fcw_t, fcw[l], [], ["fcw_t"])
            self.dma(self.fcb_t, fcb[l], [], ["fcb_t"])
            stg = [self.f32([8, 512], "astage%d" % i) for i in range(2)]
            awv = ada_w[l].rearrange("(k p) n -> p k n", p=128)
            for og in range(12):
                sv, sk = stg[og % 2]
                self.dma(sv, awv[:, :, og * 512:(og + 1) * 512], [], [sk])
                pa, pk = q[og % 2], qk[og % 2]
                for j in range(4):
                    self.mm(pa[:, 2 * j:2 * j + 2],
                            [(sv[:, k, 128 * j:128 * j + 128], scc[:, k, :]) for k in range(8)], [sk, "scc"], [pk])
                self.tt("dve", mods[:, og * 4:(og + 1) * 4, :], pa[:, 0:8].rearrange("p (j v) -> p j v", v=2),
                        adab[:, og * 4:(og + 1) * 4].unsqueeze(2).to_broadcast([128, 4, 2]), ALU.add,
                        [pk, "adab"], ["mods"])
            self.stt("dve", gain1, mods[:, 8:16, :], 1.0, nmw_t.unsqueeze(2).to_broadcast([128, 8, 2]),
                     ALU.add, ALU.mult, ["mods", "nmw_t"], ["gain1"])
            self.stt("dve", gain2, mods[:, 32:40, :], 1.0, nfw_t.unsqueeze(2).to_broadcast([128, 8, 2]),
                     ALU.add, ALU.mult, ["mods", "nfw_t"], ["gain2"])
            self.norm_phase(src, gain1, "gain1", 0)
            self.mixers_phase(l)
            self.phase_o(l, src)
            src = X
            self.norm_phase(X, gain2, "gain2", 24)
            self.ffn_up_phase(l)
            self.ffn_down_phase(l)
        self.final_norm(fnw_t, out)
        P.finish(self.out_toks)
        P.emit()
        self.st.close()
        return nc

    def rms_rstd(self, xtile, xkey, n, nchunks, rstd, rt, sqb, pq, pqk, eps=EPS):
        self.act(sqb[:, 0:nchunks, 0:n], xtile[:, 0:nchunks, 0:n], AF.Square, [xkey], ["sqb"])
        self.mm(pq[:, 0:n], [(self.ones_bf, sqb[:, k, 0:n]) for k in range(nchunks)], ["sqb", "ones_bf"], [pqk])
        self.act(rt[:, 0:n], pq[:, 0:n], AF.Sqrt, [pqk], ["rt"], bias=eps, scale=1.0 / (128 * nchunks))
        self.P.op("dve", lambda e: e.reciprocal(rstd[:, 0:n], rt[:, 0:n]), ["rt"], ["rstd"])

    def norm_phase(self, src, gain, gkey, shift_base):
        self.phase()
        xts = [self.f32([8, 512], "xt%d" % i) for i in range(2)]
        sqb, _ = self.bf([8, 512], "sqb")
        rstd, _ = self.alloc(512, "rstd")
        rt, _ = self.alloc(512, "rt")
        for pc_ in (0, LC + 1, TP - 1):
            self.P.op("pool", lambda e, pc_=pc_: e.memset(self.xn[:, :, pc_:pc_ + 1], 0.0), [], ["xn"])
        for ti, (c0, n, isc) in enumerate(TILES):
            xtile, xk = xts[ti % 2]
            self.dma(xtile[:, :, 0:n], src[:, c0:c0 + n].rearrange("(k p) n -> p k n", p=128), [("X", ti)], [xk])
            pq, pqk = self.q[ti % 2], self.qk[ti % 2]
            self.rms_rstd(xtile, xk, n, 8, rstd, rt, sqb, pq, pqk)
            self.tt("dve", xtile[:, :, 0:n], xtile[:, :, 0:n], rstd[:, 0:n].unsqueeze(1).to_broadcast([128, 8, n]),
                    ALU.mult, [xk, "rstd"], [xk])
            pc = pcol(c0)
            for k in range(8):
                self.act(self.xn[:, k, pc:pc + n], xtile[:, k, 0:n], AF.Identity, [xk, gkey, "mods"], ["xn"],
                         scale=gain[:, k, isc:isc + 1], bias=self.mods[:, shift_base + k, isc:isc + 1])

    def load_w_chunk(self, wsrc_cols, stg, wbf, use_act):
        sv, sk = stg
        wv, wk = wbf
        self.dma(sv, wsrc_cols.rearrange("(k p) n -> p k n", p=128), [], [sk])
        if use_act:
            self.act(wv, sv, AF.Copy, [sk], [wk])
        else:
            self.cp("pool", wv, sv, [sk], [wk])

    def proj_rows(self, wv, wk, dst_fn, m=128):
        for ti, (c0, n, isc) in enumerate(TILES):
            pq, pqk = self.q[2 + ti % 2], self.qk[2 + ti % 2]
            pc = pcol(c0)
            self.mm(pq[0:m, 0:n], [(wv[:, k, 0:m], self.xn[:, k, pc:pc + n]) for k in range(8)], [wk, "xn"], [pqk])
            dst_fn(ti, c0, n, isc, pq[0:m, 0:n], pqk)

    def phase_o(self, l, src):
        self.phase()
        stg = [self.f32([8, 512], "ostage%d" % i) for i in range(2)]
        wo, wok = self.bf([8, 1024], "wo")
        for i in range(2):
            sv, sk = stg[i]
            self.dma(sv, self.w_out[l][:, i * 512:(i + 1) * 512].rearrange("(k p) n -> p k n", p=128), [], [sk])
            self.act(wo[:, :, i * 512:(i + 1) * 512], sv, AF.Copy, [sk], [wok])
        mixt, mk = self.f32([12, 512], "mixt")
        mixb, mbk = self.bf([8, 512], "mixb")
        sqb, _ = self.bf([2, 512], "sqb")
        rstd, _ = self.alloc(512, "rstd")
        rt, _ = self.alloc(512, "rt")
        sg, sgk = self.f32([2, 512], "silug")
        xts = [self.f32([8, 512], "oxt%d" % i) for i in range(2)]
        MIXv = self.MIX.rearrange("(k p) n -> p k n", p=128)
        for ti, (c0, n, isc) in enumerate(TILES):
            xtile, xk = xts[ti % 2]
            self.dma(xtile[:, :, 0:n], src[:, c0:c0 + n].rearrange("(k p) n -> p k n", p=128), [("X", ti)], [xk])
            self.dma(mixt[:, :, 0:n], MIXv[:, :, c0:c0 + n], [("MIX", ti)], [mk])
            self.tt("dve", mixt[:, 0:2, 0:n], mixt[:, 0:2, 0:n], mixt[:, 2:4, 0:n], ALU.add, [mk], [mk])
            self.act(sg[:, :, 0:n], mixt[:, 10:12, 0:n], AF.Silu, [mk], [sgk])
            for g, c in enumerate((0, 4, 6, 8)):
                pq, pqk = self.q[g % 2], self.qk[g % 2]
                self.rms_rstd(mixt[:, c:c + 2, :], mk, n, 2, rstd, rt, sqb, pq, pqk)
                for j in range(2):
                    self.stt("dve", mixt[:, c + j, 0:n], mixt[:, c + j, 0:n], self.mnw_t[:, 2 * g + j:2 * g + j + 1],
                             rstd[:, 0:n], ALU.mult, ALU.mult, [mk, "rstd", "mnw_t"], [mk])
                    if g == 0:
                        self.tt("dve", mixb[:, j, 0:n], mixt[:, j, 0:n], sg[:, j, 0:n], ALU.mult, [mk, sgk], [mbk])
                    else:
                        self.cp("pool", mixb[:, 2 * g + j, 0:n], mixt[:, c + j, 0:n], [mk], [mbk])
            for oc in range(8):
                pq, pqk = self.q[2 + oc % 2], self.qk[2 + oc % 2]
                self.mm(pq[:, 0:n], [(wo[:, k, oc * 128:(oc + 1) * 128], mixb[:, k, 0:n]) for k in range(8)],
                        [wok, mbk], [pqk])
                self.stt("dve", xtile[:, oc, 0:n], pq[:, 0:n], self.mods[:, 16 + oc, isc:isc + 1], xtile[:, oc, 0:n],
                         ALU.mult, ALU.add, [pqk, xk, "mods"], [xk])
            self.dma(self.X[:, c0:c0 + n].rearrange("(k p) n -> p k n", p=128), xtile[:, :, 0:n], [xk], [("X", ti)])

    def ffn_up_phase(self, l):
        self.phase()
        stg = [[self.f32([8, 128], "ustage%d%d" % (i, j)) for j in range(2)] for i in range(2)]
        wbf = [[self.bf([8, 128], "uw%d%d" % (i, j)) for j in range(2)] for i in range(2)]
        a_full, ak = self.alloc(T, "a_full")
        c_full, ck = self.alloc(T, "c_full")
        hb = [self.alloc(T // 2, "hb%d" % i) for i in range(2)]
        Hv = self.H
        segs = [(0, LC), (LC, T)]
        for oc in range(22):
            pb = oc % 2
            self.load_w_chunk(self.w_up[l][:, oc * 128:(oc + 1) * 128], stg[pb][0], wbf[pb][0], True)
            self.load_w_chunk(self.w_up[l][:, DFF + oc * 128:DFF + (oc + 1) * 128], stg[pb][1], wbf[pb][1], False)
            hv = hb[pb][0].bitcast(BF16)
            hk = hb[pb][1]
            wa, wak = wbf[pb][0]
            wv_, wvk = wbf[pb][1]
            def dst_a(ti, c0, n, isc, ps, pk):
                self.act(a_full[:, c0:c0 + n], ps, AF.Copy, [pk], [ak])
            self.proj_rows(wa, wak, dst_a)
            w0 = self.fcw_t[:, 0, oc:oc + 1]
            w1 = self.fcw_t[:, 1, oc:oc + 1]
            w2 = self.fcw_t[:, 2, oc:oc + 1]
            self.act(c_full, a_full, AF.Identity, [ak, "fcw_t", "fcb_t"], [ck], scale=w1, bias=self.fcb_t[:, oc:oc + 1])
            for (s, e) in segs:
                self.stt("dve", c_full[:, s + 1:e], a_full[:, s:e - 1], w0, c_full[:, s + 1:e], ALU.mult, ALU.add,
                         [ak, ck, "fcw_t"], [ck])
                self.stt("dve", c_full[:, s:e - 1], a_full[:, s + 1:e], w2, c_full[:, s:e - 1], ALU.mult, ALU.add,
                         [ak, ck, "fcw_t"], [ck])
            self.act(c_full, c_full, AF.Silu, [ck], [ck])
            def dst_v(ti, c0, n, isc, ps, pk):
                self.tt("dve", hv[:, c0:c0 + n], ps, c_full[:, c0:c0 + n], ALU.mult, [pk, ck], [hk])
            self.proj_rows(wv_, wvk, dst_v)
            self.dma(Hv[oc * 128:(oc + 1) * 128, :], hv, [hk], [("H", oc)])

    def ffn_down_phase(self, l):
        self.phase()
        stg = [self.f32([8, 512], "dstage%d" % i) for i in range(2)]
        wd = self.arena[:, 0:11264].bitcast(BF16).rearrange("p (k n) -> p k n", k=22)
        wdk = "xn"
        i = 0
        for k0, kn in ((0, 8), (8, 8), (16, 6)):
            for c in range(2):
                sv, sk = stg[i % 2]
                self.dma(sv[:, 0:kn, :], self.w_dn[l][k0 * 128:(k0 + kn) * 128, c * 512:(c + 1) * 512]
                         .rearrange("(k p) n -> p k n", p=128), [], [sk])
                if i % 2:
                    self.act(wd[:, k0:k0 + kn, c * 512:(c + 1) * 512], sv[:, 0:kn, :], AF.Copy, [sk], [wdk])
                else:
                    self.cp("pool", wd[:, k0:k0 + kn, c * 512:(c + 1) * 512], sv[:, 0:kn, :], [sk], [wdk])
                i += 1
        hts = [self.bf([22, 512], "ht%d" % i) for i in range(2)]
        xts = [self.f32([8, 512], "dxt%d" % i) for i in range(2)]
        Hv = self.H.rearrange("(k p) n -> p k n", p=128)
        for ti, (c0, n, isc) in enumerate(TILES):
            xtile, xk = xts[ti % 2]
            ht, hk = hts[ti % 2]
            self.dma(xtile[:, :, 0:n], self.X[:, c0:c0 + n].rearrange("(k p) n -> p k n", p=128), [("X", ti)], [xk])
            self.dma(ht[:, :, 0:n], Hv[:, :, c0:c0 + n], [("H", oc) for oc in range(22)], [hk])
            for oc in range(8):
                pq, pqk = self.q[oc % 4], self.qk[oc % 4]
                self.mm(pq[:, 0:n], [(wd[:, k, oc * 128:(oc + 1) * 128], ht[:, k, 0:n]) for k in range(22)],
                        [wdk, hk], [pqk])
                self.stt("dve", xtile[:, oc, 0:n], pq[:, 0:n], self.mods[:, 40 + oc, isc:isc + 1], xtile[:, oc, 0:n],
                         ALU.mult, ALU.add, [pqk, xk, "mods"], [xk])
            self.dma(self.X[:, c0:c0 + n].rearrange("(k p) n -> p k n", p=128), xtile[:, :, 0:n], [xk], [("X", ti)])

    def final_norm(self, fnw_t, out):
        self.phase()
        xts = [self.f32([8, 512], "fxt%d" % i) for i in range(2)]
        sqb, _ = self.bf([8, 512], "sqb")
        rstd, _ = self.alloc(512, "rstd")
        rt, _ = self.alloc(512, "rt")
        for ti, (c0, n, isc) in enumerate(TILES):
            if isc:
                continue
            xtile, xk = xts[ti % 2]
            self.dma(xtile[:, :, 0:n], self.X[:, c0:c0 + n].rearrange("(k p) n -> p k n", p=128), [("X", ti)], [xk])
            pq, pqk = self.q[ti % 2], self.qk[ti % 2]
            self.rms_rstd(xtile, xk, n, 8, rstd, rt, sqb, pq, pqk)
            for k in range(8):
                self.stt("dve", xtile[:, k, 0:n], xtile[:, k, 0:n], fnw_t[:, k:k + 1], rstd[:, 0:n], ALU.mult, ALU.mult,
                         [xk, "rstd", "fnw_t"], [xk])
            tok = self.dma(out[:, c0 - LC:c0 - LC + n].rearrange("(k p) n -> p k n", p=128), xtile[:, :, 0:n], [xk], [])
            self.out_toks.append(tok)

    CH_HG = 0
    CH_G = 16
    CH_U = 18
    CH_AQ = 20
    CH_AK = 22
    CH_HY = 24
    CH_TV = 30
    NCH = 31

    def declare_mixer_inputs(self):
        inp = self.inp
        self.w_mx = inp("w_mx", [DEPTH, D, self.NCH * 128])
        self.ropeC = inp("ropeC", [128, LL])
        self.ropeS = inp("ropeS", [128, LL])
        self.sinkp = inp("sinkp", [DEPTH, 128, 2])
        self.cmask = inp("cmask", [128, 2, 128])
        self.hmask = inp("hmask", [128, 2])
        self.onesp = inp("onesp", [128, 2, 128])
        self.s5_are = inp("s5_are", [DEPTH, 128, 2, 8])
        self.s5_aim = inp("s5_aim", [DEPTH, 128, 2, 8])
        self.s5_ls = inp("s5_ls", [DEPTH, 128, 2, 8])
        self.s5_bre = inp("s5_bre", [DEPTH, 128, 8, 16])
        self.s5_bim = inp("s5_bim", [DEPTH, 128, 8, 16])
        self.s5_cre = inp("s5_cre", [DEPTH, 128, 2, 8, 128])
        self.s5_cim = inp("s5_cim", [DEPTH, 128, 2, 8, 128])
        self.s5_dv = inp("s5_dv", [DEPTH, 128, 2])
        self.s5_glu = inp("s5_glu", [DEPTH, 256, 256])
        self.ident_in = inp("ident", [128, 128])
        self.iota_in = inp("iota", [128, 1024])
        self.hg_lbl = inp("hg_lbl", [128, 2, 4, DEPTH])
        self.hg_E = inp("hg_E", [128, 4096])
        self.hg_S2 = inp("hg_S2", [128, 2048])
        self.declare_hyena_inputs()

    def alloc_mixer_persistent(self):
        self.lbt, _ = self.f32([2, 4, DEPTH], "lbt")
        self.oml, _ = self.f32([2, 4, DEPTH], "oml")

    def mixer_setup(self):
        lg, k = self.f32([2, 4, DEPTH], "lbl")
        sm, sk = self.f32([2, 4], "lbsum")
        self.dma(lg, self.hg_lbl, [], [k])
        self.act(lg, lg, AF.Exp, [k], [k])
        self.P.op("dve", lambda e: e.tensor_reduce(sm, lg, mybir.AxisListType.X, ALU.add), [k], [sk])
        self.P.op("dve", lambda e: e.reciprocal(sm, sm), [sk], [sk])
        self.tt("dve", lg, lg, sm.unsqueeze(3).to_broadcast([128, 2, 4, DEPTH]), ALU.mult, [k, sk], [k])
        self.P.op("dve", lambda e: e.memset(self.lbt[:, :, :, 0:1], 0.0), [], ["lbt"])
        for l in range(1, DEPTH):
            self.tt("dve", self.lbt[:, :, :, l:l + 1], self.lbt[:, :, :, l - 1:l], lg[:, :, :, l:l + 1], ALU.add,
                    [k, "lbt"], ["lbt"])
        self.ts("dve", self.oml, self.lbt, -1.0, 1.0, ALU.mult, ALU.add, ["lbt"], ["oml"])

    def wrot_init(self):
        self.wrot = [(self.f32([8, 128], "wst%d" % i), self.bf([8, 128], "wbf%d" % i)) for i in range(3)]
        self.wrot_i = 0

    def wchunk(self, l, ci):
        stg, wbf = self.wrot[self.wrot_i % 3]
        self.wrot_i += 1
        self.load_w_chunk(self.w_mx[l][:, ci * 128:(ci + 1) * 128], stg, wbf, self.wrot_i % 2 == 0)
        return wbf

    def wchunk_own(self, l, ci, name):
        stg = self.f32([8, 128], name + "_st")
        wbf = self.bf([8, 128], name + "_bf")
        self.load_w_chunk(self.w_mx[l][:, ci * 128:(ci + 1) * 128], stg, wbf, True)
        return wbf

    def zero_mix_rows(self, r0, r1):
        z, zk = self.alloc(T, "zero")
        self.P.op("pool", lambda e: e.memset(z, 0.0), [], [zk])
        for r in range(r0, r1):
            self.dma(self.MIX[r * 128:(r + 1) * 128, :], z, [zk], [("MIX", ti) for ti in range(len(TILES))])

    def mixers_phase(self, l):
        if l == 0:
            self.phase()
            self.mixer_setup()
        allk = [("MIX", ti) for ti in range(len(TILES))]
        self.phase()
        if "hg" in self.mixers:
            self.mixer_hg(l, allk)
        else:
            self.zero_mix_rows(0, 4)
        self.phase()
        self.wrot_init()
        for c in range(2):
            wv, wk = self.wchunk(l, self.CH_G + c)
            gb, gk = self.alloc(T, "gfull%d" % c)
            def dst(ti, c0, n, isc, ps, pk, gb=gb, gk=gk):
                self.act(gb[:, c0:c0 + n], ps, AF.Copy, [pk], [gk])
            self.proj_rows(wv, wk, dst)
            self.dma(self.MIX[1280 + c * 128:1280 + (c + 1) * 128, :], gb, [gk], allk)
        self.phase()
        if "s5" in self.mixers:
            self.mixer_s5(l, allk)
        else:
            self.zero_mix_rows(4, 6)
        self.phase()
        if "hy" in self.mixers:
            self.mixer_hy(l, allk)
        else:
            self.zero_mix_rows(6, 8)
        self.phase()
        if "at" in self.mixers:
            self.mixer_at(l, allk)
        else:
            self.zero_mix_rows(8, 10)

    def mixer_at(self, l, allk):
        P = self.P
        self.wrot_init()
        HW = 2048
        Ch, ck = self.alloc(HW, "ropeC")
        Sh, sk = self.alloc(HW, "ropeS")
        cm32, cmk = self.f32([2, 128], "cm32")
        cmask, cmbk = self.bf([2, 128], "cmask")
        self.dma(cm32, self.cmask, [], [cmk])
        self.cp("dve", cmask, cm32, [cmk], [cmbk])
        hm, hmk = self.alloc(2, "hmask")
        self.dma(hm, self.hmask, [], [hmk])
        op32, opk = self.f32([2, 128], "op32")
        onesp, onk = self.bf([2, 128], "onesp")
        self.dma(op32, self.onesp, [], [opk])
        self.cp("dve", onesp, op32, [opk], [onk])
        esink, esk = self.alloc(2, "esink")
        self.dma(esink, self.sinkp[l], [], [esk])
        self.act(esink, esink, AF.Exp, [esk], [esk])
        raw, rk = self.alloc(T, "raw")
        tb, tbk = self.alloc(HW, "ropeB")
        tsw, tswk = self.alloc(HW, "ropeSw")
        Qb, qbk = self.bf([T], "Qb")
        Km = [self.bf([T], "Km%d" % i) for i in range(2)]
        Vp, vpk = self.bf([34, 2, 128], "Vp")
        PT = [self.bf([640], "PT%d" % i) for i in range(2)]
        atb = [self.alloc(128, "atb%d" % i) for i in range(2)]
        dn, dnk = self.alloc(128, "den")
        wtv, wtvk = self.wchunk_own(l, self.CH_TV, "wtv")
        P.op("pool", lambda e: e.memset(Vp, 0.0), [], [vpk])

        def rope_raw():
            for half in range(2):
                c0 = LC + half * HW
                self.dma(Ch, self.ropeC[:, half * HW:(half + 1) * HW], [], [ck])
                self.dma(Sh, self.ropeS[:, half * HW:(half + 1) * HW], [], [sk])
                self.tt("pool", tb, raw[:, c0:c0 + HW], Sh, ALU.mult, [rk, sk], [tbk])
                self.cp("dve", tsw[0:64, :], tb[64:128, :], [tbk], [tswk])
                self.cp("dve", tsw[64:128, :], tb[0:64, :], [tbk], [tswk])
                self.tt("dve", raw[:, c0:c0 + HW], raw[:, c0:c0 + HW], Ch, ALU.mult, [rk, ck], [rk])
                self.tt("dve", raw[:, c0:c0 + HW], raw[:, c0:c0 + HW], tsw, ALU.add, [rk, tswk], [rk])

        def dstq(ti, c0, n, isc, ps, pk):
            self.act(raw[:, c0:c0 + n], ps, AF.Copy, [pk], [rk])

        for kv in range(2):
            wq, wqk = self.wchunk(l, self.CH_AQ + kv)
            self.proj_rows(wq, wqk, dstq)
            rope_raw()
            self.cp("pool", Qb, raw, [rk], [qbk])
            wk_, wkk = self.wchunk(l, self.CH_AK + kv)
            self.proj_rows(wk_, wkk, dstq)
            rope_raw()
            for hl in range(2):
                kmv, kmk = Km[hl]
                self.ts("dve", kmv, raw, hm[:, hl:hl + 1], None, ALU.mult, None, [rk, hmk], [kmk])
            for b0 in range(0, 34, 8):
                nb = min(8, 34 - b0)
                pq, pqk = self.q[3], self.qk[3]
                for bi in range(nb):
                    blk = b0 + bi
                    pc = pcol(blk * 128)
                    self.mm(pq[:, bi * 64:(bi + 1) * 64],
                            [(self.xn[:, k, pc:pc + 128], wtv[:, k, kv * 64:(kv + 1) * 64]) for k in range(8)],
                            ["xn", wtvk], [pqk])
                src = pq[:, 0:nb * 64].rearrange("p (b d) -> p b d", d=64)
                self.act(Vp[:, b0:b0 + nb, 0, 0:64], src, AF.Copy, [pqk], [vpk])
                self.cp("dve", Vp[:, b0:b0 + nb, 1, 64:128], src, [pqk], [vpk])
            for qb in range(34):
                q0 = qb * 128
                kblocks = [(0, None), (1, None)]
                if qb >= 2:
                    if qb - 1 >= 2:
                        kblocks.append((qb - 1, 0))
                    kblocks.append((qb, None))
                    if qb + 1 < 34:
                        kblocks.append((qb + 1, 1))
                nkb = len(kblocks)
                po, pok = self.q[2], self.qk[2]
                nmm = 2 * nkb
                imm = 0
                for hl in range(2):
                    ps, psk = self.q[hl], self.qk[hl]
                    kmv, kmk = Km[hl]
                    ptv, ptk = PT[hl]
                    for bi, (kb, mk) in enumerate(kblocks):
                        self.mm(ps[:, bi * 128:(bi + 1) * 128],
                                [(kmv[:, kb * 128:(kb + 1) * 128], Qb[:, q0:q0 + 128])], [kmk, qbk], [psk])
                    self.act(ptv[:, 0:nkb * 128], ps[:, 0:nkb * 128], AF.Exp, [psk], [ptk], scale=0.125)
                    for bi, (kb, mk) in enumerate(kblocks):
                        if mk is not None:
                            self.tt("pool", ptv[:, bi * 128:(bi + 1) * 128], ptv[:, bi * 128:(bi + 1) * 128],
                                    cmask[:, mk, :], ALU.mult, [ptk, cmbk], [ptk])
                    for bi, (kb, mk) in enumerate(kblocks):
                        P.op("pe", lambda e, kb=kb, bi=bi, hl=hl, ptv=ptv, imm=imm, po=po, nmm=nmm: e.matmul(
                            po[:, 0:128], Vp[:, kb, hl, :], ptv[:, bi * 128:(bi + 1) * 128],
                            start=(imm == 0), stop=(imm == nmm - 1)), [vpk, ptk], [pok])
                        P.op("pe", lambda e, kb=kb, bi=bi, hl=hl, ptv=ptv, imm=imm, po=po, nmm=nmm: e.matmul(
                            po[:, 512:640], onesp[:, hl, :], ptv[:, bi * 128:(bi + 1) * 128],
                            start=(imm == 0), stop=(imm == nmm - 1)), [onk, ptk], [pok])
                        imm += 1
                av, avk = atb[qb % 2]
                self.ts("dve", dn, po[:, 512:640], esink[:, kv:kv + 1], None, ALU.add, None, [pok, esk], [dnk])
                P.op("dve", lambda e: e.reciprocal(dn, dn), [dnk], [dnk])
                self.tt("dve", av, po[:, 0:128], dn, ALU.mult, [pok, dnk], [avk])
                tix = 0 if qb < 2 else 1 + (qb - 2) // 4
                self.dma(self.MIX[1024 + kv * 128:1024 + (kv + 1) * 128, q0:q0 + 128], av, [avk], [("MIX", tix)])

    def mixer_s5(self, l, allk):
        P = self.P
        TWO_PI = 2.0 * math.pi
        self.wrot_init()
        SEG = 512
        segs = [(0, LC)] + [(LC + SEG * i, SEG) for i in range(LL // SEG)]
        def small(name, shape=(2, 8)):
            return self.f32(list(shape), "s5" + name)
        are, k_are = small("are"); aim, k_aim = small("aim"); ls, k_ls = small("ls")
        self.dma(are, self.s5_are[l], [], [k_are])
        self.dma(aim, self.s5_aim[l], [], [k_aim])
        self.dma(ls, self.s5_ls[l], [], [k_ls])
        dt, k_dt = small("dt"); mag, k_mag = small("mag"); th, k_th = small("th")
        tmp, k_tmp = small("tmp"); tmp2, k_tmp2 = small("tmp2")
        ti_v, k_ti = self.alloc(16, "s5ti")
        ti = ti_v.bitcast(I32).rearrange("p (a b) -> p a b", a=2)
        cosv, k_cos = small("cosv"); sinv, k_sin = small("sinv")
        fr, k_fr = small("fr"); fi, k_fi = small("fi")
        self.act(dt, ls, AF.Exp, [k_ls], [k_dt])
        self.tt("dve", tmp, are, dt, ALU.mult, [k_are, k_dt], [k_tmp])
        self.act(mag, tmp, AF.Exp, [k_tmp], [k_mag])
        self.tt("dve", th, aim, dt, ALU.mult, [k_aim, k_dt], [k_th])
        self.ts("dve", th, th, 1.0 / TWO_PI, None, ALU.mult, None, [k_th], [k_th])

        def sin_turns(dst, dkey, src, skey, shift, shp_tmp, k_t, tint, k_i, shp_tmp2, k_t2):
            self.ts("dve", shp_tmp, src, shift, None, ALU.add, None, [skey], [k_t])
            self.cp("dve", tint, shp_tmp, [k_t], [k_i])
            self.cp("dve", shp_tmp2, tint, [k_i], [k_t2])
            self.tt("dve", shp_tmp, shp_tmp, shp_tmp2, ALU.subtract, [k_t, k_t2], [k_t])
            self.act(dst, shp_tmp, AF.Sin, [k_t], [dkey], scale=TWO_PI)

        sin_turns(sinv, k_sin, th, k_th, 0.0, tmp, k_tmp, ti, k_ti, tmp2, k_tmp2)
        sin_turns(cosv, k_cos, th, k_th, 0.25, tmp, k_tmp, ti, k_ti, tmp2, k_tmp2)
        abre, k_abre = small("abre"); abim, k_abim = small("abim")
        self.tt("dve", abre, mag, cosv, ALU.mult, [k_mag, k_cos], [k_abre])
        self.tt("dve", abim, mag, sinv, ALU.mult, [k_mag, k_sin], [k_abim])
        den, k_den = small("den")
        self.tt("dve", den, are, are, ALU.mult, [k_are], [k_den])
        self.tt("dve", tmp, aim, aim, ALU.mult, [k_aim], [k_tmp])
        self.tt("dve", den, den, tmp, ALU.add, [k_den, k_tmp], [k_den])
        P.op("dve", lambda e: e.reciprocal(den, den), [k_den], [k_den])
        nr, k_nr = small("nr")
        self.ts("dve", nr, abre, -1.0, None, ALU.add, None, [k_abre], [k_nr])
        self.tt("dve", fr, nr, are, ALU.mult, [k_nr, k_are], [k_fr])
        self.tt("dve", tmp, abim, aim, ALU.mult, [k_abim, k_aim], [k_tmp])
        self.tt("dve", fr, fr, tmp, ALU.add, [k_fr, k_tmp], [k_fr])
        self.tt("dve", fr, fr, den, ALU.mult, [k_fr, k_den], [k_fr])
        self.tt("dve", fi, abim, are, ALU.mult, [k_abim, k_are], [k_fi])
        self.tt("dve", tmp, nr, aim, ALU.mult, [k_nr, k_aim], [k_tmp])
        self.tt("dve", fi, fi, tmp, ALU.subtract, [k_fi, k_tmp], [k_fi])
        self.tt("dve", fi, fi, den, ALU.mult, [k_fi, k_den], [k_fi])
        bre, k_bre = self.f32([8, 16], "s5bre"); bim, k_bim = self.f32([8, 16], "s5bim")
        self.dma(bre, self.s5_bre[l], [], [k_bre])
        self.dma(bim, self.s5_bim[l], [], [k_bim])
        dv, k_dv = self.alloc(2, "s5dv")
        self.dma(dv, self.s5_dv[l], [], [k_dv])
        ident, k_id = self.alloc(128, "ident")
        self.dma(ident, self.ident_in, [], [k_id])
        iota, k_io = self.alloc(SEG, "iota")
        self.dma(iota, self.iota_in[:, 0:SEG], [], [k_io])
        glu32 = self.f32([2, 256], "glu32")
        glub, k_glub = self.bf([2, 256], "glub")
        self.dma(glu32[0], self.s5_glu[l].rearrange("(k p) n -> p k n", p=128), [], [glu32[1]])
        self.cp("dve", glub, glu32[0], [glu32[1]], [k_glub])
        U, k_u = self.f32([2, T], "s5U")
        Y, k_y = self.f32([2, T], "s5Y")
        P.op("pool", lambda e: e.memset(Y, 0.0), [], [k_y])
        for c in range(2):
            wv, wk = self.wchunk(l, self.CH_U + c)
            def dst(ti_, c0, n, isc, ps, pk, c=c):
                self.act(U[:, c, c0:c0 + n], ps, AF.Copy, [pk], [k_u])
            self.proj_rows(wv, wk, dst)
        bb1, k_bb1 = self.alloc(16, "bb1"); bb2, k_bb2 = self.alloc(16, "bb2")
        Bpad, k_bp = self.alloc(128, "Bpad")
        BBt, k_bbt = self.f32([2, 128], "BBt")
        Cre, k_cre = self.alloc(128, "Cre"); Cim, k_cim = self.alloc(128, "Cim")
        RB, k_rb = self.alloc(SEG, "RB")
        st, k_st = self.alloc(2, "s5st")
        bufs = {}
        for nm in ("T1", "T2", "T3", "CS", "SN", "ZR", "ZI", "TF"):
            bufs[nm] = self.alloc(SEG, "s5" + nm)
        TIv, k_TI = self.alloc(SEG, "s5TI")
        TI = TIv.bitcast(I32)
        T1, k1 = bufs["T1"]; T2, k2 = bufs["T2"]; T3, k3 = bufs["T3"]
        CS, kc = bufs["CS"]; SN, ks = bufs["SN"]; ZR, kzr = bufs["ZR"]; ZI, kzi = bufs["ZI"]; TF, ktf = bufs["TF"]

        def V(ap, c0, n, rev):
            a = ap[:, c0:c0 + n]
            return a[:, ::-1] if rev else a

        for d in range(2):
            rev = (d == 1)
            order = list(range(len(segs))) if d == 0 else [0] + list(range(len(segs) - 1, 0, -1))
            for j in range(8):
                uc = j // 4
                m0 = 32 * (j % 4)
                for ri in range(2):
                    x1, kx1 = (bre, k_bre) if ri == 0 else (bim, k_bim)
                    x2, kx2 = (bim, k_bim) if ri == 0 else (bre, k_bre)
                    self.ts("dve", bb1, x1[:, j, :], fr[:, d, j:j + 1], None, ALU.mult, None, [kx1, k_fr], [k_bb1])
                    self.ts("dve", bb2, x2[:, j, :], fi[:, d, j:j + 1], None, ALU.mult, None, [kx2, k_fi], [k_bb2])
                    self.tt("dve", bb1, bb1, bb2, ALU.subtract if ri == 0 else ALU.add, [k_bb1, k_bb2], [k_bb1])
                    P.op("dve", lambda e: e.memset(Bpad, 0.0), [], [k_bp])
                    self.cp("dve", Bpad[0:64, m0:m0 + 16], bb1[0:64, :], [k_bb1], [k_bp])
                    self.cp("dve", Bpad[64:128, m0 + 16:m0 + 32], bb1[64:128, :], [k_bb1], [k_bp])
                    pq, pqk = self.q[3], self.qk[3]
                    P.op("pe", lambda e, pq=pq: e.transpose(pq[:, 0:128], Bpad, ident), [k_bp, k_id], [pqk])
                    self.act(BBt[:, ri, :], pq[:, 0:128], AF.Copy, [pqk], [k_bbt])
                self.dma(Cre, self.s5_cre[l][:, d, j, :], [], [k_cre])
                self.dma(Cim, self.s5_cim[l][:, d, j, :], [], [k_cim])
                self.act(RB, iota, AF.Identity, [k_io, k_mag], [k_rb], scale=0.0, bias=mag[:, d, j:j + 1])
                P.op("dve", lambda e: e.memset(st, 0.0), [], [k_st])
                for si in order:
                    c0, n = segs[si]
                    if d == 0:
                        n0 = c0
                    else:
                        n0 = 0 if si == 0 else LC + (T - c0 - n)
                    urhs = V(U[:, uc, :], c0, n, rev)
                    pre, kpre = self.q[0], self.qk[0]
                    pim, kpim = self.q[1], self.qk[1]
                    self.mm(pre[:, 0:n], [(BBt[:, 0, :], urhs)], [k_bbt, k_u], [kpre])
                    self.mm(pim[:, 0:n], [(BBt[:, 1, :], urhs)], [k_bbt, k_u], [kpim])
                    P.op("dve", lambda e, n=n, n0=n0, d=d, j=j: e.tensor_scalar(
                        T1[:, 0:n], iota[:, 0:n], float(n0), th[:, d, j:j + 1], ALU.add, ALU.mult), [k_io, k_th], [k1])
                    for (dst_, kd, shift) in ((SN, ks, 0.0), (CS, kc, 0.25)):
                        if shift:
                            self.ts("dve", T1[:, 0:n], T1[:, 0:n], shift, None, ALU.add, None, [k1], [k1])
                        self.cp("dve", TI[:, 0:n], T1[:, 0:n], [k1], [k_TI])
                        self.cp("dve", TF[:, 0:n], TI[:, 0:n], [k_TI], [ktf])
                        self.tt("dve", T2[:, 0:n], T1[:, 0:n], TF[:, 0:n], ALU.subtract, [k1, ktf], [k2])
                        self.act(dst_[:, 0:n], T2[:, 0:n], AF.Sin, [k2], [kd], scale=TWO_PI)
                    self.tt("dve", T1[:, 0:n], pre[:, 0:n], CS[:, 0:n], ALU.mult, [kpre, kc], [k1])
                    self.tt("dve", T2[:, 0:n], pim[:, 0:n], SN[:, 0:n], ALU.mult, [kpim, ks], [k2])
                    self.tt("dve", T1[:, 0:n], T1[:, 0:n], T2[:, 0:n], ALU.add, [k1, k2], [k1])
                    self.tt("dve", T3[:, 0:n], pim[:, 0:n], CS[:, 0:n], ALU.mult, [kpim, kc], [k3])
                    self.tt("dve", T2[:, 0:n], pre[:, 0:n], SN[:, 0:n], ALU.mult, [kpre, ks], [k2])
                    self.tt("dve", T3[:, 0:n], T3[:, 0:n], T2[:, 0:n], ALU.subtract, [k3, k2], [k3])
                    P.op("dve", lambda e, n=n: e.tensor_tensor_scan(ZR[:, 0:n], RB[:, 0:n], T1[:, 0:n], st[:, 0:1],
                                                                    ALU.mult, ALU.add), [k_rb, k1, k_st], [kzr])
                    P.op("dve", lambda e, n=n: e.tensor_tensor_scan(ZI[:, 0:n], RB[:, 0:n], T3[:, 0:n], st[:, 1:2],
                                                                    ALU.mult, ALU.add), [k_rb, k3, k_st], [kzi])
                    self.cp("dve", st[:, 0:1], ZR[:, n - 1:n], [kzr], [k_st])
                    self.cp("dve", st[:, 1:2], ZI[:, n - 1:n], [kzi], [k_st])
                    self.tt("dve", T1[:, 0:n], ZR[:, 0:n], CS[:, 0:n], ALU.mult, [kzr, kc], [k1])
                    self.tt("dve", T2[:, 0:n], ZI[:, 0:n], SN[:, 0:n], ALU.mult, [kzi, ks], [k2])
                    self.tt("dve", T1[:, 0:n], T1[:, 0:n], T2[:, 0:n], ALU.subtract, [k1, k2], [k1])
                    self.tt("dve", T3[:, 0:n], ZR[:, 0:n], SN[:, 0:n], ALU.mult, [kzr, ks], [k3])
                    self.tt("dve", T2[:, 0:n], ZI[:, 0:n], CS[:, 0:n], ALU.mult, [kzi, kc], [k2])
                    self.stt("dve", T3[:, 0:n], T2[:, 0:n], -1.0, T3[:, 0:n], ALU.mult, ALU.subtract, [k2, k3], [k3])
                    py, kpy = self.q[2], self.qk[2]
                    self.mm(py[:, 0:n], [(Cre, T1[:, 0:n]), (Cim, T3[:, 0:n])], [k_cre, k_cim, k1, k3], [kpy])
                    yv = V(Y[:, uc, :], c0, n, rev)
                    self.tt("dve", yv, yv, py[:, 0:n], ALU.add, [k_y, kpy], [k_y])
        zt, kz = self.f32([2, 512], "s5z")
        zb, kzb = self.bf([2, 512], "s5zb")
        sg, ksg = self.alloc(512, "s5sg")
        ob = [self.alloc(512, "s5ob%d" % i) for i in range(2)]
        for ti_, (c0, n, isc) in enumerate(TILES):
            for c in range(2):
                self.stt("dve", zt[:, c, 0:n], U[:, c, c0:c0 + n], dv[:, c:c + 1], Y[:, c, c0:c0 + n], ALU.mult, ALU.add,
                         [k_u, k_y, k_dv], [kz])
            self.act(zt[:, :, 0:n], zt[:, :, 0:n], AF.Gelu, [kz], [kz])
            self.cp("pool", zb[:, :, 0:n], zt[:, :, 0:n], [kz], [kzb])
            for oc in range(2):
                pq, pqk = self.q[oc], self.qk[oc]
                self.mm(pq[:, 0:n], [(glub[:, k, oc * 128:(oc + 1) * 128], zb[:, k, 0:n]) for k in range(2)],
                        [k_glub, kzb], [pqk])
                self.act(sg[:, 0:n], pq[:, 0:n], AF.Sigmoid, [pqk], [ksg])
                ov, ok = ob[oc]
                self.tt("dve", ov[:, 0:n], zt[:, oc, 0:n], sg[:, 0:n], ALU.mult, [kz, ksg], [ok])
                self.dma(self.MIX[512 + oc * 128:512 + (oc + 1) * 128, c0:c0 + n], ov[:, 0:n], [ok], [("MIX", ti_)])

    def mixer_hg(self, l, allk):
        P = self.P
        self.wrot_init()
        SEG = 1024
        segs = [(0, LC)] + [(LC + SEG * i, SEG) for i in range(LL // SEG)]
        Eb, k_e = self.bf([4096], "hgE")
        S2b, k_s2 = self.bf([2048], "hgS2")
        stg, k_stg = self.alloc(1024, "hgstg")
        for i in range(4):
            self.dma(stg, self.hg_E[:, i * 1024:(i + 1) * 1024], [], [k_stg])
            self.cp("dve", Eb[:, i * 1024:(i + 1) * 1024], stg, [k_stg], [k_e])
        for i in range(2):
            self.dma(stg, self.hg_S2[:, i * 1024:(i + 1) * 1024], [], [k_stg])
            self.cp("dve", S2b[:, i * 1024:(i + 1) * 1024], stg, [k_stg], [k_s2])
        Qp, k_q = self.alloc(T, "hgQ")
        F, k_f = self.alloc(T, "hgF")
        FM, k_fm = self.alloc(T, "hgFM")
        OSs = [self.alloc(SEG, "hgOS%d" % i) for i in range(2)]
        VB, k_vb = self.bf([T], "hgVB")
        stt, k_stt = self.alloc(32, "hgst")
        VbS = [self.bf([SEG], "hgVbS%d" % i) for i in range(2)]
        NB = [self.alloc(SEG, "hgNB%d" % i) for i in range(2)]
        SS = [self.alloc(SEG, "hgS%d" % i) for i in range(2)]
        SQ = [self.bf([SEG], "hgSQ%d" % i) for i in range(2)]

        def V(ap, c0, n, rev):
            a = ap[:, c0:c0 + n]
            return a[:, ::-1] if rev else a

        for h in range(4):
            wq, wqk = self.wchunk(l, self.CH_HG + 4 * h + 0)
            def dq(ti, c0, n, isc, ps, pk):
                self.act(Qp[:, c0:c0 + n], ps, AF.Silu, [pk], [k_q])
            self.proj_rows(wq, wqk, dq)
            wv, wvk = self.wchunk(l, self.CH_HG + 4 * h + 3)
            def dv_(ti, c0, n, isc, ps, pk):
                self.act(VB[0:64, c0:c0 + n], ps, AF.Copy, [pk], [k_vb])
            self.proj_rows(wv, wvk, dv_, m=64)
            for d in range(2):
                rev = (d == 1)
                wf, wfk = self.wchunk(l, self.CH_HG + 4 * h + 1 + d)
                def df(ti, c0, n, isc, ps, pk):
                    self.act(F[:, c0:c0 + n], ps, AF.Sigmoid, [pk], [k_f])
                self.proj_rows(wf, wfk, df)
                self.ts("dve", F, F, self.oml[:, d, h, l:l + 1], self.lbt[:, d, h, l:l + 1], ALU.mult, ALU.add,
                        [k_f, "oml", "lbt"], [k_f])
                self.ts("dve", FM, F, -1.0, None, ALU.add, None, [k_f], [k_fm])
                P.op("dve", lambda e: e.memset(stt, 0.0), [], [k_stt])
                order = list(range(len(segs))) if d == 0 else [0] + list(range(len(segs) - 1, 0, -1))
                for si in order:
                    c0, n = segs[si]
                    Fv = V(F, c0, n, rev)
                    FMv = V(FM, c0, n, rev)
                    Qv = V(Qp, c0, n, rev)
                    VBv = V(VB[0:64, :], c0, n, rev)
                    po, kpo = self.q[2], self.qk[2]
                    for j in range(32):
                        b = j % 2
                        pv, kpv = self.q[b], self.qk[b]
                        for c in range(0, n, 512):
                            cn = min(512, n - c)
                            self.mm(pv[:, c:c + cn], [(Eb[0:64, 128 * j:128 * j + 128], VBv[:, c:c + cn])],
                                    [k_e, k_vb], [kpv])
                        vs, kvs = VbS[b]
                        self.act(vs[:, 0:n], pv[:, 0:n], AF.Copy, [kpv], [kvs])
                        nb_, knb = NB[b]
                        self.tt("pool", nb_[:, 0:n], FMv, vs[:, 0:n], ALU.mult, [k_fm, kvs], [knb])
                        s_, ks_ = SS[b]
                        P.op("dve", lambda e, s_=s_, Fv=Fv, nb_=nb_, j=j, n=n: e.tensor_tensor_scan(
                            s_[:, 0:n], Fv, nb_[:, 0:n], stt[:, j:j + 1], ALU.mult, ALU.add), [k_f, knb, k_stt], [ks_])
                        self.cp("pool", stt[:, j:j + 1], s_[:, n - 1:n], [ks_], [k_stt])
                        sq_, ksq = SQ[b]
                        self.tt("pool", sq_[:, 0:n], s_[:, 0:n], Qv, ALU.mult, [ks_, k_q], [ksq])
                        for c in range(0, n, 512):
                            cn = min(512, n - c)
                            P.op("pe", lambda e, c=c, cn=cn, j=j, sq_=sq_, po=po: e.matmul(
                                po[0:64, c:c + cn], S2b[:, 64 * j:64 * j + 64], sq_[:, c:c + cn],
                                start=(j == 0), stop=(j == 31)), [k_s2, ksq], [kpo])
                    osv, k_os = OSs[si % 2]
                    self.act(V(osv[0:64, :], 0, n, rev), po[0:64, 0:n], AF.Copy, [kpo], [k_os], scale=-1.0)
                    self.dma(self.MIX[d * 256 + h * 64:d * 256 + (h + 1) * 64, c0:c0 + n], osv[0:64, 0:n], [k_os], allk)

    def declare_hyena_inputs(self):
        inp = self.inp
        self.hy_zemb = {LL: inp("hy_zemb_l", [33, LL]), LC: inp("hy_zemb_c", [33, LC])}
        self.hy_win = {LL: inp("hy_win_l", [256, LL]), LC: inp("hy_win_c", [256, LC])}
        self.hy_w1 = inp("hy_w1", [DEPTH, 33, 64])
        self.hy_b1 = inp("hy_b1", [DEPTH, 64, 1])
        self.hy_fr = inp("hy_fr", [DEPTH, 64, 2])
        self.hy_w2 = inp("hy_w2", [DEPTH, 64, 64])
        self.hy_b2 = inp("hy_b2", [DEPTH, 64, 1])
        self.hy_w3 = inp("hy_w3", [DEPTH, 64, 1024])
        self.hy_cw = inp("hy_cw", [DEPTH, 3, 768])
        self.hy_cb = inp("hy_cb", [DEPTH, 768])
        self.hy_bias = inp("hy_bias", [DEPTH, 2, 256])
        self.jmat = inp("jmat", [128, 128])
        self.GK = {LL: self.dram("GKl", [2, 256, 2 * LL], BF16), LC: self.dram("GKc", [2, 256, 2 * LC], BF16)}

    def hy_filters(self, l):
        P = self.P
        TWO_PI = 2.0 * math.pi
        self.phase()
        w1, k_w1 = self.alloc(64, "hw1"); w2, k_w2 = self.alloc(64, "hw2"); w3, k_w3 = self.alloc(1024, "hw3")
        b1, k_b1 = self.alloc(1, "hb1"); b2, k_b2 = self.alloc(1, "hb2"); fr, k_fr = self.alloc(2, "hfr")
        self.dma(w1[0:33, :], self.hy_w1[l], [], [k_w1])
        self.dma(w2[0:64, :], self.hy_w2[l], [], [k_w2])
        self.dma(w3[0:64, :], self.hy_w3[l], [], [k_w3])
        self.dma(b1[0:64, :], self.hy_b1[l], [], [k_b1])
        self.dma(b2[0:64, :], self.hy_b2[l], [], [k_b2])
        self.dma(fr[0:64, :], self.hy_fr[l], [], [k_fr])
        self.ts("dve", fr[0:64, :], fr[0:64, :], 1.0 / TWO_PI, None, ALU.mult, None, [k_fr], [k_fr])
        ze, k_ze = self.alloc(512, "hze")
        t1, k_t1 = self.alloc(512, "ht1"); t2, k_t2 = self.alloc(512, "ht2")
        tiv, k_ti = self.alloc(512, "hti")
        tint = tiv.bitcast(I32)
        h1, k_h1 = self.alloc(512, "hh1")
        h2, k_h2 = self.alloc(LL, "hh2")
        hf, k_hf = self.alloc(LL, "hhf"); hb, k_hb = self.alloc(LL, "hhb")
        win, k_win = self.alloc(LL, "hwin")
        Gt, k_gt = self.bf([2 * LL], "hGt")

        def sin_layer(dst, kd, ps, kps, bias, kb, frs, n):
            P.op("dve", lambda e: e.tensor_scalar(t1[0:64, 0:n], ps, bias, frs, ALU.add, ALU.mult), [kps, kb, k_fr], [k_t1])
            self.cp("dve", tint[0:64, 0:n], t1[0:64, 0:n], [k_t1], [k_ti])
            self.cp("dve", t2[0:64, 0:n], tint[0:64, 0:n], [k_ti], [k_t2])
            self.tt("dve", t1[0:64, 0:n], t1[0:64, 0:n], t2[0:64, 0:n], ALU.subtract, [k_t1, k_t2], [k_t1])
            self.act(dst, t1[0:64, 0:n], AF.Sin, [k_t1], [kd], scale=TWO_PI)

        for L in (LL, LC):
            for c0 in range(0, L, 512):
                n = min(512, L - c0)
                self.dma(ze[0:33, 0:n], self.hy_zemb[L][:, c0:c0 + n], [], [k_ze])
                pq, pqk = self.q[0], self.qk[0]
                self.mm(pq[0:64, 0:n], [(w1[0:33, :], ze[0:33, 0:n])], [k_w1, k_ze], [pqk])
                sin_layer(h1[0:64, 0:n], k_h1, pq[0:64, 0:n], pqk, b1[0:64, 0:1], k_b1, fr[0:64, 0:1], n)
                pq2, pqk2 = self.q[1], self.qk[1]
                self.mm(pq2[0:64, 0:n], [(w2[0:64, :], h1[0:64, 0:n])], [k_w2, k_h1], [pqk2])
                sin_layer(h2[0:64, c0:c0 + n], k_h2, pq2[0:64, 0:n], pqk2, b2[0:64, 0:1], k_b2, fr[0:64, 1:2], n)
            for o in range(2):
                for chalf in range(2):
                    self.dma(win[:, 0:L], self.hy_win[L][chalf * 128:(chalf + 1) * 128, :], [], [k_win])
                    for di, (dst, kd) in enumerate(((hf, k_hf), (hb, k_hb))):
                        cc = o * 512 + di * 256 + chalf * 128
                        for c0 in range(0, L, 512):
                            n = min(512, L - c0)
                            pq, pqk = self.q[2 + (c0 // 512) % 2], self.qk[2 + (c0 // 512) % 2]
                            self.mm(pq[:, 0:n], [(w3[0:64, cc:cc + 128], h2[0:64, c0:c0 + n])], [k_w3, k_h2], [pqk])
                            self.tt("dve", dst[:, c0:c0 + n], pq[:, 0:n], win[:, c0:c0 + n], ALU.mult, [pqk, k_win], [kd])
                    self.cp("pool", Gt[:, 0:L - 1], hb[:, 1:L][:, ::-1], [k_hb], [k_gt])
                    self.tt("pool", Gt[:, L - 1:L], hf[:, 0:1], hb[:, 0:1], ALU.add, [k_hf, k_hb], [k_gt])
                    self.cp("pool", Gt[:, L:2 * L - 1], hf[:, 1:L], [k_hf], [k_gt])
                    P.op("pool", lambda e, L=L: e.memset(Gt[:, 2 * L - 1:2 * L], 0.0), [], [k_gt])
                    self.dma(self.GK[L][o, chalf * 128:(chalf + 1) * 128, :], Gt[:, 0:2 * L], [k_gt], [("GK", L)])

    def mixer_hy(self, l, allk):
        P = self.P
        self.hy_filters(l)
        self.phase()
        NBT = 34
        Wg, k_wg = self.bf([8, 3, 192], "yWg")
        stg, k_stg = self.f32([8, 192], "ystg")
        cwb, k_cwb = self.f32([3, 192], "ycwb")
        cbb, k_cbb = self.alloc(192, "ycbb")
        bsb, k_bsb = self.f32([2, 64], "ybsb")
        PJ, k_pj = self.f32([NBT, 192], "yPJ")
        Zl, k_zl = self.bf([94, 64], "yZl")
        Zc, k_zc = self.bf([4, 64], "yZc")
        zb, k_zb = self.bf([32, 64], "yzb")
        z1, k_z1 = self.f32([32, 64], "yz1")
        tmp, k_tmp = self.f32([32, 64], "ytmp")
        strips = [self.bf([2 * LL - 128], "ystrip%d" % i) for i in range(2)]
        j32, k_j32 = self.alloc(128, "yj32")
        Jb, k_jb = self.bf([128], "yJb")
        ident, k_id = self.alloc(128, "yident")
        OT = [self.alloc(512, "yOT%d" % i) for i in range(2)]
        self.dma(j32, self.jmat, [], [k_j32])
        self.cp("dve", Jb, j32, [k_j32], [k_jb])
        self.dma(ident, self.ident_in, [], [k_id])
        P.op("pool", lambda e: e.memset(Zl, 0.0), [], [k_zl])
        P.op("pool", lambda e: e.memset(Zc, 0.0), [], [k_zc])
        wv = self.w_mx[l][:, self.CH_HY * 128:(self.CH_HY + 6) * 128].rearrange("(k p) n -> p k n", p=128)
        si = 0
        for cg in range(4):
            for part in range(3):
                cs = part * 256 + cg * 64
                self.dma(stg[:, :, part * 64:(part + 1) * 64], wv[:, :, cs:cs + 64], [], [k_stg])
                self.dma(cwb[:, :, part * 64:(part + 1) * 64], self.hy_cw[l][:, cs:cs + 64].partition_broadcast(128), [], [k_cwb])
                self.dma(cbb[:, part * 64:(part + 1) * 64], self.hy_cb[l][cs:cs + 64].partition_broadcast(128), [], [k_cbb])
            self.dma(bsb, self.hy_bias[l][:, cg * 64:(cg + 1) * 64].partition_broadcast(128), [], [k_bsb])
            for tap in range(3):
                self.tt("dve", Wg[:, :, tap, :], stg, cwb[:, tap, :].unsqueeze(1).to_broadcast([128, 8, 192]), ALU.mult,
                        [k_stg, k_cwb], [k_wg])
            for a in range(NBT):
                pc = pcol(a * 128)
                pq, pqk = self.q[a % 2], self.qk[a % 2]
                self.mm(pq[:, 0:192], [(self.xn[:, kk, pc + tap - 1:pc + tap - 1 + 128], Wg[:, kk, tap, :])
                                       for tap in range(3) for kk in range(8)], ["xn", k_wg], [pqk])
                self.tt("dve", PJ[:, a, :], pq[:, 0:192], cbb, ALU.add, [pqk, k_cbb], [k_pj])
            for (blk0, nb, L, Zp, k_zp) in ((0, 2, LC, Zc, k_zc), (2, 32, LL, Zl, k_zl)):
                X1 = PJ[:, blk0:blk0 + nb, 0:64]
                X2 = PJ[:, blk0:blk0 + nb, 64:128]
                Z0 = PJ[:, blk0:blk0 + nb, 128:192]
                for o in range(2):
                    zprev = Z0 if o == 0 else z1[:, 0:nb, :]
                    kprev = k_pj if o == 0 else k_z1
                    gate = X1 if o == 0 else X2
                    self.cp("pool", zb[:, 0:nb, :], zprev, [kprev], [k_zb])
                    for a0 in range(0, nb, 8):
                        an = min(8, nb - a0)
                        pq, pqk = self.q[2], self.qk[2]
                        self.mm(pq[:, 0:an * 64], [(Jb, zb[:, a0:a0 + an, :])], [k_jb, k_zb], [pqk])
                        self.act(Zp[:, nb - 1 + a0:nb - 1 + a0 + an, :],
                                 pq[:, 0:an * 64].rearrange("p (a c) -> p a c", c=64), AF.Copy, [pqk], [k_zp])
                    for c16 in range(4):
                        pq, pqk = self.q[c16 % 2], self.qk[c16 % 2]
                        for ci in range(16):
                            c = c16 * 16 + ci
                            ch = cg * 64 + c
                            sv, k_sv = strips[si % 2]
                            si += 1
                            W_ = 2 * L - 128
                            gk = self.GK[L]
                            src = bass.AP(tensor=gk.tensor, offset=gk[o, ch, 0:1].offset, ap=[[1, 128], [1, W_]])
                            self.dma(sv[:, 0:W_], src, [("GK", L)], [k_sv])
                            nl = 2 * nb - 1
                            for di in range(nl):
                                d = di - (nb - 1)
                                P.op("pe", lambda e, pq=pq, ci=ci, nb=nb, sv=sv, di=di, d=d, Zp=Zp, c=c, nl=nl: e.matmul(
                                    pq[:, ci * nb:(ci + 1) * nb], sv[:, 128 * di:128 * di + 128],
                                    Zp[:, nb - 1 - d:nb - 1 - d + nb, c], start=(di == 0), stop=(di == nl - 1)),
                                    [k_sv, k_zp], [pqk])
                        cs_ = slice(c16 * 16, (c16 + 1) * 16)
                        tv = tmp[:, 0:nb, cs_]
                        self.tt("dve", tv, zprev[:, :, cs_], bsb[:, o, cs_].unsqueeze(1).to_broadcast([128, nb, 16]), ALU.mult,
                                [kprev, k_bsb], [k_tmp])
                        self.tt("dve", tv, tv, pq[:, 0:16 * nb].rearrange("p (c a) -> p a c", a=nb), ALU.add,
                                [k_tmp, pqk], [k_tmp])
                    if o == 0:
                        self.tt("dve", z1[:, 0:nb, :], tmp[:, 0:nb, :], gate, ALU.mult, [k_tmp, k_pj], [k_z1])
                    else:
                        self.tt("dve", tmp[:, 0:nb, :], tmp[:, 0:nb, :], gate, ALU.mult, [k_tmp, k_pj], [k_tmp])
                for a0 in range(0, nb, 4):
                    an = min(4, nb - a0)
                    pq, pqk = self.q[3], self.qk[3]
                    for ai in range(an):
                        P.op("pe", lambda e, pq=pq, ai=ai, a0=a0: e.transpose(
                            pq[0:64, ai * 128:(ai + 1) * 128], tmp[:, a0 + ai, :], ident), [k_tmp, k_id], [pqk])
                    ov, k_ov = OT[(a0 // 4) % 2]
                    self.act(ov[0:64, 0:an * 128], pq[0:64, 0:an * 128], AF.Copy, [pqk], [k_ov])
                    col = (blk0 + a0) * 128
                    self.dma(self.MIX[768 + cg * 64:768 + (cg + 1) * 64, col:col + an * 128], ov[0:64, 0:an * 128],
                             [k_ov], allk)


def _pk(v, k):
    return np.ascontiguousarray(v.reshape(k, 128).T)


def prep_shared(inp):
    f = lambda a: np.ascontiguousarray(a, dtype=np.float32)
    sh = {}
    sh["ada_w"] = f(inp["ada_w"])
    sh["ada_b"] = f(inp["ada_b"].reshape(DEPTH, 48, 128).transpose(0, 2, 1))
    sh["nmw"] = f(inp["norm_mix_w"].reshape(DEPTH, 8, 128).transpose(0, 2, 1))
    sh["nfw"] = f(inp["norm_ffn_w"].reshape(DEPTH, 8, 128).transpose(0, 2, 1))
    sh["fnw"] = f(_pk(inp["final_norm_w"], 8))
    sh["w_in"] = f(inp["w_in"])
    sh["w_out"] = f(inp["w_out"])
    sh["w_up"] = f(inp["ffn_w_up"])
    sh["w_dn"] = f(inp["ffn_w_down"])
    sh["fcw"] = f(inp["ffn_conv_w"].reshape(DEPTH, 3, 22, 128).transpose(0, 3, 1, 2))
    sh["fcb"] = f(inp["ffn_conv_b"].reshape(DEPTH, 22, 128).transpose(0, 2, 1))
    sh["mnw"] = f(inp["merge_norm_w"].reshape(DEPTH, 8, 128).transpose(0, 2, 1))
    return sh


def prep_core(inp, b):
    d = {}
    d["xin"] = np.ascontiguousarray(np.concatenate([inp["ctx"][b].T, inp["x"][b].T], axis=1), dtype=np.float32)
    cv = np.zeros((128, 8, 2), np.float32)
    cv[:, :, 0] = _pk(inp["c"][b], 8)
    cv[:, :, 1] = _pk(inp["c_ctx"], 8)
    d["cvec"] = cv
    return d


def kernel(**inputs):
    inp = {k: np.asarray(v) for k, v in inputs.items()}
    bld = Builder()
    nc = bld.build()
    sh = prep_shared(inp)
    sh.update(prep_mixer_shared(inp))
    in_maps = []
    for b in range(NCORES):
        d = dict(sh)
        d.update(prep_core(inp, b))
        in_maps.append({k: d[k] for k in bld.din})
    res = run_bass_kernel_spmd(nc, in_maps, core_ids=list(range(NCORES)))
    outs = [np.asarray(r["out"]).T for r in res.results]
    return np.ascontiguousarray(np.stack(outs, axis=0).astype(np.float32))


def prep_mixer_shared(inp):
    f = lambda a: np.ascontiguousarray(a, dtype=np.float32)
    sh = {}
    q0, ff0, fb0, v0, g0, u0, hy0, tq0, tk0, tv0 = 0, 256, 512, 768, 1024, 1280, 1536, 2304, 2560, 2688
    cols = []
    for h in range(4):
        for base in (q0, ff0, fb0, v0):
            c = [base + 64 * h + k for k in range(64)]
            cols += c + c
    cols += list(range(g0, g0 + 256)) + list(range(u0, u0 + 256))
    for kv in range(2):
        for par in range(2):
            cols += [tq0 + (2 * kv + hl) * 64 + 2 * i + par for hl in range(2) for i in range(32)]
    for kv in range(2):
        for par in range(2):
            cols += [tk0 + kv * 64 + 2 * i + par for hl in range(2) for i in range(32)]
    cols += list(range(hy0, hy0 + 768)) + list(range(tv0, tv0 + 128))
    assert len(cols) == 31 * 128
    sh["w_mx"] = f(inp["w_in"][:, :, np.array(cols)])
    L = LL
    row = np.repeat(np.arange(L // 64), 64).astype(np.float32)
    col = np.tile(np.arange(64), L // 64).astype(np.float32)
    inv = (1.0 / (10000.0 ** (np.arange(0, 32, 2, dtype=np.float32) / 32.0))).astype(np.float32)
    ang = np.concatenate([row[:, None] * inv, col[:, None] * inv], axis=-1).astype(np.float32)
    cs, sn = np.cos(ang).T, np.sin(ang).T
    sh["ropeC"] = f(np.tile(cs, (4, 1)))
    sh["ropeS"] = f(np.concatenate([np.tile(sn, (2, 1)), -np.tile(sn, (2, 1))], axis=0))
    sk = np.zeros((DEPTH, 128, 2), np.float32)
    for kv in range(2):
        for r in range(128):
            sk[:, r, kv] = inp["att_sink"][:, 2 * kv + r // 64]
    sh["sinkp"] = sk
    s_ = np.arange(128)[:, None]
    t_ = np.arange(128)[None, :]
    sh["cmask"] = f(np.stack([(s_ >= t_), (s_ <= t_)], axis=1))
    r = np.arange(128)
    sh["hmask"] = f(np.stack([((r % 64) // 32 == 0), ((r % 64) // 32 == 1)], axis=1))
    op = np.zeros((128, 2, 128), np.float32)
    op[:, 0, 0:64] = 1.0
    op[:, 1, 64:128] = 1.0
    sh["onesp"] = op
    def gp(a):
        return f(a.reshape(DEPTH, 2, 8, 2, 64).transpose(0, 3, 4, 1, 2).reshape(DEPTH, 128, 2, 8))
    sh["s5_are"] = gp(inp["s5_a_re"])
    sh["s5_aim"] = gp(inp["s5_a_im"])
    sh["s5_ls"] = gp(np.broadcast_to(inp["s5_log_step"][:, :, :, None], (DEPTH, 2, 16, 64)))
    def bp(a):
        return f(a.reshape(DEPTH, 8, 2, 64, 16).transpose(0, 2, 3, 1, 4).reshape(DEPTH, 128, 8, 16))
    sh["s5_bre"] = bp(inp["s5_b_re"])
    sh["s5_bim"] = bp(inp["s5_b_im"])
    def cpad(a):
        o = np.zeros((DEPTH, 2, 64, 2, 8, 128), np.float32)
        for j in range(8):
            for gl in range(2):
                m0 = 32 * (j % 4) + 16 * gl
                o[:, gl, :, :, j, m0:m0 + 16] = a[:, :, 2 * j + gl, :, :].transpose(0, 3, 1, 2)
        return f(o.reshape(DEPTH, 128, 2, 8, 128))
    sh["s5_cre"] = cpad(inp["s5_c_re"])
    sh["s5_cim"] = cpad(inp["s5_c_im"])
    sh["s5_dv"] = f(inp["s5_d"].reshape(DEPTH, 2, 128).transpose(0, 2, 1))
    sh["s5_glu"] = f(inp["s5_glu_w"])
    sh["ident"] = f(np.eye(128))
    sh["iota"] = f(np.broadcast_to(np.arange(1024, dtype=np.float32)[None, :], (128, 1024)))
    lg = inp["hg_lb_logits"].reshape(DEPTH, 2, 4, 64)
    lg = lg.transpose(3, 1, 2, 0)
    sh["hg_lbl"] = f(np.concatenate([lg, lg], axis=0))
    E = np.zeros((128, 4096), np.float32)
    for k in range(64):
        E[k, 64 * k:64 * k + 64] = 1.0
    sh["hg_E"] = E
    S2 = np.zeros((128, 32, 64), np.float32)
    for j in range(32):
        S2[0:64, j, 2 * j] = 1.0
        S2[64:128, j, 2 * j + 1] = 1.0
    sh["hg_S2"] = S2.reshape(128, 2048)
    sh.update(prep_hyena_shared(inp))
    return sh


def prep_hyena_shared(inp):
    f = lambda a: np.ascontiguousarray(a, dtype=np.float32)
    sh = {}
    for L, tag in ((LL, "l"), (LC, "c")):
        t01 = np.linspace(0.0, 1.0, L, dtype=np.float32)[:, None]
        w = (2.0 * np.float32(math.pi) * np.arange(L, dtype=np.float32)[:, None] / np.float32(L)).astype(np.float32)
        fr = np.linspace(1e-4, 15.0, 16, dtype=np.float32)[None, :]
        z = np.concatenate([t01, np.cos(fr * w), -np.sin(fr * w)], axis=-1).astype(np.float32)
        sh["hy_zemb_" + tag] = f(z.T)
        hmin = math.log(1e-2) / 1.5
        hmax = math.log(1e-2) / 0.3
        deltas = np.linspace(hmin, hmax, 256, dtype=np.float32)
        win = np.exp(-t01 * np.abs(deltas)[None, :]).astype(np.float32)
        sh["hy_win_" + tag] = f(win.T)
    sh["hy_w1"] = f(inp["hy_w1"])
    sh["hy_b1"] = f(inp["hy_b1"][:, :, None])
    sh["hy_fr"] = f(inp["hy_freq"].transpose(0, 2, 1))
    sh["hy_w2"] = f(inp["hy_w2"])
    sh["hy_b2"] = f(inp["hy_b2"][:, :, None])
    sh["hy_w3"] = f(inp["hy_w3"])
    sh["hy_cw"] = f(inp["hy_conv_w"])
    sh["hy_cb"] = f(inp["hy_conv_b"])
    sh["hy_bias"] = f(inp["hy_bias"])
    sh["jmat"] = f(np.eye(128)[::-1])
    return sh
```

```python
import contextlib
import math
import numpy as np
import concourse.bass as bass
import concourse.mybir as mybir
from concourse.bass_utils import run_bass_kernel_spmd

F32 = mybir.dt.float32
BF16 = mybir.dt.bfloat16
I32 = mybir.dt.int32
ALU = mybir.AluOpType
AF = mybir.ActivationFunctionType

D = 1024
DEPTH = 4
LC = 256
LL = 4096
T = LC + LL
DFF = 2816
EPS = 1e-6
NCORES = 8
TILES = [(0, 256, 1)] + [(256 + 512 * i, 512, 0) for i in range(8)]
TP = T + 3


def pcol(c):
    return c + 1 if c < LC else c + 2


ENGS = ("pe", "act", "dve", "pool", "sp")
NDMA = 12


class Prog:
    def __init__(self, nc):
        self.nc = nc
        self.ops = {e: [] for e in ENGS}
        self.cnt = {e: 0 for e in ENGS}
        self.seen = {e: {} for e in ENGS}
        self.lastw = {}
        self.readers = {}
        self.dma_i = {e: 0 for e in ENGS}
        self.dma_val = {}
        self.sems = {}

    def _wait(self, eng, tok):
        sk, v = tok
        if sk == "pe" and eng == "pe":
            return
        if self.seen[eng].get(sk, 0) >= v:
            return
        self.seen[eng][sk] = v
        self.ops[eng].append(("wait", sk, v))

    def _deps(self, eng, reads, writes):
        for k in reads:
            if k in self.lastw:
                self._wait(eng, self.lastw[k])
        for k in writes:
            if k in self.lastw:
                self._wait(eng, self.lastw[k])
            for t in self.readers.get(k, ()):
                self._wait(eng, t)

    def _commit(self, tok, reads, writes):
        for k in reads:
            lst = self.readers.setdefault(k, [])
            lst.append(tok)
            if len(lst) > 32:
                d = {}
                for sk, v in lst:
                    d[sk] = max(d.get(sk, 0), v)
                self.readers[k] = list(d.items())
        for k in writes:
            self.lastw[k] = tok
            self.readers[k] = []

    def op(self, eng, fn, reads=(), writes=()):
        self._deps(eng, reads, writes)
        self.cnt[eng] += 1
        tok = (eng, self.cnt[eng])
        self.ops[eng].append(("op", fn))
        self._commit(tok, reads, writes)
        return tok

    def dma(self, eng, out, in_, reads=(), writes=(), **kw):
        self._deps(eng, reads, writes)
        i = self.dma_i[eng]
        self.dma_i[eng] += 1
        slot = (eng, i % NDMA)
        prev = self.dma_val.get(slot, 0)
        if prev:
            self._wait(eng, (slot, prev))
        val = prev + 16
        self.dma_val[slot] = val
        self.ops[eng].append(("dma", out, in_, slot, kw))
        tok = (slot, val)
        self._commit(tok, reads, writes)
        return tok

    def finish(self, toks):
        for t in toks:
            self._wait("sp", t)

    def emit(self):
        nc = self.nc
        with contextlib.ExitStack() as st:
            semkeys = list(ENGS)
            for e in ENGS:
                for s in range(NDMA):
                    if (e, s) in self.dma_val:
                        semkeys.append((e, s))
            for sk in semkeys:
                nm = sk if isinstance(sk, str) else "d_%s_%d" % sk
                self.sems[sk] = st.enter_context(nc.semaphore("s_" + nm))
            block = st.enter_context(nc.Block())

            def run(eng_obj, lst, ename):
                sem_own = self.sems[ename]
                for o in lst:
                    if o[0] == "wait":
                        eng_obj.wait_ge(self.sems[o[1]], o[2])
                    elif o[0] == "op":
                        o[1](eng_obj).then_inc(sem_own, 1)
                    else:
                        _, out, in_, slot, kw = o
                        eng_obj.dma_start(out=out, in_=in_, **kw).then_inc(self.sems[slot], 16)

            @block.tensor
            def _(e):
                run(e, self.ops["pe"], "pe")

            @block.scalar
            def _(e):
                run(e, self.ops["act"], "act")

            @block.vector
            def _(e):
                run(e, self.ops["dve"], "dve")

            @block.gpsimd
            def _(e):
                run(e, self.ops["pool"], "pool")

            @block.sync
            def _(e):
                run(e, self.ops["sp"], "sp")


def _barrier(P):
    toks = [(e, P.cnt[e]) for e in ENGS if P.cnt[e] > 0]
    toks += [(slot, v) for slot, v in P.dma_val.items()]
    for e in ENGS:
        for t in toks:
            P._wait(e, t)


ARENA_WORDS = 51200
XN_WORDS = 8 * TP // 2 + 4


class Builder:
    def __init__(self, depth=DEPTH, mixers=("hg", "s5", "hy", "at"), debug=()):
        self.depth = depth
        self.mixers = mixers
        self.debug = debug
        self.nc = bass.Bass("TRN2", target_bir_lowering=False)
        self.P = Prog(self.nc)
        self.st = contextlib.ExitStack()
        self.din = {}
        self.out_toks = []

    def inp(self, name, shape, dt=F32):
        self.din[name] = self.nc.dram_tensor(name, list(shape), dt, kind="ExternalInput").ap()
        return self.din[name]

    def dram(self, name, shape, dt=F32, kind="Internal"):
        return self.nc.dram_tensor(name, list(shape), dt, kind=kind).ap()

    def phase(self):
        _barrier(self.P)
        self.ptr = self.base
        self.nphase += 1

    def alloc(self, words, name):
        a = self.ptr
        self.ptr += (words + 1) // 2 * 2
        assert self.ptr <= ARENA_WORDS, (name, self.ptr)
        key = "%s@%d" % (name, self.nphase)
        return self.arena[:, a:a + words], key

    def f32(self, shape, name):
        n = int(np.prod(shape))
        v, k = self.alloc(n, name)
        if len(shape) == 2:
            v = v.rearrange("p (a b) -> p a b", a=shape[0])
        elif len(shape) == 3:
            v = v.rearrange("p (a b c) -> p a b c", a=shape[0], b=shape[1])
        return v, k

    def bf(self, shape, name):
        n = int(np.prod(shape))
        v, k = self.alloc((n + 1) // 2, name)
        v = v.bitcast(BF16)[:, 0:n]
        if len(shape) == 2:
            v = v.rearrange("p (a b) -> p a b", a=shape[0])
        elif len(shape) == 3:
            v = v.rearrange("p (a b c) -> p a b c", a=shape[0], b=shape[1])
        return v, k

    def act(self, out, in_, func, reads, writes, **kw):
        return self.P.op("act", lambda e: e.activation(out, in_, func, **kw), reads, writes)

    def tt(self, eng, out, a, b, op, reads, writes):
        return self.P.op(eng, lambda e: e.tensor_tensor(out, a, b, op), reads, writes)

    def ts(self, eng, out, a, s1, s2, op0, op1, reads, writes):
        if s2 is None:
            return self.P.op(eng, lambda e: e.tensor_scalar(out, a, s1, None, op0), reads, writes)
        return self.P.op(eng, lambda e: e.tensor_scalar(out, a, s1, s2, op0, op1), reads, writes)

    def stt(self, eng, out, a, s, b, op0, op1, reads, writes):
        return self.P.op(eng, lambda e: e.scalar_tensor_tensor(out, a, s, b, op0, op1), reads, writes)

    def cp(self, eng, out, in_, reads, writes):
        return self.P.op(eng, lambda e: e.tensor_copy(out, in_), reads, writes)

    def mm(self, out, pairs, reads, writes):
        n = len(pairs)
        for i, (l, r) in enumerate(pairs):
            self.P.op("pe", lambda e, l=l, r=r, i=i: e.matmul(out, l, r, start=(i == 0), stop=(i == n - 1)),
                      reads, writes)

    def dma(self, out, in_, reads, writes, eng="sp"):
        return self.P.dma(eng, out, in_, reads, writes)

    def build(self):
        nc, P = self.nc, self.P
        inp = self.inp
        xin = inp("xin", [D, T])
        cvec = inp("cvec", [128, 8, 2])
        ada_w = inp("ada_w", [DEPTH, D, 6 * D])
        ada_b = inp("ada_b", [DEPTH, 128, 48])
        nmw = inp("nmw", [DEPTH, 128, 8])
        nfw = inp("nfw", [DEPTH, 128, 8])
        fnw = inp("fnw", [128, 8])
        self.w_in = inp("w_in", [DEPTH, D, 2816])
        self.w_out = inp("w_out", [DEPTH, D, D])
        self.w_up = inp("w_up", [DEPTH, D, 2 * DFF])
        self.w_dn = inp("w_dn", [DEPTH, DFF, D])
        fcw = inp("fcw", [DEPTH, 128, 3, 22])
        fcb = inp("fcb", [DEPTH, 128, 22])
        mnw = inp("mnw", [DEPTH, 128, 8])
        self.declare_mixer_inputs()
        out = self.dram("out", [D, LL], kind="ExternalOutput")
        X = self.dram("Xs", [D, T])
        self.X = X
        self.MIX = self.dram("MIXs", [1536, T], kind=("ExternalOutput" if getattr(self, "mix_debug", False) else "Internal"))
        self.H = self.dram("Hs", [DFF, T], BF16)
        self.dbg = {}
        for name, shape in self.debug:
            self.dbg[name] = self.dram("dbg_" + name, shape, kind="ExternalOutput")

        self.arena = self.st.enter_context(nc.sbuf_tensor("arena", [128, ARENA_WORDS], F32))
        self.q = [self.st.enter_context(nc.psum_tensor("q%d" % i, [128, 1024], F32)) for i in range(4)]
        self.qk = ["q%d" % i for i in range(4)]
        self.ptr = 0
        self.nphase = 0
        xnv, _ = self.alloc(XN_WORDS, "xn")
        self.xn = xnv.bitcast(BF16)[:, 0:8 * TP].rearrange("p (k n) -> p k n", k=8)
        ob, _ = self.alloc(64, "ones")
        self.ones_bf = ob.bitcast(BF16)
        self.mods, _ = self.f32([48, 2], "mods")
        self.gain1, _ = self.f32([8, 2], "gain1")
        self.gain2, _ = self.f32([8, 2], "gain2")
        cc, _ = self.f32([8, 2], "cc")
        scc, _ = self.f32([8, 2], "scc")
        adab, _ = self.alloc(48, "adab")
        nmw_t, _ = self.alloc(8, "nmw_t")
        nfw_t, _ = self.alloc(8, "nfw_t")
        fnw_t, _ = self.alloc(8, "fnw_t")
        self.mnw_t, _ = self.alloc(8, "mnw_t")
        self.fcw_t, _ = self.f32([3, 22], "fcw_t")
        self.fcb_t, _ = self.alloc(22, "fcb_t")
        self.alloc_mixer_persistent()
        self.base = self.ptr
        mods, gain1, gain2 = self.mods, self.gain1, self.gain2
        q, qk = self.q, self.qk

        P.op("pool", lambda e: e.memset(self.ones_bf, 1.0), [], ["ones_bf"])
        P.op("pool", lambda e: e.memset(self.xn, 0.0), [], ["xn"])
        self.dma(cc, cvec, [], ["cc"])
        self.dma(fnw_t, fnw, [], ["fnw_t"])
        self.act(scc, cc, AF.Silu, ["cc"], ["scc"])

        src = xin
        for l in range(self.depth):
            self.phase()
            self.dma(adab, ada_b[l], [], ["adab"])
            self.dma(nmw_t, nmw[l], [], ["nmw_t"])
            self.dma(nfw_t, nfw[l], [], ["nfw_t"])
            self.dma(self.mnw_t, mnw[l], [], ["mnw_t"])
            self.dma(self.fcw_t, fcw[l], [], ["fcw_t"])
            self.dma(self.fcb_t, fcb[l], [], ["fcb_t"])
            stg = [self.f32([8, 512], "astage%d" % i) for i in range(2)]
            awv = ada_w[l].rearrange("(k p) n -> p k n", p=128)
            for og in range(12):
                sv, sk = stg[og % 2]
                self.dma(sv, awv[:, :, og * 512:(og + 1) * 512], [], [sk])
                pa, pk = q[og % 2], qk[og % 2]
                for j in range(4):
                    self.mm(pa[:, 2 * j:2 * j + 2],
                            [(sv[:, k, 128 * j:128 * j + 128], scc[:, k, :]) for k in range(8)], [sk, "scc"], [pk])
                self.tt("dve", mods[:, og * 4:(og + 1) * 4, :], pa[:, 0:8].rearrange("p (j v) -> p j v", v=2),
                        adab[:, og * 4:(og + 1) * 4].unsqueeze(2).to_broadcast([128, 4, 2]), ALU.add,
                        [pk, "adab"], ["mods"])
            self.stt("dve", gain1, mods[:, 8:16, :], 1.0, nmw_t.unsqueeze(2).to_broadcast([128, 8, 2]),
                     ALU.add, ALU.mult, ["mods", "nmw_t"], ["gain1"])
            self.stt("dve", gain2, mods[:, 32:40, :], 1.0, nfw_t.unsqueeze(2).to_broadcast([128, 8, 2]),
                     ALU.add, ALU.mult, ["mods", "nfw_t"], ["gain2"])
            self.norm_phase(src, gain1, "gain1", 0)
            self.mixers_phase(l)
            self.phase_o(l, src)
            src = X
            self.norm_phase(X, gain2, "gain2", 24)
            self.ffn_up_phase(l)
            self.ffn_down_phase(l)
        self.final_norm(fnw_t, out)
        P.finish(self.out_toks)
        P.emit()
        self.st.close()
        return nc

    def rms_rstd(self, xtile, xkey, n, nchunks, rstd, rt, sqb, pq, pqk, eps=EPS):
        self.act(sqb[:, 0:nchunks, 0:n], xtile[:, 0:nchunks, 0:n], AF.Square, [xkey], ["sqb"])
        self.mm(pq[:, 0:n], [(self.ones_bf, sqb[:, k, 0:n]) for k in range(nchunks)], ["sqb", "ones_bf"], [pqk])
        self.act(rt[:, 0:n], pq[:, 0:n], AF.Sqrt, [pqk], ["rt"], bias=eps, scale=1.0 / (128 * nchunks))
        self.P.op("dve", lambda e: e.reciprocal(rstd[:, 0:n], rt[:, 0:n]), ["rt"], ["rstd"])

    def norm_phase(self, src, gain, gkey, shift_base):
        self.phase()
        xts = [self.f32([8, 512], "xt%d" % i) for i in range(2)]
        sqb, _ = self.bf([8, 512], "sqb")
        rstd, _ = self.alloc(512, "rstd")
        rt, _ = self.alloc(512, "rt")
        for pc_ in (0, LC + 1, TP - 1):
            self.P.op("pool", lambda e, pc_=pc_: e.memset(self.xn[:, :, pc_:pc_ + 1], 0.0), [], ["xn"])
        for ti, (c0, n, isc) in enumerate(TILES):
            xtile, xk = xts[ti % 2]
            self.dma(xtile[:, :, 0:n], src[:, c0:c0 + n].rearrange("(k p) n -> p k n", p=128), [("X", ti)], [xk])
            pq, pqk = self.q[ti % 2], self.qk[ti % 2]
            self.rms_rstd(xtile, xk, n, 8, rstd, rt, sqb, pq, pqk)
            self.tt("dve", xtile[:, :, 0:n], xtile[:, :, 0:n], rstd[:, 0:n].unsqueeze(1).to_broadcast([128, 8, n]),
                    ALU.mult, [xk, "rstd"], [xk])
            pc = pcol(c0)
            for k in range(8):
                self.act(self.xn[:, k, pc:pc + n], xtile[:, k, 0:n], AF.Identity, [xk, gkey, "mods"], ["xn"],
                         scale=gain[:, k, isc:isc + 1], bias=self.mods[:, shift_base + k, isc:isc + 1])

    def load_w_chunk(self, wsrc_cols, stg, wbf, use_act):
        sv, sk = stg
        wv, wk = wbf
        self.dma(sv, wsrc_cols.rearrange("(k p) n -> p k n", p=128), [], [sk])
        if use_act:
            self.act(wv, sv, AF.Copy, [sk], [wk])
        else:
            self.cp("pool", wv, sv, [sk], [wk])

    def proj_rows(self, wv, wk, dst_fn, m=128):
        for ti, (c0, n, isc) in enumerate(TILES):
            pq, pqk = self.q[2 + ti % 2], self.qk[2 + ti % 2]
            pc = pcol(c0)
            self.mm(pq[0:m, 0:n], [(wv[:, k, 0:m], self.xn[:, k, pc:pc + n]) for k in range(8)], [wk, "xn"], [pqk])
            dst_fn(ti, c0, n, isc, pq[0:m, 0:n], pqk)

    def phase_o(self, l, src):
        self.phase()
        stg = [self.f32([8, 512], "ostage%d" % i) for i in range(2)]
        wo, wok = self.bf([8, 1024], "wo")
        for i in range(2):
            sv, sk = stg[i]
            self.dma(sv, self.w_out[l][:, i * 512:(i + 1) * 512].rearrange("(k p) n -> p k n", p=128), [], [sk])
            self.act(wo[:, :, i * 512:(i + 1) * 512], sv, AF.Copy, [sk], [wok])
        mixt, mk = self.f32([12, 512], "mixt")
        mixb, mbk = self.bf([8, 512], "mixb")
        sqb, _ = self.bf([2, 512], "sqb")
        rstd, _ = self.alloc(512, "rstd")
        rt, _ = self.alloc(512, "rt")
        sg, sgk = self.f32([2, 512], "silug")
        xts = [self.f32([8, 512], "oxt%d" % i) for i in range(2)]
        MIXv = self.MIX.rearrange("(k p) n -> p k n", p=128)
        for ti, (c0, n, isc) in enumerate(TILES):
            xtile, xk = xts[ti % 2]
            self.dma(xtile[:, :, 0:n], src[:, c0:c0 + n].rearrange("(k p) n -> p k n", p=128), [("X", ti)], [xk])
            self.dma(mixt[:, :, 0:n], MIXv[:, :, c0:c0 + n], [("MIX", ti)], [mk])
            self.tt("dve", mixt[:, 0:2, 0:n], mixt[:, 0:2, 0:n], mixt[:, 2:4, 0:n], ALU.add, [mk], [mk])
            self.act(sg[:, :, 0:n], mixt[:, 10:12, 0:n], AF.Silu, [mk], [sgk])
            for g, c in enumerate((0, 4, 6, 8)):
                pq, pqk = self.q[g % 2], self.qk[g % 2]
                self.rms_rstd(mixt[:, c:c + 2, :], mk, n, 2, rstd, rt, sqb, pq, pqk)
                for j in range(2):
                    self.stt("dve", mixt[:, c + j, 0:n], mixt[:, c + j, 0:n], self.mnw_t[:, 2 * g + j:2 * g + j + 1],
                             rstd[:, 0:n], ALU.mult, ALU.mult, [mk, "rstd", "mnw_t"], [mk])
                    if g == 0:
                        self.tt("dve", mixb[:, j, 0:n], mixt[:, j, 0:n], sg[:, j, 0:n], ALU.mult, [mk, sgk], [mbk])
                    else:
                        self.cp("pool", mixb[:, 2 * g + j, 0:n], mixt[:, c + j, 0:n], [mk], [mbk])
            for oc in range(8):
                pq, pqk = self.q[2 + oc % 2], self.qk[2 + oc % 2]
                self.mm(pq[:, 0:n], [(wo[:, k, oc * 128:(oc + 1) * 128], mixb[:, k, 0:n]) for k in range(8)],
                        [wok, mbk], [pqk])
                self.stt("dve", xtile[:, oc, 0:n], pq[:, 0:n], self.mods[:, 16 + oc, isc:isc + 1], xtile[:, oc, 0:n],
                         ALU.mult, ALU.add, [pqk, xk, "mods"], [xk])
            self.dma(self.X[:, c0:c0 + n].rearrange("(k p) n -> p k n", p=128), xtile[:, :, 0:n], [xk], [("X", ti)])

    def ffn_up_phase(self, l):
        self.phase()
        stg = [[self.f32([8, 128], "ustage%d%d" % (i, j)) for j in range(2)] for i in range(2)]
        wbf = [[self.bf([8, 128], "uw%d%d" % (i, j)) for j in range(2)] for i in range(2)]
        afs = [self.alloc(T, "a_full%d" % i) for i in range(2)]
        cfs = [self.alloc(T, "c_full%d" % i) for i in range(2)]
        hb = [self.alloc(T // 2, "hb%d" % i) for i in range(2)]
        Hv = self.H
        segs = [(0, LC), (LC, T)]
        def part_a(oc):
            pb = oc % 2
            a_full, ak = afs[pb]
            c_full, ck = cfs[pb]
            self.load_w_chunk(self.w_up[l][:, oc * 128:(oc + 1) * 128], stg[pb][0], wbf[pb][0], True)
            self.load_w_chunk(self.w_up[l][:, DFF + oc * 128:DFF + (oc + 1) * 128], stg[pb][1], wbf[pb][1], False)
            wa, wak = wbf[pb][0]
            w0 = self.fcw_t[:, 0, oc:oc + 1]
            w1 = self.fcw_t[:, 1, oc:oc + 1]
            w2 = self.fcw_t[:, 2, oc:oc + 1]
            def dst_a(ti, c0, n, isc, ps, pk, a_full=a_full, ak=ak):
                self.act(a_full[:, c0:c0 + n], ps, AF.Copy, [pk], [ak])
            self.proj_rows(wa, wak, dst_a)
            self.act(c_full, a_full, AF.Identity, [ak, "fcw_t", "fcb_t"], [ck], scale=w1, bias=self.fcb_t[:, oc:oc + 1])
            for (s_, e_) in segs:
                self.stt("dve", c_full[:, s_ + 1:e_], a_full[:, s_:e_ - 1], w0, c_full[:, s_ + 1:e_], ALU.mult, ALU.add,
                         [ak, ck, "fcw_t"], [ck])
                self.stt("dve", c_full[:, s_:e_ - 1], a_full[:, s_ + 1:e_], w2, c_full[:, s_:e_ - 1], ALU.mult, ALU.add,
                         [ak, ck, "fcw_t"], [ck])
            self.act(c_full, c_full, AF.Silu, [ck], [ck])

        def part_v(oc):
            pb = oc % 2
            c_full, ck = cfs[pb]
            hv = hb[pb][0].bitcast(BF16)
            hk = hb[pb][1]
            wv_, wvk = wbf[pb][1]
            def dst_v(ti, c0, n, isc, ps, pk, c_full=c_full, ck=ck, hv=hv, hk=hk):
                self.tt("dve", hv[:, c0:c0 + n], ps, c_full[:, c0:c0 + n], ALU.mult, [pk, ck], [hk])
            self.proj_rows(wv_, wvk, dst_v)
            self.dma(Hv[oc * 128:(oc + 1) * 128, :], hv, [hk], [("H", oc)])

        part_a(0)
        for oc in range(22):
            if oc + 1 < 22:
                part_a(oc + 1)
            part_v(oc)

    def ffn_down_phase(self, l):
        self.phase()
        stg = [self.f32([8, 512], "dstage%d" % i) for i in range(2)]
        wd = self.arena[:, 0:11264].bitcast(BF16).rearrange("p (k n) -> p k n", k=22)
        wdk = "xn"
        i = 0
        for k0, kn in ((0, 8), (8, 8), (16, 6)):
            for c in range(2):
                sv, sk = stg[i % 2]
                self.dma(sv[:, 0:kn, :], self.w_dn[l][k0 * 128:(k0 + kn) * 128, c * 512:(c + 1) * 512]
                         .rearrange("(k p) n -> p k n", p=128), [], [sk])
                if i % 2:
                    self.act(wd[:, k0:k0 + kn, c * 512:(c + 1) * 512], sv[:, 0:kn, :], AF.Copy, [sk], [wdk])
                else:
                    self.cp("pool", wd[:, k0:k0 + kn, c * 512:(c + 1) * 512], sv[:, 0:kn, :], [sk], [wdk])
                i += 1
        hts = [self.bf([22, 512], "ht%d" % i) for i in range(2)]
        xts = [self.f32([8, 512], "dxt%d" % i) for i in range(2)]
        Hv = self.H.rearrange("(k p) n -> p k n", p=128)
        for ti, (c0, n, isc) in enumerate(TILES):
            xtile, xk = xts[ti % 2]
            ht, hk = hts[ti % 2]
            self.dma(xtile[:, :, 0:n], self.X[:, c0:c0 + n].rearrange("(k p) n -> p k n", p=128), [("X", ti)], [xk])
            self.dma(ht[:, :, 0:n], Hv[:, :, c0:c0 + n], [("H", oc) for oc in range(22)], [hk])
            for oc in range(8):
                pq, pqk = self.q[oc % 4], self.qk[oc % 4]
                self.mm(pq[:, 0:n], [(wd[:, k, oc * 128:(oc + 1) * 128], ht[:, k, 0:n]) for k in range(22)],
                        [wdk, hk], [pqk])
                self.stt("dve", xtile[:, oc, 0:n], pq[:, 0:n], self.mods[:, 40 + oc, isc:isc + 1], xtile[:, oc, 0:n],
                         ALU.mult, ALU.add, [pqk, xk, "mods"], [xk])
            self.dma(self.X[:, c0:c0 + n].rearrange("(k p) n -> p k n", p=128), xtile[:, :, 0:n], [xk], [("X", ti)])

    def final_norm(self, fnw_t, out):
        self.phase()
        xts = [self.f32([8, 512], "fxt%d" % i) for i in range(2)]
        sqb, _ = self.bf([8, 512], "sqb")
        rstd, _ = self.alloc(512, "rstd")
        rt, _ = self.alloc(512, "rt")
        for ti, (c0, n, isc) in enumerate(TILES):
            if isc:
                continue
            xtile, xk = xts[ti % 2]
            self.dma(xtile[:, :, 0:n], self.X[:, c0:c0 + n].rearrange("(k p) n -> p k n", p=128), [("X", ti)], [xk])
            pq, pqk = self.q[ti % 2], self.qk[ti % 2]
            self.rms_rstd(xtile, xk, n, 8, rstd, rt, sqb, pq, pqk)
            for k in range(8):
                self.stt("dve", xtile[:, k, 0:n], xtile[:, k, 0:n], fnw_t[:, k:k + 1], rstd[:, 0:n], ALU.mult, ALU.mult,
                         [xk, "rstd", "fnw_t"], [xk])
            tok = self.dma(out[:, c0 - LC:c0 - LC + n].rearrange("(k p) n -> p k n", p=128), xtile[:, :, 0:n], [xk], [])
            self.out_toks.append(tok)

    CH_HG = 0
    CH_G = 16
    CH_U = 18
    CH_AQ = 20
    CH_AK = 22
    CH_HY = 24
    CH_TV = 30
    NCH = 31

    def declare_mixer_inputs(self):
        inp = self.inp
        self.w_mx = inp("w_mx", [DEPTH, D, self.NCH * 128])
        self.ropeC = inp("ropeC", [128, LL])
        self.ropeS = inp("ropeS", [128, LL])
        self.sinkp = inp("sinkp", [DEPTH, 128, 2])
        self.cmask = inp("cmask", [128, 2, 128])
        self.hmask = inp("hmask", [128, 2])
        self.onesp = inp("onesp", [128, 2, 128])
        self.s5_are = inp("s5_are", [DEPTH, 128, 2, 8])
        self.s5_aim = inp("s5_aim", [DEPTH, 128, 2, 8])
        self.s5_ls = inp("s5_ls", [DEPTH, 128, 2, 8])
        self.s5_bre = inp("s5_bre", [DEPTH, 128, 8, 16])
        self.s5_bim = inp("s5_bim", [DEPTH, 128, 8, 16])
        self.s5_cre = inp("s5_cre", [DEPTH, 128, 2, 8, 128])
        self.s5_cim = inp("s5_cim", [DEPTH, 128, 2, 8, 128])
        self.s5_dv = inp("s5_dv", [DEPTH, 128, 2])
        self.s5_glu = inp("s5_glu", [DEPTH, 256, 256])
        self.ident_in = inp("ident", [128, 128])
        self.iota_in = inp("iota", [128, 1024])
        self.hg_lbl = inp("hg_lbl", [128, 2, 4, DEPTH])
        self.hg_E = inp("hg_E", [128, 4096])
        self.hg_S2 = inp("hg_S2", [128, 2048])
        self.declare_hyena_inputs()

    def alloc_mixer_persistent(self):
        self.lbt, _ = self.f32([2, 4, DEPTH], "lbt")
        self.oml, _ = self.f32([2, 4, DEPTH], "oml")

    def mixer_setup(self):
        lg, k = self.f32([2, 4, DEPTH], "lbl")
        sm, sk = self.f32([2, 4], "lbsum")
        self.dma(lg, self.hg_lbl, [], [k])
        self.act(lg, lg, AF.Exp, [k], [k])
        self.P.op("dve", lambda e: e.tensor_reduce(sm, lg, mybir.AxisListType.X, ALU.add), [k], [sk])
        self.P.op("dve", lambda e: e.reciprocal(sm, sm), [sk], [sk])
        self.tt("dve", lg, lg, sm.unsqueeze(3).to_broadcast([128, 2, 4, DEPTH]), ALU.mult, [k, sk], [k])
        self.P.op("dve", lambda e: e.memset(self.lbt[:, :, :, 0:1], 0.0), [], ["lbt"])
        for l in range(1, DEPTH):
            self.tt("dve", self.lbt[:, :, :, l:l + 1], self.lbt[:, :, :, l - 1:l], lg[:, :, :, l:l + 1], ALU.add,
                    [k, "lbt"], ["lbt"])
        self.ts("dve", self.oml, self.lbt, -1.0, 1.0, ALU.mult, ALU.add, ["lbt"], ["oml"])

    def wrot_init(self, nbuf=3):
        self.wrot = [(self.f32([8, 128], "wst%d" % i), self.bf([8, 128], "wbf%d" % i)) for i in range(nbuf)]
        self.wrot_i = 0

    def wchunk(self, l, ci):
        stg, wbf = self.wrot[self.wrot_i % len(self.wrot)]
        self.wrot_i += 1
        self.load_w_chunk(self.w_mx[l][:, ci * 128:(ci + 1) * 128], stg, wbf, self.wrot_i % 2 == 0)
        return wbf

    def wchunk_own(self, l, ci, name):
        stg = self.f32([8, 128], name + "_st")
        wbf = self.bf([8, 128], name + "_bf")
        self.load_w_chunk(self.w_mx[l][:, ci * 128:(ci + 1) * 128], stg, wbf, True)
        return wbf

    def zero_mix_rows(self, r0, r1):
        z, zk = self.alloc(T, "zero")
        self.P.op("pool", lambda e: e.memset(z, 0.0), [], [zk])
        for r in range(r0, r1):
            self.dma(self.MIX[r * 128:(r + 1) * 128, :], z, [zk], [("MIX", ti) for ti in range(len(TILES))])

    def mixers_phase(self, l):
        if l == 0:
            self.phase()
            self.mixer_setup()
        allk = [("MIX", ti) for ti in range(len(TILES))]
        self.phase()
        if "hg" in self.mixers:
            self.mixer_hg(l, allk)
        else:
            self.zero_mix_rows(0, 4)
        self.phase()
        self.wrot_init()
        for c in range(2):
            wv, wk = self.wchunk(l, self.CH_G + c)
            gb, gk = self.alloc(T, "gfull%d" % c)
            def dst(ti, c0, n, isc, ps, pk, gb=gb, gk=gk):
                self.act(gb[:, c0:c0 + n], ps, AF.Copy, [pk], [gk])
            self.proj_rows(wv, wk, dst)
            self.dma(self.MIX[1280 + c * 128:1280 + (c + 1) * 128, :], gb, [gk], allk)
        self.phase()
        if "s5" in self.mixers:
            self.mixer_s5(l, allk)
        else:
            self.zero_mix_rows(4, 6)
        self.phase()
        if "hy" in self.mixers:
            self.mixer_hy(l, allk)
        else:
            self.zero_mix_rows(6, 8)
        self.phase()
        if "at" in self.mixers:
            self.mixer_at(l, allk)
        else:
            self.zero_mix_rows(8, 10)

    def mixer_at(self, l, allk):
        P = self.P
        self.wrot_init()
        HW = 2048
        Ch, ck = self.alloc(HW, "ropeC")
        Sh, sk = self.alloc(HW, "ropeS")
        cm32, cmk = self.f32([2, 128], "cm32")
        cmask, cmbk = self.bf([2, 128], "cmask")
        self.dma(cm32, self.cmask, [], [cmk])
        self.cp("dve", cmask, cm32, [cmk], [cmbk])
        hm, hmk = self.alloc(2, "hmask")
        self.dma(hm, self.hmask, [], [hmk])
        op32, opk = self.f32([2, 128], "op32")
        onesp, onk = self.bf([2, 128], "onesp")
        self.dma(op32, self.onesp, [], [opk])
        self.cp("dve", onesp, op32, [opk], [onk])
        esink, esk = self.alloc(2, "esink")
        self.dma(esink, self.sinkp[l], [], [esk])
        self.act(esink, esink, AF.Exp, [esk], [esk])
        raw, rk = self.alloc(T, "raw")
        tb, tbk = self.alloc(HW, "ropeB")
        tsw, tswk = self.alloc(HW, "ropeSw")
        Qb, qbk = self.bf([T], "Qb")
        Km = [self.bf([T], "Km%d" % i) for i in range(2)]
        Vp, vpk = self.bf([34, 2, 128], "Vp")
        PT = [self.bf([640], "PT%d" % i) for i in range(2)]
        atb = [self.alloc(128, "atb%d" % i) for i in range(2)]
        dn, dnk = self.alloc(128, "den")
        wtv, wtvk = self.wchunk_own(l, self.CH_TV, "wtv")
        P.op("pool", lambda e: e.memset(Vp, 0.0), [], [vpk])

        def rope_raw():
            for half in range(2):
                c0 = LC + half * HW
                self.dma(Ch, self.ropeC[:, half * HW:(half + 1) * HW], [], [ck])
                self.dma(Sh, self.ropeS[:, half * HW:(half + 1) * HW], [], [sk])
                self.tt("pool", tb, raw[:, c0:c0 + HW], Sh, ALU.mult, [rk, sk], [tbk])
                self.cp("dve", tsw[0:64, :], tb[64:128, :], [tbk], [tswk])
                self.cp("dve", tsw[64:128, :], tb[0:64, :], [tbk], [tswk])
                self.tt("dve", raw[:, c0:c0 + HW], raw[:, c0:c0 + HW], Ch, ALU.mult, [rk, ck], [rk])
                self.tt("dve", raw[:, c0:c0 + HW], raw[:, c0:c0 + HW], tsw, ALU.add, [rk, tswk], [rk])

        def dstq(ti, c0, n, isc, ps, pk):
            self.act(raw[:, c0:c0 + n], ps, AF.Copy, [pk], [rk])

        for kv in range(2):
            wq, wqk = self.wchunk(l, self.CH_AQ + kv)
            self.proj_rows(wq, wqk, dstq)
            rope_raw()
            self.cp("pool", Qb, raw, [rk], [qbk])
            wk_, wkk = self.wchunk(l, self.CH_AK + kv)
            self.proj_rows(wk_, wkk, dstq)
            rope_raw()
            for hl in range(2):
                kmv, kmk = Km[hl]
                self.ts("dve", kmv, raw, hm[:, hl:hl + 1], None, ALU.mult, None, [rk, hmk], [kmk])
            for b0 in range(0, 34, 8):
                nb = min(8, 34 - b0)
                pq, pqk = self.q[3], self.qk[3]
                for bi in range(nb):
                    blk = b0 + bi
                    pc = pcol(blk * 128)
                    self.mm(pq[:, bi * 64:(bi + 1) * 64],
                            [(self.xn[:, k, pc:pc + 128], wtv[:, k, kv * 64:(kv + 1) * 64]) for k in range(8)],
                            ["xn", wtvk], [pqk])
                src = pq[:, 0:nb * 64].rearrange("p (b d) -> p b d", d=64)
                self.act(Vp[:, b0:b0 + nb, 0, 0:64], src, AF.Copy, [pqk], [vpk])
                self.cp("dve", Vp[:, b0:b0 + nb, 1, 64:128], src, [pqk], [vpk])
            for qb in range(34):
                q0 = qb * 128
                kblocks = [(0, None), (1, None)]
                if qb >= 2:
                    if qb - 1 >= 2:
                        kblocks.append((qb - 1, 0))
                    kblocks.append((qb, None))
                    if qb + 1 < 34:
                        kblocks.append((qb + 1, 1))
                nkb = len(kblocks)
                po, pok = self.q[2], self.qk[2]
                nmm = 2 * nkb
                imm = 0
                for hl in range(2):
                    ps, psk = self.q[hl], self.qk[hl]
                    kmv, kmk = Km[hl]
                    ptv, ptk = PT[hl]
                    for bi, (kb, mk) in enumerate(kblocks):
                        self.mm(ps[:, bi * 128:(bi + 1) * 128],
                                [(kmv[:, kb * 128:(kb + 1) * 128], Qb[:, q0:q0 + 128])], [kmk, qbk], [psk])
                    self.act(ptv[:, 0:nkb * 128], ps[:, 0:nkb * 128], AF.Exp, [psk], [ptk], scale=0.125)
                    for bi, (kb, mk) in enumerate(kblocks):
                        if mk is not None:
                            self.tt("pool", ptv[:, bi * 128:(bi + 1) * 128], ptv[:, bi * 128:(bi + 1) * 128],
                                    cmask[:, mk, :], ALU.mult, [ptk, cmbk], [ptk])
                    for bi, (kb, mk) in enumerate(kblocks):
                        P.op("pe", lambda e, kb=kb, bi=bi, hl=hl, ptv=ptv, imm=imm, po=po, nmm=nmm: e.matmul(
                            po[:, 0:128], Vp[:, kb, hl, :], ptv[:, bi * 128:(bi + 1) * 128],
                            start=(imm == 0), stop=(imm == nmm - 1)), [vpk, ptk], [pok])
                        P.op("pe", lambda e, kb=kb, bi=bi, hl=hl, ptv=ptv, imm=imm, po=po, nmm=nmm: e.matmul(
                            po[:, 512:640], onesp[:, hl, :], ptv[:, bi * 128:(bi + 1) * 128],
                            start=(imm == 0), stop=(imm == nmm - 1)), [onk, ptk], [pok])
                        imm += 1
                av, avk = atb[qb % 2]
                self.ts("dve", dn, po[:, 512:640], esink[:, kv:kv + 1], None, ALU.add, None, [pok, esk], [dnk])
                P.op("dve", lambda e: e.reciprocal(dn, dn), [dnk], [dnk])
                self.tt("dve", av, po[:, 0:128], dn, ALU.mult, [pok, dnk], [avk])
                tix = 0 if qb < 2 else 1 + (qb - 2) // 4
                self.dma(self.MIX[1024 + kv * 128:1024 + (kv + 1) * 128, q0:q0 + 128], av, [avk], [("MIX", tix)])

    def mixer_s5(self, l, allk):
        P = self.P
        TWO_PI = 2.0 * math.pi
        self.wrot_init(1)
        SEG = 512
        segs = [(0, LC)] + [(LC + SEG * i, SEG) for i in range(LL // SEG)]
        def small(name, shape=(2, 8)):
            return self.f32(list(shape), "s5" + name)
        are, k_are = small("are"); aim, k_aim = small("aim"); ls, k_ls = small("ls")
        self.dma(are, self.s5_are[l], [], [k_are])
        self.dma(aim, self.s5_aim[l], [], [k_aim])
        self.dma(ls, self.s5_ls[l], [], [k_ls])
        dt, k_dt = small("dt"); mag, k_mag = small("mag"); th, k_th = small("th")
        tmp, k_tmp = small("tmp"); tmp2, k_tmp2 = small("tmp2")
        ti_v, k_ti = self.alloc(16, "s5ti")
        ti = ti_v.bitcast(I32).rearrange("p (a b) -> p a b", a=2)
        cosv, k_cos = small("cosv"); sinv, k_sin = small("sinv")
        fr, k_fr = small("fr"); fi, k_fi = small("fi")
        self.act(dt, ls, AF.Exp, [k_ls], [k_dt])
        self.tt("dve", tmp, are, dt, ALU.mult, [k_are, k_dt], [k_tmp])
        self.act(mag, tmp, AF.Exp, [k_tmp], [k_mag])
        self.tt("dve", th, aim, dt, ALU.mult, [k_aim, k_dt], [k_th])
        self.ts("dve", th, th, 1.0 / TWO_PI, None, ALU.mult, None, [k_th], [k_th])

        def sin_turns(dst, dkey, src, skey, shift, shp_tmp, k_t, tint, k_i, shp_tmp2, k_t2):
            self.ts("dve", shp_tmp, src, shift, None, ALU.add, None, [skey], [k_t])
            self.cp("dve", tint, shp_tmp, [k_t], [k_i])
            self.cp("dve", shp_tmp2, tint, [k_i], [k_t2])
            self.tt("dve", shp_tmp, shp_tmp, shp_tmp2, ALU.subtract, [k_t, k_t2], [k_t])
            self.act(dst, shp_tmp, AF.Sin, [k_t], [dkey], scale=TWO_PI)

        sin_turns(sinv, k_sin, th, k_th, 0.0, tmp, k_tmp, ti, k_ti, tmp2, k_tmp2)
        sin_turns(cosv, k_cos, th, k_th, 0.25, tmp, k_tmp, ti, k_ti, tmp2, k_tmp2)
        abre, k_abre = small("abre"); abim, k_abim = small("abim")
        self.tt("dve", abre, mag, cosv, ALU.mult, [k_mag, k_cos], [k_abre])
        self.tt("dve", abim, mag, sinv, ALU.mult, [k_mag, k_sin], [k_abim])
        den, k_den = small("den")
        self.tt("dve", den, are, are, ALU.mult, [k_are], [k_den])
        self.tt("dve", tmp, aim, aim, ALU.mult, [k_aim], [k_tmp])
        self.tt("dve", den, den, tmp, ALU.add, [k_den, k_tmp], [k_den])
        P.op("dve", lambda e: e.reciprocal(den, den), [k_den], [k_den])
        nr, k_nr = small("nr")
        self.ts("dve", nr, abre, -1.0, None, ALU.add, None, [k_abre], [k_nr])
        self.tt("dve", fr, nr, are, ALU.mult, [k_nr, k_are], [k_fr])
        self.tt("dve", tmp, abim, aim, ALU.mult, [k_abim, k_aim], [k_tmp])
        self.tt("dve", fr, fr, tmp, ALU.add, [k_fr, k_tmp], [k_fr])
        self.tt("dve", fr, fr, den, ALU.mult, [k_fr, k_den], [k_fr])
        self.tt("dve", fi, abim, are, ALU.mult, [k_abim, k_are], [k_fi])
        self.tt("dve", tmp, nr, aim, ALU.mult, [k_nr, k_aim], [k_tmp])
        self.tt("dve", fi, fi, tmp, ALU.subtract, [k_fi, k_tmp], [k_fi])
        self.tt("dve", fi, fi, den, ALU.mult, [k_fi, k_den], [k_fi])
        bre, k_bre = self.f32([8, 16], "s5bre"); bim, k_bim = self.f32([8, 16], "s5bim")
        self.dma(bre, self.s5_bre[l], [], [k_bre])
        self.dma(bim, self.s5_bim[l], [], [k_bim])
        dv, k_dv = self.alloc(2, "s5dv")
        self.dma(dv, self.s5_dv[l], [], [k_dv])
        ident, k_id = self.alloc(128, "ident")
        self.dma(ident, self.ident_in, [], [k_id])
        iota, k_io = self.alloc(SEG, "iota")
        self.dma(iota, self.iota_in[:, 0:SEG], [], [k_io])
        glu32 = self.f32([2, 256], "glu32")
        glub, k_glub = self.bf([2, 256], "glub")
        self.dma(glu32[0], self.s5_glu[l].rearrange("(k p) n -> p k n", p=128), [], [glu32[1]])
        self.cp("dve", glub, glu32[0], [glu32[1]], [k_glub])
        U, k_u = self.f32([2, T], "s5U")
        Y, k_y = self.f32([2, T], "s5Y")
        P.op("pool", lambda e: e.memset(Y, 0.0), [], [k_y])
        for c in range(2):
            wv, wk = self.wchunk(l, self.CH_U + c)
            def dst(ti_, c0, n, isc, ps, pk, c=c):
                self.act(U[:, c, c0:c0 + n], ps, AF.Copy, [pk], [k_u])
            self.proj_rows(wv, wk, dst)
        bb1, k_bb1 = self.alloc(16, "bb1"); bb2, k_bb2 = self.alloc(16, "bb2")
        Bpad, k_bp = self.alloc(128, "Bpad")
        BBts = [self.f32([2, 128], "BBt%d" % i) for i in range(2)]
        Cres = [self.alloc(128, "Cre%d" % i) for i in range(2)]
        Cims = [self.alloc(128, "Cim%d" % i) for i in range(2)]
        RBs = [self.alloc(SEG, "RB%d" % i) for i in range(2)]
        sts = [self.alloc(2, "s5st%d" % i) for i in range(2)]
        T1, k1 = self.alloc(SEG, "s5T1"); T2, k2 = self.alloc(SEG, "s5T2"); T3, k3 = self.alloc(SEG, "s5T3")
        TF, ktf = self.alloc(SEG, "s5TF")
        TIv, k_TI = self.alloc(SEG, "s5TI")
        TI = TIv.bitcast(I32)
        CSs = [self.alloc(SEG, "s5CS%d" % i) for i in range(2)]
        SNs = [self.alloc(SEG, "s5SN%d" % i) for i in range(2)]
        ZRZ, _ = self.alloc(2 * SEG, "s5ZRZ")
        ZRs = [(ZRZ[:, 0:SEG], "s5ZR0k"), (ZRZ[:, SEG:2 * SEG], "s5ZR1k")]
        ZIs = [self.alloc(SEG, "s5ZI%d" % i) for i in range(2)]
        U1, ku1 = self.alloc(SEG, "s5U1"); U2, ku2 = self.alloc(SEG, "s5U2"); U3, ku3 = self.alloc(SEG, "s5U3")

        def V(ap, c0, n, rev):
            a_ = ap[:, c0:c0 + n]
            return a_[:, ::-1] if rev else a_

        its = []
        for d in range(2):
            order = list(range(len(segs))) if d == 0 else [0] + list(range(len(segs) - 1, 0, -1))
            for j in range(8):
                for oi, si in enumerate(order):
                    its.append((d, j, oi, si))

        def prep_dj(d, j):
            pb = (d * 8 + j) % 2
            BBt, k_bbt = BBts[pb]
            m0 = 32 * (j % 4)
            for ri in range(2):
                x1, kx1 = (bre, k_bre) if ri == 0 else (bim, k_bim)
                x2, kx2 = (bim, k_bim) if ri == 0 else (bre, k_bre)
                self.ts("dve", bb1, x1[:, j, :], fr[:, d, j:j + 1], None, ALU.mult, None, [kx1, k_fr], [k_bb1])
                self.ts("dve", bb2, x2[:, j, :], fi[:, d, j:j + 1], None, ALU.mult, None, [kx2, k_fi], [k_bb2])
                self.tt("dve", bb1, bb1, bb2, ALU.subtract if ri == 0 else ALU.add, [k_bb1, k_bb2], [k_bb1])
                P.op("dve", lambda e: e.memset(Bpad, 0.0), [], [k_bp])
                self.cp("dve", Bpad[0:64, m0:m0 + 16], bb1[0:64, :], [k_bb1], [k_bp])
                self.cp("dve", Bpad[64:128, m0 + 16:m0 + 32], bb1[64:128, :], [k_bb1], [k_bp])
                pq, pqk = self.q[3], self.qk[3]
                P.op("pe", lambda e, pq=pq: e.transpose(pq[:, 512:640], Bpad, ident), [k_bp, k_id], [(pqk, "tr")])
                self.act(BBt[:, ri, :], pq[:, 512:640], AF.Copy, [(pqk, "tr")], [k_bbt])
            Cre, k_cre = Cres[pb]
            Cim, k_cim = Cims[pb]
            self.dma(Cre, self.s5_cre[l][:, d, j, :], [], [k_cre])
            self.dma(Cim, self.s5_cim[l][:, d, j, :], [], [k_cim])
            RB, k_rb = RBs[pb]
            self.act(RB, iota, AF.Identity, [k_io, k_mag], [k_rb], scale=0.0, bias=mag[:, d, j:j + 1])
            st, k_st = sts[pb]
            P.op("dve", lambda e, st=st: e.memset(st, 0.0), [], [k_st])

        def stage_a(idx):
            d, j, oi, si = its[idx]
            if oi == 0:
                prep_dj(d, j)
            pb = (d * 8 + j) % 2
            b = idx % 2
            rev = (d == 1)
            c0, n = segs[si]
            n0 = c0 if d == 0 else (0 if si == 0 else LC + (T - c0 - n))
            BBt, k_bbt = BBts[pb]
            urhs = V(U[:, j // 4, :], c0, n, rev)
            pp, kpp = self.q[b], self.qk[b]
            self.mm(pp[:, 0:n], [(BBt[:, 0, :], urhs)], [k_bbt, k_u], [kpp])
            self.mm(pp[:, 512:512 + n], [(BBt[:, 1, :], urhs)], [k_bbt, k_u], [kpp])
            CS, kc = CSs[b]
            SN, ks = SNs[b]
            P.op("dve", lambda e, n=n, n0=n0, d=d, j=j: e.tensor_scalar(
                T1[:, 0:n], iota[:, 0:n], float(n0), th[:, d, j:j + 1], ALU.add, ALU.mult), [k_io, k_th], [k1])
            for (dst_, kd, shift) in ((SN, ks, 0.0), (CS, kc, 0.25)):
                if shift:
                    self.ts("dve", T1[:, 0:n], T1[:, 0:n], shift, None, ALU.add, None, [k1], [k1])
                self.cp("dve", TI[:, 0:n], T1[:, 0:n], [k1], [k_TI])
                self.cp("dve", TF[:, 0:n], TI[:, 0:n], [k_TI], [ktf])
                self.tt("dve", T2[:, 0:n], T1[:, 0:n], TF[:, 0:n], ALU.subtract, [k1, ktf], [k2])
                self.act(dst_[:, 0:n], T2[:, 0:n], AF.Sin, [k2], [kd], scale=TWO_PI)

        def stage_b(idx):
            d, j, oi, si = its[idx]
            pb = (d * 8 + j) % 2
            b = idx % 2
            rev = (d == 1)
            c0, n = segs[si]
            pp, kpp = self.q[b], self.qk[b]
            pre = pp[:, 0:n]
            pim = pp[:, 512:512 + n]
            CS, kc = CSs[b]
            SN, ks = SNs[b]
            ZR, kzr = ZRs[b]
            ZI, kzi = ZIs[b]
            RB, k_rb = RBs[pb]
            st, k_st = sts[pb]
            Cre, k_cre = Cres[pb]
            Cim, k_cim = Cims[pb]
            self.tt("dve", T1[:, 0:n], pre, CS[:, 0:n], ALU.mult, [kpp, kc], [k1])
            self.tt("dve", T2[:, 0:n], pim, SN[:, 0:n], ALU.mult, [kpp, ks], [k2])
            self.tt("dve", T1[:, 0:n], T1[:, 0:n], T2[:, 0:n], ALU.add, [k1, k2], [k1])
            self.tt("dve", T3[:, 0:n], pim, CS[:, 0:n], ALU.mult, [kpp, kc], [k3])
            self.tt("dve", T2[:, 0:n], pre, SN[:, 0:n], ALU.mult, [kpp, ks], [k2])
            self.tt("dve", T3[:, 0:n], T3[:, 0:n], T2[:, 0:n], ALU.subtract, [k3, k2], [k3])
            P.op("dve", lambda e, n=n: e.tensor_tensor_scan(ZR[:, 0:n], RB[:, 0:n], T1[:, 0:n], st[:, 0:1],
                                                            ALU.mult, ALU.add), [k_rb, k1, k_st], [kzr])
            P.op("dve", lambda e, n=n: e.tensor_tensor_scan(ZI[:, 0:n], RB[:, 0:n], T3[:, 0:n], st[:, 1:2],
                                                            ALU.mult, ALU.add), [k_rb, k3, k_st], [kzi])
            self.act(st[:, 0:1], ZR[:, n - 1:n], AF.Copy, [kzr], [k_st])
            self.act(st[:, 1:2], ZI[:, n - 1:n], AF.Copy, [kzi], [k_st])
            self.tt("dve", U1[:, 0:n], ZR[:, 0:n], CS[:, 0:n], ALU.mult, [kzr, kc], [ku1])
            self.tt("dve", U2[:, 0:n], ZI[:, 0:n], SN[:, 0:n], ALU.mult, [kzi, ks], [ku2])
            self.tt("dve", U1[:, 0:n], U1[:, 0:n], U2[:, 0:n], ALU.subtract, [ku1, ku2], [ku1])
            self.tt("dve", U3[:, 0:n], ZR[:, 0:n], SN[:, 0:n], ALU.mult, [kzr, ks], [ku3])
            self.tt("dve", U2[:, 0:n], ZI[:, 0:n], CS[:, 0:n], ALU.mult, [kzi, kc], [ku2])
            self.stt("dve", U3[:, 0:n], U2[:, 0:n], -1.0, U3[:, 0:n], ALU.mult, ALU.subtract, [ku2, ku3], [ku3])
            pyv = self.q[2 + idx % 2][:, 0:n]
            kpyv = self.qk[2 + idx % 2]
            self.mm(pyv, [(Cre, U1[:, 0:n]), (Cim, U3[:, 0:n])], [k_cre, k_cim, ku1, ku3], [kpyv])
            if idx > 0:
                stage_c(idx - 1)

        def stage_c(idx):
            d, j, oi, si = its[idx]
            c0, n = segs[si]
            pyv = self.q[2 + idx % 2][:, 0:n]
            kpyv = self.qk[2 + idx % 2]
            yv = V(Y[:, j // 4, :], c0, n, (d == 1))
            self.tt("dve", yv, yv, pyv, ALU.add, [k_y, kpyv], [k_y])

        stage_a(0)
        for idx in range(len(its)):
            if idx + 1 < len(its):
                stage_a(idx + 1)
            stage_b(idx)
        stage_c(len(its) - 1)
        _barrier(P)
        zt, kz = ZRZ.rearrange("p (a b) -> p a b", a=2), "s5ztk"
        zbv, kzb = SNs[0]
        zb = zbv.bitcast(BF16).rearrange("p (a b) -> p a b", a=2)
        sg, ksg = SNs[1]
        ob = [CSs[0], CSs[1]]
        for ti_, (c0, n, isc) in enumerate(TILES):
            for c in range(2):
                self.stt("dve", zt[:, c, 0:n], U[:, c, c0:c0 + n], dv[:, c:c + 1], Y[:, c, c0:c0 + n], ALU.mult, ALU.add,
                         [k_u, k_y, k_dv], [kz])
            self.act(zt[:, :, 0:n], zt[:, :, 0:n], AF.Gelu, [kz], [kz])
            self.cp("pool", zb[:, :, 0:n], zt[:, :, 0:n], [kz], [kzb])
            for oc in range(2):
                pq, pqk = self.q[oc], self.qk[oc]
                self.mm(pq[:, 0:n], [(glub[:, k, oc * 128:(oc + 1) * 128], zb[:, k, 0:n]) for k in range(2)],
                        [k_glub, kzb], [pqk])
                self.act(sg[:, 0:n], pq[:, 0:n], AF.Sigmoid, [pqk], [ksg])
                ov, ok = ob[oc]
                self.tt("dve", ov[:, 0:n], zt[:, oc, 0:n], sg[:, 0:n], ALU.mult, [kz, ksg], [ok])
                self.dma(self.MIX[512 + oc * 128:512 + (oc + 1) * 128, c0:c0 + n], ov[:, 0:n], [ok], [("MIX", ti_)])

    def mixer_hg(self, l, allk):
        P = self.P
        self.wrot_init(2)
        SEG = 1024
        segs = [(0, LC)] + [(LC + SEG * i, SEG) for i in range(LL // SEG)]
        Eb, k_e = self.bf([4096], "hgE")
        S2b, k_s2 = self.bf([2048], "hgS2")
        stg, k_stg = self.alloc(1024, "hgstg")
        for i in range(4):
            self.dma(stg, self.hg_E[:, i * 1024:(i + 1) * 1024], [], [k_stg])
            self.cp("dve", Eb[:, i * 1024:(i + 1) * 1024], stg, [k_stg], [k_e])
        for i in range(2):
            self.dma(stg, self.hg_S2[:, i * 1024:(i + 1) * 1024], [], [k_stg])
            self.cp("dve", S2b[:, i * 1024:(i + 1) * 1024], stg, [k_stg], [k_s2])
        Qp, k_q = self.alloc(T, "hgQ")
        F, k_f = self.alloc(T, "hgF")
        FM, k_fm = self.alloc(T, "hgFM")
        OSs = [self.alloc(SEG, "hgOS%d" % i) for i in range(2)]
        VB, k_vb = self.bf([T], "hgVB")
        stt, k_stt = self.alloc(32, "hgst")
        VbS = [self.bf([SEG], "hgVbS%d" % i) for i in range(3)]
        NB = [self.alloc(SEG, "hgNB%d" % i) for i in range(3)]
        SS = [self.alloc(SEG, "hgS%d" % i) for i in range(2)]
        SQ = [self.bf([SEG], "hgSQ%d" % i) for i in range(2)]

        def V(ap, c0, n, rev):
            a = ap[:, c0:c0 + n]
            return a[:, ::-1] if rev else a

        for h in range(4):
            wq, wqk = self.wchunk(l, self.CH_HG + 4 * h + 0)
            def dq(ti, c0, n, isc, ps, pk):
                self.act(Qp[:, c0:c0 + n], ps, AF.Silu, [pk], [k_q])
            self.proj_rows(wq, wqk, dq)
            wv, wvk = self.wchunk(l, self.CH_HG + 4 * h + 3)
            def dv_(ti, c0, n, isc, ps, pk):
                self.act(VB[0:64, c0:c0 + n], ps, AF.Copy, [pk], [k_vb])
            self.proj_rows(wv, wvk, dv_, m=64)
            for d in range(2):
                rev = (d == 1)
                wf, wfk = self.wchunk(l, self.CH_HG + 4 * h + 1 + d)
                def df(ti, c0, n, isc, ps, pk):
                    self.act(F[:, c0:c0 + n], ps, AF.Sigmoid, [pk], [k_f])
                self.proj_rows(wf, wfk, df)
                self.ts("dve", F, F, self.oml[:, d, h, l:l + 1], self.lbt[:, d, h, l:l + 1], ALU.mult, ALU.add,
                        [k_f, "oml", "lbt"], [k_f])
                self.ts("dve", FM, F, -1.0, None, ALU.add, None, [k_f], [k_fm])
                P.op("dve", lambda e: e.memset(stt, 0.0), [], [(k_stt, j_) for j_ in range(32)])
                order = list(range(len(segs))) if d == 0 else [0] + list(range(len(segs) - 1, 0, -1))
                its = [(oi, si, j) for oi, si in enumerate(order) for j in range(32)]

                def stage_a1(idx):
                    oi, si, j = its[idx]
                    c0, n = segs[si]
                    VBv = V(VB[0:64, :], c0, n, rev)
                    pv, kpv = self.q[idx % 2], self.qk[idx % 2]
                    for c in range(0, n, 512):
                        cn = min(512, n - c)
                        self.mm(pv[:, c:c + cn], [(Eb[0:64, 128 * j:128 * j + 128], VBv[:, c:c + cn])],
                                [k_e, k_vb], [kpv])
                    vs, kvs = VbS[idx % 3]
                    self.act(vs[:, 0:n], pv[:, 0:n], AF.Copy, [kpv], [kvs])

                def stage_a2(idx):
                    oi, si, j = its[idx]
                    c0, n = segs[si]
                    FMv = V(FM, c0, n, rev)
                    vs, kvs = VbS[idx % 3]
                    nb_, knb = NB[idx % 3]
                    self.tt("dve", nb_[:, 0:n], FMv, vs[:, 0:n], ALU.mult, [k_fm, kvs], [knb])

                def stage_b(idx):
                    oi, si, j = its[idx]
                    c0, n = segs[si]
                    b = idx % 2
                    Fv = V(F, c0, n, rev)
                    Qv = V(Qp, c0, n, rev)
                    po, kpo = self.q[2 + oi % 2], self.qk[2 + oi % 2]
                    nb_, knb = NB[idx % 3]
                    s_, ks_ = SS[b]
                    P.op("dve", lambda e, s_=s_, Fv=Fv, nb_=nb_, j=j, n=n: e.tensor_tensor_scan(
                        s_[:, 0:n], Fv, nb_[:, 0:n], stt[:, j:j + 1], ALU.mult, ALU.add), [k_f, knb, (k_stt, j)], [ks_])
                    self.act(stt[:, j:j + 1], s_[:, n - 1:n], AF.Copy, [ks_], [(k_stt, j)])
                    sq_, ksq = SQ[b]
                    self.tt("dve", sq_[:, 0:n], s_[:, 0:n], Qv, ALU.mult, [ks_, k_q], [ksq])
                    for c in range(0, n, 512):
                        cn = min(512, n - c)
                        P.op("pe", lambda e, c=c, cn=cn, j=j, sq_=sq_, po=po: e.matmul(
                            po[0:64, c:c + cn], S2b[:, 64 * j:64 * j + 64], sq_[:, c:c + cn],
                            start=(j == 0), stop=(j == 31)), [k_s2, ksq], [kpo])
                    if j == 31:
                        osv, k_os = OSs[oi % 2]
                        self.act(V(osv[0:64, :], 0, n, rev), po[0:64, 0:n], AF.Copy, [kpo], [k_os], scale=-1.0)
                        self.dma(self.MIX[d * 256 + h * 64:d * 256 + (h + 1) * 64, c0:c0 + n], osv[0:64, 0:n], [k_os], allk)

                nit = len(its)
                stage_a1(0)
                stage_a1(1)
                stage_a2(0)
                for idx in range(nit):
                    if idx + 2 < nit:
                        stage_a1(idx + 2)
                    if idx + 1 < nit:
                        stage_a2(idx + 1)
                    stage_b(idx)

    def declare_hyena_inputs(self):
        inp = self.inp
        self.hy_zemb = {LL: inp("hy_zemb_l", [33, LL]), LC: inp("hy_zemb_c", [33, LC])}
        self.hy_win = {LL: inp("hy_win_l", [256, LL]), LC: inp("hy_win_c", [256, LC])}
        self.hy_w1 = inp("hy_w1", [DEPTH, 33, 64])
        self.hy_b1 = inp("hy_b1", [DEPTH, 64, 1])
        self.hy_fr = inp("hy_fr", [DEPTH, 64, 2])
        self.hy_w2 = inp("hy_w2", [DEPTH, 64, 64])
        self.hy_b2 = inp("hy_b2", [DEPTH, 64, 1])
        self.hy_w3 = inp("hy_w3", [DEPTH, 64, 1024])
        self.hy_cw = inp("hy_cw", [DEPTH, 3, 768])
        self.hy_cb = inp("hy_cb", [DEPTH, 768])
        self.hy_bias = inp("hy_bias", [DEPTH, 2, 256])
        self.jmat = inp("jmat", [128, 128])
        self.GK = {LL: self.dram("GKl", [2, 256, 2 * LL], BF16), LC: self.dram("GKc", [2, 256, 2 * LC], BF16)}

    def hy_filters(self, l):
        P = self.P
        TWO_PI = 2.0 * math.pi
        self.phase()
        w1, k_w1 = self.alloc(64, "hw1"); w2, k_w2 = self.alloc(64, "hw2"); w3, k_w3 = self.alloc(1024, "hw3")
        b1, k_b1 = self.alloc(1, "hb1"); b2, k_b2 = self.alloc(1, "hb2"); fr, k_fr = self.alloc(2, "hfr")
        self.dma(w1[0:33, :], self.hy_w1[l], [], [k_w1])
        self.dma(w2[0:64, :], self.hy_w2[l], [], [k_w2])
        self.dma(w3[0:64, :], self.hy_w3[l], [], [k_w3])
        self.dma(b1[0:64, :], self.hy_b1[l], [], [k_b1])
        self.dma(b2[0:64, :], self.hy_b2[l], [], [k_b2])
        self.dma(fr[0:64, :], self.hy_fr[l], [], [k_fr])
        self.ts("dve", fr[0:64, :], fr[0:64, :], 1.0 / TWO_PI, None, ALU.mult, None, [k_fr], [k_fr])
        ze, k_ze = self.alloc(512, "hze")
        t1, k_t1 = self.alloc(512, "ht1"); t2, k_t2 = self.alloc(512, "ht2")
        tiv, k_ti = self.alloc(512, "hti")
        tint = tiv.bitcast(I32)
        h1, k_h1 = self.alloc(512, "hh1")
        h2, k_h2 = self.alloc(LL, "hh2")
        hf, k_hf = self.alloc(LL, "hhf"); hb, k_hb = self.alloc(LL, "hhb")
        win, k_win = self.alloc(LL, "hwin")
        Gt, k_gt = self.bf([2 * LL], "hGt")

        def sin_layer(dst, kd, ps, kps, bias, kb, frs, n):
            P.op("dve", lambda e: e.tensor_scalar(t1[0:64, 0:n], ps, bias, frs, ALU.add, ALU.mult), [kps, kb, k_fr], [k_t1])
            self.cp("dve", tint[0:64, 0:n], t1[0:64, 0:n], [k_t1], [k_ti])
            self.cp("dve", t2[0:64, 0:n], tint[0:64, 0:n], [k_ti], [k_t2])
            self.tt("dve", t1[0:64, 0:n], t1[0:64, 0:n], t2[0:64, 0:n], ALU.subtract, [k_t1, k_t2], [k_t1])
            self.act(dst, t1[0:64, 0:n], AF.Sin, [k_t1], [kd], scale=TWO_PI)

        for L in (LL, LC):
            for c0 in range(0, L, 512):
                n = min(512, L - c0)
                self.dma(ze[0:33, 0:n], self.hy_zemb[L][:, c0:c0 + n], [], [k_ze])
                pq, pqk = self.q[0], self.qk[0]
                self.mm(pq[0:64, 0:n], [(w1[0:33, :], ze[0:33, 0:n])], [k_w1, k_ze], [pqk])
                sin_layer(h1[0:64, 0:n], k_h1, pq[0:64, 0:n], pqk, b1[0:64, 0:1], k_b1, fr[0:64, 0:1], n)
                pq2, pqk2 = self.q[1], self.qk[1]
                self.mm(pq2[0:64, 0:n], [(w2[0:64, :], h1[0:64, 0:n])], [k_w2, k_h1], [pqk2])
                sin_layer(h2[0:64, c0:c0 + n], k_h2, pq2[0:64, 0:n], pqk2, b2[0:64, 0:1], k_b2, fr[0:64, 1:2], n)
            for o in range(2):
                for chalf in range(2):
                    self.dma(win[:, 0:L], self.hy_win[L][chalf * 128:(chalf + 1) * 128, :], [], [k_win])
                    for di, (dst, kd) in enumerate(((hf, k_hf), (hb, k_hb))):
                        cc = o * 512 + di * 256 + chalf * 128
                        for c0 in range(0, L, 512):
                            n = min(512, L - c0)
                            pq, pqk = self.q[2 + (c0 // 512) % 2], self.qk[2 + (c0 // 512) % 2]
                            self.mm(pq[:, 0:n], [(w3[0:64, cc:cc + 128], h2[0:64, c0:c0 + n])], [k_w3, k_h2], [pqk])
                            self.tt("dve", dst[:, c0:c0 + n], pq[:, 0:n], win[:, c0:c0 + n], ALU.mult, [pqk, k_win], [kd])
                    self.cp("pool", Gt[:, 0:L - 1], hb[:, 1:L][:, ::-1], [k_hb], [k_gt])
                    self.tt("pool", Gt[:, L - 1:L], hf[:, 0:1], hb[:, 0:1], ALU.add, [k_hf, k_hb], [k_gt])
                    self.cp("pool", Gt[:, L:2 * L - 1], hf[:, 1:L], [k_hf], [k_gt])
                    P.op("pool", lambda e, L=L: e.memset(Gt[:, 2 * L - 1:2 * L], 0.0), [], [k_gt])
                    self.dma(self.GK[L][o, chalf * 128:(chalf + 1) * 128, :], Gt[:, 0:2 * L], [k_gt], [("GK", L)])

    def mixer_hy(self, l, allk):
        P = self.P
        self.hy_filters(l)
        self.phase()
        NBT = 34
        Wg, k_wg = self.bf([8, 3, 192], "yWg")
        stg, k_stg = self.f32([8, 192], "ystg")
        cwb, k_cwb = self.f32([3, 192], "ycwb")
        cbb, k_cbb = self.alloc(192, "ycbb")
        bsb, k_bsb = self.f32([2, 64], "ybsb")
        PJ, k_pj = self.f32([NBT, 192], "yPJ")
        Zl, k_zl = self.bf([94, 64], "yZl")
        Zc, k_zc = self.bf([4, 64], "yZc")
        zb, k_zb = self.bf([32, 64], "yzb")
        z1, k_z1 = self.f32([32, 64], "yz1")
        tmp, k_tmp = self.f32([32, 64], "ytmp")
        strips = [self.bf([2 * LL - 128], "ystrip%d" % i) for i in range(2)]
        j32, k_j32 = self.alloc(128, "yj32")
        Jb, k_jb = self.bf([128], "yJb")
        ident, k_id = self.alloc(128, "yident")
        OT = [self.alloc(512, "yOT%d" % i) for i in range(2)]
        self.dma(j32, self.jmat, [], [k_j32])
        self.cp("dve", Jb, j32, [k_j32], [k_jb])
        self.dma(ident, self.ident_in, [], [k_id])
        P.op("pool", lambda e: e.memset(Zl, 0.0), [], [k_zl])
        P.op("pool", lambda e: e.memset(Zc, 0.0), [], [k_zc])
        wv = self.w_mx[l][:, self.CH_HY * 128:(self.CH_HY + 6) * 128].rearrange("(k p) n -> p k n", p=128)
        si = 0
        for cg in range(4):
            for part in range(3):
                cs = part * 256 + cg * 64
                self.dma(stg[:, :, part * 64:(part + 1) * 64], wv[:, :, cs:cs + 64], [], [k_stg])
                self.dma(cwb[:, :, part * 64:(part + 1) * 64], self.hy_cw[l][:, cs:cs + 64].partition_broadcast(128), [], [k_cwb])
                self.dma(cbb[:, part * 64:(part + 1) * 64], self.hy_cb[l][cs:cs + 64].partition_broadcast(128), [], [k_cbb])
            self.dma(bsb, self.hy_bias[l][:, cg * 64:(cg + 1) * 64].partition_broadcast(128), [], [k_bsb])
            for tap in range(3):
                self.tt("dve", Wg[:, :, tap, :], stg, cwb[:, tap, :].unsqueeze(1).to_broadcast([128, 8, 192]), ALU.mult,
                        [k_stg, k_cwb], [k_wg])
            for a in range(NBT):
                pc = pcol(a * 128)
                pq, pqk = self.q[a % 2], self.qk[a % 2]
                self.mm(pq[:, 0:192], [(self.xn[:, kk, pc + tap - 1:pc + tap - 1 + 128], Wg[:, kk, tap, :])
                                       for tap in range(3) for kk in range(8)], ["xn", k_wg], [pqk])
                self.tt("dve", PJ[:, a, :], pq[:, 0:192], cbb, ALU.add, [pqk, k_cbb], [k_pj])
            for (blk0, nb, L, Zp, k_zp) in ((0, 2, LC, Zc, k_zc), (2, 32, LL, Zl, k_zl)):
                X1 = PJ[:, blk0:blk0 + nb, 0:64]
                X2 = PJ[:, blk0:blk0 + nb, 64:128]
                Z0 = PJ[:, blk0:blk0 + nb, 128:192]
                for o in range(2):
                    zprev = Z0 if o == 0 else z1[:, 0:nb, :]
                    kprev = k_pj if o == 0 else k_z1
                    gate = X1 if o == 0 else X2
                    self.cp("pool", zb[:, 0:nb, :], zprev, [kprev], [k_zb])
                    for a0 in range(0, nb, 8):
                        an = min(8, nb - a0)
                        pq, pqk = self.q[2], self.qk[2]
                        self.mm(pq[:, 0:an * 64], [(Jb, zb[:, a0:a0 + an, :])], [k_jb, k_zb], [pqk])
                        self.act(Zp[:, nb - 1 + a0:nb - 1 + a0 + an, :],
                                 pq[:, 0:an * 64].rearrange("p (a c) -> p a c", c=64), AF.Copy, [pqk], [k_zp])
                    for c16 in range(4):
                        pq, pqk = self.q[c16 % 2], self.qk[c16 % 2]
                        for ci in range(16):
                            c = c16 * 16 + ci
                            ch = cg * 64 + c
                            sv, k_sv = strips[si % 2]
                            si += 1
                            W_ = 2 * L - 128
                            gk = self.GK[L]
                            src = bass.AP(tensor=gk.tensor, offset=gk[o, ch, 0:1].offset, ap=[[1, 128], [1, W_]])
                            self.dma(sv[:, 0:W_], src, [("GK", L)], [k_sv])
                            nl = 2 * nb - 1
                            for di in range(nl):
                                d = di - (nb - 1)
                                P.op("pe", lambda e, pq=pq, ci=ci, nb=nb, sv=sv, di=di, d=d, Zp=Zp, c=c, nl=nl: e.matmul(
                                    pq[:, ci * nb:(ci + 1) * nb], sv[:, 128 * di:128 * di + 128],
                                    Zp[:, nb - 1 - d:nb - 1 - d + nb, c], start=(di == 0), stop=(di == nl - 1)),
                                    [k_sv, k_zp], [pqk])
                        cs_ = slice(c16 * 16, (c16 + 1) * 16)
                        tv = tmp[:, 0:nb, cs_]
                        self.tt("dve", tv, zprev[:, :, cs_], bsb[:, o, cs_].unsqueeze(1).to_broadcast([128, nb, 16]), ALU.mult,
                                [kprev, k_bsb], [k_tmp])
                        self.tt("dve", tv, tv, pq[:, 0:16 * nb].rearrange("p (c a) -> p a c", a=nb), ALU.add,
                                [k_tmp, pqk], [k_tmp])
                    if o == 0:
                        self.tt("dve", z1[:, 0:nb, :], tmp[:, 0:nb, :], gate, ALU.mult, [k_tmp, k_pj], [k_z1])
                    else:
                        self.tt("dve", tmp[:, 0:nb, :], tmp[:, 0:nb, :], gate, ALU.mult, [k_tmp, k_pj], [k_tmp])
                for a0 in range(0, nb, 4):
                    an = min(4, nb - a0)
                    pq, pqk = self.q[3], self.qk[3]
                    for ai in range(an):
                        P.op("pe", lambda e, pq=pq, ai=ai, a0=a0: e.transpose(
                            pq[0:64, ai * 128:(ai + 1) * 128], tmp[:, a0 + ai, :], ident), [k_tmp, k_id], [pqk])
                    ov, k_ov = OT[(a0 // 4) % 2]
                    self.act(ov[0:64, 0:an * 128], pq[0:64, 0:an * 128], AF.Copy, [pqk], [k_ov])
                    col = (blk0 + a0) * 128
                    self.dma(self.MIX[768 + cg * 64:768 + (cg + 1) * 64, col:col + an * 128], ov[0:64, 0:an * 128],
                             [k_ov], allk)


def _pk(v, k):
    return np.ascontiguousarray(v.reshape(k, 128).T)


def prep_shared(inp):
    f = lambda a: np.ascontiguousarray(a, dtype=np.float32)
    sh = {}
    sh["ada_w"] = f(inp["ada_w"])
    sh["ada_b"] = f(inp["ada_b"].reshape(DEPTH, 48, 128).transpose(0, 2, 1))
    sh["nmw"] = f(inp["norm_mix_w"].reshape(DEPTH, 8, 128).transpose(0, 2, 1))
    sh["nfw"] = f(inp["norm_ffn_w"].reshape(DEPTH, 8, 128).transpose(0, 2, 1))
    sh["fnw"] = f(_pk(inp["final_norm_w"], 8))
    sh["w_in"] = f(inp["w_in"])
    sh["w_out"] = f(inp["w_out"])
    sh["w_up"] = f(inp["ffn_w_up"])
    sh["w_dn"] = f(inp["ffn_w_down"])
    sh["fcw"] = f(inp["ffn_conv_w"].reshape(DEPTH, 3, 22, 128).transpose(0, 3, 1, 2))
    sh["fcb"] = f(inp["ffn_conv_b"].reshape(DEPTH, 22, 128).transpose(0, 2, 1))
    sh["mnw"] = f(inp["merge_norm_w"].reshape(DEPTH, 8, 128).transpose(0, 2, 1))
    return sh


def prep_core(inp, b):
    d = {}
    d["xin"] = np.ascontiguousarray(np.concatenate([inp["ctx"][b].T, inp["x"][b].T], axis=1), dtype=np.float32)
    cv = np.zeros((128, 8, 2), np.float32)
    cv[:, :, 0] = _pk(inp["c"][b], 8)
    cv[:, :, 1] = _pk(inp["c_ctx"], 8)
    d["cvec"] = cv
    return d


def kernel(**inputs):
    inp = {k: np.asarray(v) for k, v in inputs.items()}
    bld = Builder()
    nc = bld.build()
    sh = prep_shared(inp)
    sh.update(prep_mixer_shared(inp))
    in_maps = []
    for b in range(NCORES):
        d = dict(sh)
        d.update(prep_core(inp, b))
        in_maps.append({k: d[k] for k in bld.din})
    res = run_bass_kernel_spmd(nc, in_maps, core_ids=list(range(NCORES)))
    outs = [np.asarray(r["out"]).T for r in res.results]
    return np.ascontiguousarray(np.stack(outs, axis=0).astype(np.float32))


def prep_mixer_shared(inp):
    f = lambda a: np.ascontiguousarray(a, dtype=np.float32)
    sh = {}
    q0, ff0, fb0, v0, g0, u0, hy0, tq0, tk0, tv0 = 0, 256, 512, 768, 1024, 1280, 1536, 2304, 2560, 2688
    cols = []
    for h in range(4):
        for base in (q0, ff0, fb0, v0):
            c = [base + 64 * h + k for k in range(64)]
            cols += c + c
    cols += list(range(g0, g0 + 256)) + list(range(u0, u0 + 256))
    for kv in range(2):
        for par in range(2):
            cols += [tq0 + (2 * kv + hl) * 64 + 2 * i + par for hl in range(2) for i in range(32)]
    for kv in range(2):
        for par in range(2):
            cols += [tk0 + kv * 64 + 2 * i + par for hl in range(2) for i in range(32)]
    cols += list(range(hy0, hy0 + 768)) + list(range(tv0, tv0 + 128))
    assert len(cols) == 31 * 128
    sh["w_mx"] = f(inp["w_in"][:, :, np.array(cols)])
    L = LL
    row = np.repeat(np.arange(L // 64), 64).astype(np.float32)
    col = np.tile(np.arange(64), L // 64).astype(np.float32)
    inv = (1.0 / (10000.0 ** (np.arange(0, 32, 2, dtype=np.float32) / 32.0))).astype(np.float32)
    ang = np.concatenate([row[:, None] * inv, col[:, None] * inv], axis=-1).astype(np.float32)
    cs, sn = np.cos(ang).T, np.sin(ang).T
    sh["ropeC"] = f(np.tile(cs, (4, 1)))
    sh["ropeS"] = f(np.concatenate([np.tile(sn, (2, 1)), -np.tile(sn, (2, 1))], axis=0))
    sk = np.zeros((DEPTH, 128, 2), np.float32)
    for kv in range(2):
        for r in range(128):
            sk[:, r, kv] = inp["att_sink"][:, 2 * kv + r // 64]
    sh["sinkp"] = sk
    s_ = np.arange(128)[:, None]
    t_ = np.arange(128)[None, :]
    sh["cmask"] = f(np.stack([(s_ >= t_), (s_ <= t_)], axis=1))
    r = np.arange(128)
    sh["hmask"] = f(np.stack([((r % 64) // 32 == 0), ((r % 64) // 32 == 1)], axis=1))
    op = np.zeros((128, 2, 128), np.float32)
    op[:, 0, 0:64] = 1.0
    op[:, 1, 64:128] = 1.0
    sh["onesp"] = op
    def gp(a):
        return f(a.reshape(DEPTH, 2, 8, 2, 64).transpose(0, 3, 4, 1, 2).reshape(DEPTH, 128, 2, 8))
    sh["s5_are"] = gp(inp["s5_a_re"])
    sh["s5_aim"] = gp(inp["s5_a_im"])
    sh["s5_ls"] = gp(np.broadcast_to(inp["s5_log_step"][:, :, :, None], (DEPTH, 2, 16, 64)))
    def bp(a):
        return f(a.reshape(DEPTH, 8, 2, 64, 16).transpose(0, 2, 3, 1, 4).reshape(DEPTH, 128, 8, 16))
    sh["s5_bre"] = bp(inp["s5_b_re"])
    sh["s5_bim"] = bp(inp["s5_b_im"])
    def cpad(a):
        o = np.zeros((DEPTH, 2, 64, 2, 8, 128), np.float32)
        for j in range(8):
            for gl in range(2):
                m0 = 32 * (j % 4) + 16 * gl
                o[:, gl, :, :, j, m0:m0 + 16] = a[:, :, 2 * j + gl, :, :].transpose(0, 3, 1, 2)
        return f(o.reshape(DEPTH, 128, 2, 8, 128))
    sh["s5_cre"] = cpad(inp["s5_c_re"])
    sh["s5_cim"] = cpad(inp["s5_c_im"])
    sh["s5_dv"] = f(inp["s5_d"].reshape(DEPTH, 2, 128).transpose(0, 2, 1))
    sh["s5_glu"] = f(inp["s5_glu_w"])
    sh["ident"] = f(np.eye(128))
    sh["iota"] = f(np.broadcast_to(np.arange(1024, dtype=np.float32)[None, :], (128, 1024)))
    lg = inp["hg_lb_logits"].reshape(DEPTH, 2, 4, 64)
    lg = lg.transpose(3, 1, 2, 0)
    sh["hg_lbl"] = f(np.concatenate([lg, lg], axis=0))
    E = np.zeros((128, 4096), np.float32)
    for k in range(64):
        E[k, 64 * k:64 * k + 64] = 1.0
    sh["hg_E"] = E
    S2 = np.zeros((128, 32, 64), np.float32)
    for j in range(32):
        S2[0:64, j, 2 * j] = 1.0
        S2[64:128, j, 2 * j + 1] = 1.0
    sh["hg_S2"] = S2.reshape(128, 2048)
    sh.update(prep_hyena_shared(inp))
    return sh


def prep_hyena_shared(inp):
    f = lambda a: np.ascontiguousarray(a, dtype=np.float32)
    sh = {}
    for L, tag in ((LL, "l"), (LC, "c")):
        t01 = np.linspace(0.0, 1.0, L, dtype=np.float32)[:, None]
        w = (2.0 * np.float32(math.pi) * np.arange(L, dtype=np.float32)[:, None] / np.float32(L)).astype(np.float32)
        fr = np.linspace(1e-4, 15.0, 16, dtype=np.float32)[None, :]
        z = np.concatenate([t01, np.cos(fr * w), -np.sin(fr * w)], axis=-1).astype(np.float32)
        sh["hy_zemb_" + tag] = f(z.T)
        hmin = math.log(1e-2) / 1.5
        hmax = math.log(1e-2) / 0.3
        deltas = np.linspace(hmin, hmax, 256, dtype=np.float32)
        win = np.exp(-t01 * np.abs(deltas)[None, :]).astype(np.float32)
        sh["hy_win_" + tag] = f(win.T)
    sh["hy_w1"] = f(inp["hy_w1"])
    sh["hy_b1"] = f(inp["hy_b1"][:, :, None])
    sh["hy_fr"] = f(inp["hy_freq"].transpose(0, 2, 1))
    sh["hy_w2"] = f(inp["hy_w2"])
    sh["hy_b2"] = f(inp["hy_b2"][:, :, None])
    sh["hy_w3"] = f(inp["hy_w3"])
    sh["hy_cw"] = f(inp["hy_conv_w"])
    sh["hy_cb"] = f(inp["hy_conv_b"])
    sh["hy_bias"] = f(inp["hy_bias"])
    sh["jmat"] = f(np.eye(128)[::-1])
    return sh
```

```python
import contextlib
import math
import numpy as np
import concourse.bass as bass
import concourse.mybir as mybir
from concourse.bass_utils import run_bass_kernel_spmd

F32 = mybir.dt.float32
BF16 = mybir.dt.bfloat16
I32 = mybir.dt.int32
ALU = mybir.AluOpType
AF = mybir.ActivationFunctionType

D = 1024
DEPTH = 4
LC = 256
LL = 4096
T = LC + LL
DFF = 2816
EPS = 1e-6
NCORES = 8
TILES = [(0, 256, 1)] + [(256 + 512 * i, 512, 0) for i in range(8)]
TP = T + 3


def pcol(c):
    return c + 1 if c < LC else c + 2


ENGS = ("pe", "act", "dve", "pool", "sp")
NDMA = 12


class Prog:
    def __init__(self, nc):
        self.nc = nc
        self.ops = {e: [] for e in ENGS}
        self.cnt = {e: 0 for e in ENGS}
        self.seen = {e: {} for e in ENGS}
        self.lastw = {}
        self.readers = {}
        self.dma_i = {e: 0 for e in ENGS}
        self.dma_val = {}
        self.sems = {}

    def _wait(self, eng, tok):
        sk, v = tok
        if sk == "pe" and eng == "pe":
            return
        if self.seen[eng].get(sk, 0) >= v:
            return
        self.seen[eng][sk] = v
        self.ops[eng].append(("wait", sk, v))

    def _deps(self, eng, reads, writes):
        for k in reads:
            if k in self.lastw:
                self._wait(eng, self.lastw[k])
        for k in writes:
            if k in self.lastw:
                self._wait(eng, self.lastw[k])
            for t in self.readers.get(k, ()):
                self._wait(eng, t)

    def _commit(self, tok, reads, writes):
        for k in reads:
            lst = self.readers.setdefault(k, [])
            lst.append(tok)
            if len(lst) > 32:
                d = {}
                for sk, v in lst:
                    d[sk] = max(d.get(sk, 0), v)
                self.readers[k] = list(d.items())
        for k in writes:
            self.lastw[k] = tok
            self.readers[k] = []

    def op(self, eng, fn, reads=(), writes=()):
        self._deps(eng, reads, writes)
        self.cnt[eng] += 1
        tok = (eng, self.cnt[eng])
        self.ops[eng].append(("op", fn))
        self._commit(tok, reads, writes)
        return tok

    def dma(self, eng, out, in_, reads=(), writes=(), **kw):
        self._deps(eng, reads, writes)
        i = self.dma_i[eng]
        self.dma_i[eng] += 1
        slot = (eng, i % NDMA)
        prev = self.dma_val.get(slot, 0)
        if prev:
            self._wait(eng, (slot, prev))
        val = prev + 16
        self.dma_val[slot] = val
        self.ops[eng].append(("dma", out, in_, slot, kw))
        tok = (slot, val)
        self._commit(tok, reads, writes)
        return tok

    def finish(self, toks):
        for t in toks:
            self._wait("sp", t)

    def emit(self):
        nc = self.nc
        with contextlib.ExitStack() as st:
            semkeys = list(ENGS)
            for e in ENGS:
                for s in range(NDMA):
                    if (e, s) in self.dma_val:
                        semkeys.append((e, s))
            for sk in semkeys:
                nm = sk if isinstance(sk, str) else "d_%s_%d" % sk
                self.sems[sk] = st.enter_context(nc.semaphore("s_" + nm))
            block = st.enter_context(nc.Block())

            def run(eng_obj, lst, ename):
                sem_own = self.sems[ename]
                for o in lst:
                    if o[0] == "wait":
                        eng_obj.wait_ge(self.sems[o[1]], o[2])
                    elif o[0] == "op":
                        o[1](eng_obj).then_inc(sem_own, 1)
                    else:
                        _, out, in_, slot, kw = o
                        eng_obj.dma_start(out=out, in_=in_, **kw).then_inc(self.sems[slot], 16)

            @block.tensor
            def _(e):
                run(e, self.ops["pe"], "pe")

            @block.scalar
            def _(e):
                run(e, self.ops["act"], "act")

            @block.vector
            def _(e):
                run(e, self.ops["dve"], "dve")

            @block.gpsimd
            def _(e):
                run(e, self.ops["pool"], "pool")

            @block.sync
            def _(e):
                run(e, self.ops["sp"], "sp")


def _barrier(P):
    toks = [(e, P.cnt[e]) for e in ENGS if P.cnt[e] > 0]
    toks += [(slot, v) for slot, v in P.dma_val.items()]
    for e in ENGS:
        for t in toks:
            P._wait(e, t)


ARENA_WORDS = 51200
XN_WORDS = 8 * TP // 2 + 4


class Builder:
    def __init__(self, depth=DEPTH, mixers=("hg", "s5", "hy", "at"), debug=()):
        self.depth = depth
        self.mixers = mixers
        self.debug = debug
        self.nc = bass.Bass("TRN2", target_bir_lowering=False)
        self.P = Prog(self.nc)
        self.st = contextlib.ExitStack()
        self.din = {}
        self.out_toks = []

    def inp(self, name, shape, dt=F32):
        self.din[name] = self.nc.dram_tensor(name, list(shape), dt, kind="ExternalInput").ap()
        return self.din[name]

    def dram(self, name, shape, dt=F32, kind="Internal"):
        return self.nc.dram_tensor(name, list(shape), dt, kind=kind).ap()

    def phase(self):
        _barrier(self.P)
        self.ptr = self.base
        self.nphase += 1

    def alloc(self, words, name):
        a = self.ptr
        self.ptr += (words + 1) // 2 * 2
        assert self.ptr <= ARENA_WORDS, (name, self.ptr)
        key = "%s@%d" % (name, self.nphase)
        return self.arena[:, a:a + words], key

    def f32(self, shape, name):
        n = int(np.prod(shape))
        v, k = self.alloc(n, name)
        if len(shape) == 2:
            v = v.rearrange("p (a b) -> p a b", a=shape[0])
        elif len(shape) == 3:
            v = v.rearrange("p (a b c) -> p a b c", a=shape[0], b=shape[1])
        return v, k

    def bf(self, shape, name):
        n = int(np.prod(shape))
        v, k = self.alloc((n + 1) // 2, name)
        v = v.bitcast(BF16)[:, 0:n]
        if len(shape) == 2:
            v = v.rearrange("p (a b) -> p a b", a=shape[0])
        elif len(shape) == 3:
            v = v.rearrange("p (a b c) -> p a b c", a=shape[0], b=shape[1])
        return v, k

    def act(self, out, in_, func, reads, writes, **kw):
        return self.P.op("act", lambda e: e.activation(out, in_, func, **kw), reads, writes)

    def tt(self, eng, out, a, b, op, reads, writes):
        return self.P.op(eng, lambda e: e.tensor_tensor(out, a, b, op), reads, writes)

    def ts(self, eng, out, a, s1, s2, op0, op1, reads, writes):
        if s2 is None:
            return self.P.op(eng, lambda e: e.tensor_scalar(out, a, s1, None, op0), reads, writes)
        return self.P.op(eng, lambda e: e.tensor_scalar(out, a, s1, s2, op0, op1), reads, writes)

    def stt(self, eng, out, a, s, b, op0, op1, reads, writes):
        return self.P.op(eng, lambda e: e.scalar_tensor_tensor(out, a, s, b, op0, op1), reads, writes)

    def cp(self, eng, out, in_, reads, writes):
        return self.P.op(eng, lambda e: e.tensor_copy(out, in_), reads, writes)

    def mm(self, out, pairs, reads, writes):
        n = len(pairs)
        for i, (l, r) in enumerate(pairs):
            self.P.op("pe", lambda e, l=l, r=r, i=i: e.matmul(out, l, r, start=(i == 0), stop=(i == n - 1)),
                      reads, writes)

    def dma(self, out, in_, reads, writes, eng="sp"):
        return self.P.dma(eng, out, in_, reads, writes)

    def build(self):
        nc, P = self.nc, self.P
        inp = self.inp
        xin = inp("xin", [D, T])
        cvec = inp("cvec", [128, 8, 2])
        ada_w = inp("ada_w", [DEPTH, D, 6 * D])
        ada_b = inp("ada_b", [DEPTH, 128, 48])
        nmw = inp("nmw", [DEPTH, 128, 8])
        nfw = inp("nfw", [DEPTH, 128, 8])
        fnw = inp("fnw", [128, 8])
        self.w_in = inp("w_in", [DEPTH, D, 2816])
        self.w_out = inp("w_out", [DEPTH, D, D])
        self.w_up = inp("w_up", [DEPTH, D, 2 * DFF])
        self.w_dn = inp("w_dn", [DEPTH, DFF, D])
        fcw = inp("fcw", [DEPTH, 128, 3, 22])
        fcb = inp("fcb", [DEPTH, 128, 22])
        mnw = inp("mnw", [DEPTH, 128, 8])
        self.declare_mixer_inputs()
        out = self.dram("out", [D, LL], kind="ExternalOutput")
        X = self.dram("Xs", [D, T])
        self.X = X
        self.MIX = self.dram("MIXs", [1536, T], kind=("ExternalOutput" if getattr(self, "mix_debug", False) else "Internal"))
        self.H = self.dram("Hs", [DFF, T], BF16)
        self.dbg = {}
        for name, shape in self.debug:
            self.dbg[name] = self.dram("dbg_" + name, shape, kind="ExternalOutput")

        self.arena = self.st.enter_context(nc.sbuf_tensor("arena", [128, ARENA_WORDS], F32))
        self.q = [self.st.enter_context(nc.psum_tensor("q%d" % i, [128, 1024], F32)) for i in range(4)]
        self.qk = ["q%d" % i for i in range(4)]
        self.ptr = 0
        self.nphase = 0
        xnv, _ = self.alloc(XN_WORDS, "xn")
        self.xn = xnv.bitcast(BF16)[:, 0:8 * TP].rearrange("p (k n) -> p k n", k=8)
        ob, _ = self.alloc(64, "ones")
        self.ones_bf = ob.bitcast(BF16)
        self.mods, _ = self.f32([48, 2], "mods")
        self.gain1, _ = self.f32([8, 2], "gain1")
        self.gain2, _ = self.f32([8, 2], "gain2")
        cc, _ = self.f32([8, 2], "cc")
        scc, _ = self.f32([8, 2], "scc")
        adab, _ = self.alloc(48, "adab")
        nmw_t, _ = self.alloc(8, "nmw_t")
        nfw_t, _ = self.alloc(8, "nfw_t")
        fnw_t, _ = self.alloc(8, "fnw_t")
        self.mnw_t, _ = self.alloc(8, "mnw_t")
        self.fcw_t, _ = self.f32([3, 22], "fcw_t")
        self.fcb_t, _ = self.alloc(22, "fcb_t")
        self.alloc_mixer_persistent()
        self.base = self.ptr
        mods, gain1, gain2 = self.mods, self.gain1, self.gain2
        q, qk = self.q, self.qk

        P.op("pool", lambda e: e.memset(self.ones_bf, 1.0), [], ["ones_bf"])
        P.op("pool", lambda e: e.memset(self.xn, 0.0), [], ["xn"])
        self.dma(cc, cvec, [], ["cc"])
        self.dma(fnw_t, fnw, [], ["fnw_t"])
        self.act(scc, cc, AF.Silu, ["cc"], ["scc"])

        src = xin
        for l in range(self.depth):
            self.phase()
            self.dma(adab, ada_b[l], [], ["adab"])
            self.dma(nmw_t, nmw[l], [], ["nmw_t"])
            self.dma(nfw_t, nfw[l], [], ["nfw_t"])
            self.dma(self.mnw_t, mnw[l], [], ["mnw_t"])
            self.dma(self.fcw_t, fcw[l], [], ["fcw_t"])
            self.dma(self.fcb_t, fcb[l], [], ["fcb_t"])
            stg = [self.f32([8, 512], "astage%d" % i) for i in range(2)]
            awv = ada_w[l].rearrange("(k p) n -> p k n", p=128)
            for og in range(12):
                sv, sk = stg[og % 2]
                self.dma(sv, awv[:, :, og * 512:(og + 1) * 512], [], [sk])
                pa, pk = q[og % 2], qk[og % 2]
                for j in range(4):
                    self.mm(pa[:, 2 * j:2 * j + 2],
                            [(sv[:, k, 128 * j:128 * j + 128], scc[:, k, :]) for k in range(8)], [sk, "scc"], [pk])
                self.tt("dve", mods[:, og * 4:(og + 1) * 4, :], pa[:, 0:8].rearrange("p (j v) -> p j v", v=2),
                        adab[:, og * 4:(og + 1) * 4].unsqueeze(2).to_broadcast([128, 4, 2]), ALU.add,
                        [pk, "adab"], ["mods"])
            self.stt("dve", gain1, mods[:, 8:16, :], 1.0, nmw_t.unsqueeze(2).to_broadcast([128, 8, 2]),
                     ALU.add, ALU.mult, ["mods", "nmw_t"], ["gain1"])
            self.stt("dve", gain2, mods[:, 32:40, :], 1.0, nfw_t.unsqueeze(2).to_broadcast([128, 8, 2]),
                     ALU.add, ALU.mult, ["mods", "nfw_t"], ["gain2"])
            self.norm_phase(src, gain1, "gain1", 0)
            self.mixers_phase(l)
            self.phase_o(l, src)
            src = X
            self.norm_phase(X, gain2, "gain2", 24)
            self.ffn_up_phase(l)
            self.ffn_down_phase(l)
        self.final_norm(fnw_t, out)
        P.finish(self.out_toks)
        P.emit()
        self.st.close()
        return nc

    def rms_rstd(self, xtile, xkey, n, nchunks, rstd, rt, sqb, pq, pqk, eps=EPS):
        self.act(sqb[:, 0:nchunks, 0:n], xtile[:, 0:nchunks, 0:n], AF.Square, [xkey], ["sqb"])
        self.mm(pq[:, 0:n], [(self.ones_bf, sqb[:, k, 0:n]) for k in range(nchunks)], ["sqb", "ones_bf"], [pqk])
        self.act(rt[:, 0:n], pq[:, 0:n], AF.Sqrt, [pqk], ["rt"], bias=eps, scale=1.0 / (128 * nchunks))
        self.P.op("dve", lambda e: e.reciprocal(rstd[:, 0:n], rt[:, 0:n]), ["rt"], ["rstd"])

    def norm_phase(self, src, gain, gkey, shift_base):
        self.phase()
        xts = [self.f32([8, 512], "xt%d" % i) for i in range(2)]
        sqb, _ = self.bf([8, 512], "sqb")
        rstd, _ = self.alloc(512, "rstd")
        rt, _ = self.alloc(512, "rt")
        for pc_ in (0, LC + 1, TP - 1):
            self.P.op("pool", lambda e, pc_=pc_: e.memset(self.xn[:, :, pc_:pc_ + 1], 0.0), [], ["xn"])
        for ti, (c0, n, isc) in enumerate(TILES):
            xtile, xk = xts[ti % 2]
            self.dma(xtile[:, :, 0:n], src[:, c0:c0 + n].rearrange("(k p) n -> p k n", p=128), [("X", ti)], [xk])
            pq, pqk = self.q[ti % 2], self.qk[ti % 2]
            self.rms_rstd(xtile, xk, n, 8, rstd, rt, sqb, pq, pqk)
            self.tt("dve", xtile[:, :, 0:n], xtile[:, :, 0:n], rstd[:, 0:n].unsqueeze(1).to_broadcast([128, 8, n]),
                    ALU.mult, [xk, "rstd"], [xk])
            pc = pcol(c0)
            for k in range(8):
                self.act(self.xn[:, k, pc:pc + n], xtile[:, k, 0:n], AF.Identity, [xk, gkey, "mods"], ["xn"],
                         scale=gain[:, k, isc:isc + 1], bias=self.mods[:, shift_base + k, isc:isc + 1])

    def load_w_chunk(self, wsrc_cols, stg, wbf, use_act):
        sv, sk = stg
        wv, wk = wbf
        self.dma(sv, wsrc_cols.rearrange("(k p) n -> p k n", p=128), [], [sk])
        if use_act:
            self.act(wv, sv, AF.Copy, [sk], [wk])
        else:
            self.cp("pool", wv, sv, [sk], [wk])

    def proj_rows(self, wv, wk, dst_fn, m=128):
        for ti, (c0, n, isc) in enumerate(TILES):
            pq, pqk = self.q[2 + ti % 2], self.qk[2 + ti % 2]
            pc = pcol(c0)
            self.mm(pq[0:m, 0:n], [(wv[:, k, 0:m], self.xn[:, k, pc:pc + n]) for k in range(8)], [wk, "xn"], [pqk])
            dst_fn(ti, c0, n, isc, pq[0:m, 0:n], pqk)

    def phase_o(self, l, src):
        self.phase()
        stg = [self.f32([8, 512], "ostage%d" % i) for i in range(2)]
        wo, wok = self.bf([8, 1024], "wo")
        for i in range(2):
            sv, sk = stg[i]
            self.dma(sv, self.w_out[l][:, i * 512:(i + 1) * 512].rearrange("(k p) n -> p k n", p=128), [], [sk])
            self.act(wo[:, :, i * 512:(i + 1) * 512], sv, AF.Copy, [sk], [wok])
        mixt, mk = self.f32([12, 512], "mixt")
        mixb, mbk = self.bf([8, 512], "mixb")
        sqb, _ = self.bf([2, 512], "sqb")
        rstd, _ = self.alloc(512, "rstd")
        rt, _ = self.alloc(512, "rt")
        sg, sgk = self.f32([2, 512], "silug")
        xts = [self.f32([8, 512], "oxt%d" % i) for i in range(2)]
        MIXv = self.MIX.rearrange("(k p) n -> p k n", p=128)
        for ti, (c0, n, isc) in enumerate(TILES):
            xtile, xk = xts[ti % 2]
            self.dma(xtile[:, :, 0:n], src[:, c0:c0 + n].rearrange("(k p) n -> p k n", p=128), [("X", ti)], [xk])
            self.dma(mixt[:, :, 0:n], MIXv[:, :, c0:c0 + n], [("MIX", ti)], [mk])
            self.tt("dve", mixt[:, 0:2, 0:n], mixt[:, 0:2, 0:n], mixt[:, 2:4, 0:n], ALU.add, [mk], [mk])
            self.act(sg[:, :, 0:n], mixt[:, 10:12, 0:n], AF.Silu, [mk], [sgk])
            for g, c in enumerate((0, 4, 6, 8)):
                pq, pqk = self.q[g % 2], self.qk[g % 2]
                self.rms_rstd(mixt[:, c:c + 2, :], mk, n, 2, rstd, rt, sqb, pq, pqk)
                for j in range(2):
                    self.stt("dve", mixt[:, c + j, 0:n], mixt[:, c + j, 0:n], self.mnw_t[:, 2 * g + j:2 * g + j + 1],
                             rstd[:, 0:n], ALU.mult, ALU.mult, [mk, "rstd", "mnw_t"], [mk])
                    if g == 0:
                        self.tt("dve", mixb[:, j, 0:n], mixt[:, j, 0:n], sg[:, j, 0:n], ALU.mult, [mk, sgk], [mbk])
                    else:
                        self.cp("pool", mixb[:, 2 * g + j, 0:n], mixt[:, c + j, 0:n], [mk], [mbk])
            for oc in range(8):
                pq, pqk = self.q[2 + oc % 2], self.qk[2 + oc % 2]
                self.mm(pq[:, 0:n], [(wo[:, k, oc * 128:(oc + 1) * 128], mixb[:, k, 0:n]) for k in range(8)],
                        [wok, mbk], [pqk])
                self.stt("dve", xtile[:, oc, 0:n], pq[:, 0:n], self.mods[:, 16 + oc, isc:isc + 1], xtile[:, oc, 0:n],
                         ALU.mult, ALU.add, [pqk, xk, "mods"], [xk])
            self.dma(self.X[:, c0:c0 + n].rearrange("(k p) n -> p k n", p=128), xtile[:, :, 0:n], [xk], [("X", ti)])

    def ffn_up_phase(self, l):
        self.phase()
        stg = [[self.f32([8, 128], "ustage%d%d" % (i, j)) for j in range(2)] for i in range(2)]
        wbf = [[self.bf([8, 128], "uw%d%d" % (i, j)) for j in range(2)] for i in range(2)]
        afs = [self.alloc(T, "a_full%d" % i) for i in range(2)]
        cfs = [self.alloc(T, "c_full%d" % i) for i in range(2)]
        hb = [self.alloc(T // 2, "hb%d" % i) for i in range(2)]
        Hv = self.H
        segs = [(0, LC), (LC, T)]
        def part_a(oc):
            pb = oc % 2
            a_full, ak = afs[pb]
            c_full, ck = cfs[pb]
            self.load_w_chunk(self.w_up[l][:, oc * 128:(oc + 1) * 128], stg[pb][0], wbf[pb][0], True)
            self.load_w_chunk(self.w_up[l][:, DFF + oc * 128:DFF + (oc + 1) * 128], stg[pb][1], wbf[pb][1], False)
            wa, wak = wbf[pb][0]
            w0 = self.fcw_t[:, 0, oc:oc + 1]
            w1 = self.fcw_t[:, 1, oc:oc + 1]
            w2 = self.fcw_t[:, 2, oc:oc + 1]
            def dst_a(ti, c0, n, isc, ps, pk, a_full=a_full, ak=ak):
                self.act(a_full[:, c0:c0 + n], ps, AF.Copy, [pk], [ak])
            self.proj_rows(wa, wak, dst_a)
            self.act(c_full, a_full, AF.Identity, [ak, "fcw_t", "fcb_t"], [ck], scale=w1, bias=self.fcb_t[:, oc:oc + 1])
            for (s_, e_) in segs:
                self.stt("dve", c_full[:, s_ + 1:e_], a_full[:, s_:e_ - 1], w0, c_full[:, s_ + 1:e_], ALU.mult, ALU.add,
                         [ak, ck, "fcw_t"], [ck])
                self.stt("dve", c_full[:, s_:e_ - 1], a_full[:, s_ + 1:e_], w2, c_full[:, s_:e_ - 1], ALU.mult, ALU.add,
                         [ak, ck, "fcw_t"], [ck])
            self.act(c_full, c_full, AF.Silu, [ck], [ck])

        def part_v(oc):
            pb = oc % 2
            c_full, ck = cfs[pb]
            hv = hb[pb][0].bitcast(BF16)
            hk = hb[pb][1]
            wv_, wvk = wbf[pb][1]
            def dst_v(ti, c0, n, isc, ps, pk, c_full=c_full, ck=ck, hv=hv, hk=hk):
                self.tt("dve", hv[:, c0:c0 + n], ps, c_full[:, c0:c0 + n], ALU.mult, [pk, ck], [hk])
            self.proj_rows(wv_, wvk, dst_v)
            self.dma(Hv[oc * 128:(oc + 1) * 128, :], hv, [hk], [("H", oc)])

        part_a(0)
        for oc in range(22):
            if oc + 1 < 22:
                part_a(oc + 1)
            part_v(oc)

    def ffn_down_phase(self, l):
        self.phase()
        stg = [self.f32([8, 512], "dstage%d" % i) for i in range(2)]
        wd = self.arena[:, 0:11264].bitcast(BF16).rearrange("p (k n) -> p k n", k=22)
        wdk = "xn"
        i = 0
        for k0, kn in ((0, 8), (8, 8), (16, 6)):
            for c in range(2):
                sv, sk = stg[i % 2]
                self.dma(sv[:, 0:kn, :], self.w_dn[l][k0 * 128:(k0 + kn) * 128, c * 512:(c + 1) * 512]
                         .rearrange("(k p) n -> p k n", p=128), [], [sk])
                if i % 2:
                    self.act(wd[:, k0:k0 + kn, c * 512:(c + 1) * 512], sv[:, 0:kn, :], AF.Copy, [sk], [wdk])
                else:
                    self.cp("pool", wd[:, k0:k0 + kn, c * 512:(c + 1) * 512], sv[:, 0:kn, :], [sk], [wdk])
                i += 1
        hts = [self.bf([22, 512], "ht%d" % i) for i in range(2)]
        xts = [self.f32([8, 512], "dxt%d" % i) for i in range(2)]
        Hv = self.H.rearrange("(k p) n -> p k n", p=128)
        for ti, (c0, n, isc) in enumerate(TILES):
            xtile, xk = xts[ti % 2]
            ht, hk = hts[ti % 2]
            self.dma(xtile[:, :, 0:n], self.X[:, c0:c0 + n].rearrange("(k p) n -> p k n", p=128), [("X", ti)], [xk])
            self.dma(ht[:, :, 0:n], Hv[:, :, c0:c0 + n], [("H", oc) for oc in range(22)], [hk])
            for oc in range(8):
                pq, pqk = self.q[oc % 4], self.qk[oc % 4]
                self.mm(pq[:, 0:n], [(wd[:, k, oc * 128:(oc + 1) * 128], ht[:, k, 0:n]) for k in range(22)],
                        [wdk, hk], [pqk])
                self.stt("dve", xtile[:, oc, 0:n], pq[:, 0:n], self.mods[:, 40 + oc, isc:isc + 1], xtile[:, oc, 0:n],
                         ALU.mult, ALU.add, [pqk, xk, "mods"], [xk])
            self.dma(self.X[:, c0:c0 + n].rearrange("(k p) n -> p k n", p=128), xtile[:, :, 0:n], [xk], [("X", ti)])

    def final_norm(self, fnw_t, out):
        self.phase()
        xts = [self.f32([8, 512], "fxt%d" % i) for i in range(2)]
        sqb, _ = self.bf([8, 512], "sqb")
        rstd, _ = self.alloc(512, "rstd")
        rt, _ = self.alloc(512, "rt")
        for ti, (c0, n, isc) in enumerate(TILES):
            if isc:
                continue
            xtile, xk = xts[ti % 2]
            self.dma(xtile[:, :, 0:n], self.X[:, c0:c0 + n].rearrange("(k p) n -> p k n", p=128), [("X", ti)], [xk])
            pq, pqk = self.q[ti % 2], self.qk[ti % 2]
            self.rms_rstd(xtile, xk, n, 8, rstd, rt, sqb, pq, pqk)
            for k in range(8):
                self.stt("dve", xtile[:, k, 0:n], xtile[:, k, 0:n], fnw_t[:, k:k + 1], rstd[:, 0:n], ALU.mult, ALU.mult,
                         [xk, "rstd", "fnw_t"], [xk])
            tok = self.dma(out[:, c0 - LC:c0 - LC + n].rearrange("(k p) n -> p k n", p=128), xtile[:, :, 0:n], [xk], [])
            self.out_toks.append(tok)

    CH_HG = 0
    CH_G = 16
    CH_U = 18
    CH_AQ = 20
    CH_AK = 22
    CH_HY = 24
    CH_TV = 30
    CH_N = 31
    NCH = 39

    def declare_mixer_inputs(self):
        inp = self.inp
        self.w_mx = inp("w_mx", [DEPTH, D, self.NCH * 128])
        self.ropeC = inp("ropeC", [128, LL])
        self.ropeS = inp("ropeS", [128, LL])
        self.sinkp = inp("sinkp", [DEPTH, 128, 2])
        self.cmask = inp("cmask", [128, 2, 128])
        self.hmask = inp("hmask", [128, 2])
        self.onesp = inp("onesp", [128, 2, 128])
        self.s5_are = inp("s5_are", [DEPTH, 128, 2, 8])
        self.s5_aim = inp("s5_aim", [DEPTH, 128, 2, 8])
        self.s5_ls = inp("s5_ls", [DEPTH, 128, 2, 8])
        self.s5_bre = inp("s5_bre", [DEPTH, 128, 8, 16])
        self.s5_bim = inp("s5_bim", [DEPTH, 128, 8, 16])
        self.s5_cre = inp("s5_cre", [DEPTH, 128, 2, 8, 128])
        self.s5_cim = inp("s5_cim", [DEPTH, 128, 2, 8, 128])
        self.s5_dv = inp("s5_dv", [DEPTH, 128, 2])
        self.s5_glu = inp("s5_glu", [DEPTH, 256, 256])
        self.ident_in = inp("ident", [128, 128])
        self.iota_in = inp("iota", [128, 1024])
        self.hg_lbl = inp("hg_lbl", [128, 2, 4, DEPTH])
        self.hg_E = inp("hg_E", [128, 4096])
        self.hg_S2 = inp("hg_S2", [128, 2048])
        self.hg2_lbl = inp("hg2_lbl", [128, 2, 2, DEPTH])
        self.hg2_mask = inp("hg2_mask", [128, 2, 128])
        self.hg2_rowmask = inp("hg2_rowmask", [128, 8])
        self.declare_hyena_inputs()

    def alloc_mixer_persistent(self):
        self.lbt, _ = self.f32([2, 4, DEPTH], "lbt")
        self.oml, _ = self.f32([2, 4, DEPTH], "oml")
        self.lbt2, _ = self.f32([2, 2, DEPTH], "lbt2")
        self.oml2, _ = self.f32([2, 2, DEPTH], "oml2")

    def mixer_setup(self):
        lg, k = self.f32([2, 4, DEPTH], "lbl")
        sm, sk = self.f32([2, 4], "lbsum")
        self.dma(lg, self.hg_lbl, [], [k])
        self.act(lg, lg, AF.Exp, [k], [k])
        self.P.op("dve", lambda e: e.tensor_reduce(sm, lg, mybir.AxisListType.X, ALU.add), [k], [sk])
        self.P.op("dve", lambda e: e.reciprocal(sm, sm), [sk], [sk])
        self.tt("dve", lg, lg, sm.unsqueeze(3).to_broadcast([128, 2, 4, DEPTH]), ALU.mult, [k, sk], [k])
        self.P.op("dve", lambda e: e.memset(self.lbt[:, :, :, 0:1], 0.0), [], ["lbt"])
        for l in range(1, DEPTH):
            self.tt("dve", self.lbt[:, :, :, l:l + 1], self.lbt[:, :, :, l - 1:l], lg[:, :, :, l:l + 1], ALU.add,
                    [k, "lbt"], ["lbt"])
        self.ts("dve", self.oml, self.lbt, -1.0, 1.0, ALU.mult, ALU.add, ["lbt"], ["oml"])
        lg2, k2 = self.f32([2, 2, DEPTH], "lbl2")
        sm2, sk2 = self.f32([2, 2], "lbsum2")
        self.dma(lg2, self.hg2_lbl, [], [k2])
        self.act(lg2, lg2, AF.Exp, [k2], [k2])
        self.P.op("dve", lambda e: e.tensor_reduce(sm2, lg2, mybir.AxisListType.X, ALU.add), [k2], [sk2])
        self.P.op("dve", lambda e: e.reciprocal(sm2, sm2), [sk2], [sk2])
        self.tt("dve", lg2, lg2, sm2.unsqueeze(3).to_broadcast([128, 2, 2, DEPTH]), ALU.mult, [k2, sk2], [k2])
        self.P.op("dve", lambda e: e.memset(self.lbt2[:, :, :, 0:1], 0.0), [], ["lbt2"])
        for l in range(1, DEPTH):
            self.tt("dve", self.lbt2[:, :, :, l:l + 1], self.lbt2[:, :, :, l - 1:l], lg2[:, :, :, l:l + 1], ALU.add,
                    [k2, "lbt2"], ["lbt2"])
        self.ts("dve", self.oml2, self.lbt2, -1.0, 1.0, ALU.mult, ALU.add, ["lbt2"], ["oml2"])

    def wrot_init(self, nbuf=3):
        self.wrot = [(self.f32([8, 128], "wst%d" % i), self.bf([8, 128], "wbf%d" % i)) for i in range(nbuf)]
        self.wrot_i = 0

    def wchunk(self, l, ci):
        stg, wbf = self.wrot[self.wrot_i % len(self.wrot)]
        self.wrot_i += 1
        self.load_w_chunk(self.w_mx[l][:, ci * 128:(ci + 1) * 128], stg, wbf, self.wrot_i % 2 == 0)
        return wbf

    def wchunk_own(self, l, ci, name):
        stg = self.f32([8, 128], name + "_st")
        wbf = self.bf([8, 128], name + "_bf")
        self.load_w_chunk(self.w_mx[l][:, ci * 128:(ci + 1) * 128], stg, wbf, True)
        return wbf

    def zero_mix_rows(self, r0, r1):
        z, zk = self.alloc(T, "zero")
        self.P.op("pool", lambda e: e.memset(z, 0.0), [], [zk])
        for r in range(r0, r1):
            self.dma(self.MIX[r * 128:(r + 1) * 128, :], z, [zk], [("MIX", ti) for ti in range(len(TILES))])

    def mixers_phase(self, l):
        if l == 0:
            self.phase()
            self.mixer_setup()
        allk = [("MIX", ti) for ti in range(len(TILES))]
        self.phase()
        if "hg" in self.mixers:
            self.mixer_hg(l, allk)
        else:
            self.zero_mix_rows(0, 4)
        self.phase()
        self.wrot_init()
        for c in range(2):
            wv, wk = self.wchunk(l, self.CH_G + c)
            gb, gk = self.alloc(T, "gfull%d" % c)
            def dst(ti, c0, n, isc, ps, pk, gb=gb, gk=gk):
                self.act(gb[:, c0:c0 + n], ps, AF.Copy, [pk], [gk])
            self.proj_rows(wv, wk, dst)
            self.dma(self.MIX[1280 + c * 128:1280 + (c + 1) * 128, :], gb, [gk], allk)
        self.phase()
        if "s5" in self.mixers:
            self.mixer_s5(l, allk)
        else:
            self.zero_mix_rows(4, 6)
        self.phase()
        if "hy" in self.mixers:
            self.mixer_hy(l, allk)
        else:
            self.zero_mix_rows(6, 8)
        self.phase()
        if "at" in self.mixers:
            self.mixer_at(l, allk)
        else:
            self.zero_mix_rows(8, 10)

    def mixer_at(self, l, allk):
        P = self.P
        self.wrot_init()
        HW = 2048
        Ch, ck = self.alloc(HW, "ropeC")
        Sh, sk = self.alloc(HW, "ropeS")
        cm32, cmk = self.f32([2, 128], "cm32")
        cmask, cmbk = self.bf([2, 128], "cmask")
        self.dma(cm32, self.cmask, [], [cmk])
        self.cp("dve", cmask, cm32, [cmk], [cmbk])
        hm, hmk = self.alloc(2, "hmask")
        self.dma(hm, self.hmask, [], [hmk])
        op32, opk = self.f32([2, 128], "op32")
        onesp, onk = self.bf([2, 128], "onesp")
        self.dma(op32, self.onesp, [], [opk])
        self.cp("dve", onesp, op32, [opk], [onk])
        esink, esk = self.alloc(2, "esink")
        self.dma(esink, self.sinkp[l], [], [esk])
        self.act(esink, esink, AF.Exp, [esk], [esk])
        raw, rk = self.alloc(T, "raw")
        tb, tbk = self.alloc(HW, "ropeB")
        tsw, tswk = self.alloc(HW, "ropeSw")
        Qb, qbk = self.bf([T], "Qb")
        Km = [self.bf([T], "Km%d" % i) for i in range(2)]
        Vp, vpk = self.bf([34, 2, 128], "Vp")
        PT = [self.bf([640], "PT%d" % i) for i in range(2)]
        atb = [self.alloc(128, "atb%d" % i) for i in range(2)]
        dn, dnk = self.alloc(128, "den")
        wtv, wtvk = self.wchunk_own(l, self.CH_TV, "wtv")
        P.op("pool", lambda e: e.memset(Vp, 0.0), [], [vpk])

        def rope_raw():
            for half in range(2):
                c0 = LC + half * HW
                self.dma(Ch, self.ropeC[:, half * HW:(half + 1) * HW], [], [ck])
                self.dma(Sh, self.ropeS[:, half * HW:(half + 1) * HW], [], [sk])
                self.tt("pool", tb, raw[:, c0:c0 + HW], Sh, ALU.mult, [rk, sk], [tbk])
                self.cp("dve", tsw[0:64, :], tb[64:128, :], [tbk], [tswk])
                self.cp("dve", tsw[64:128, :], tb[0:64, :], [tbk], [tswk])
                self.tt("dve", raw[:, c0:c0 + HW], raw[:, c0:c0 + HW], Ch, ALU.mult, [rk, ck], [rk])
                self.tt("dve", raw[:, c0:c0 + HW], raw[:, c0:c0 + HW], tsw, ALU.add, [rk, tswk], [rk])

        def dstq(ti, c0, n, isc, ps, pk):
            self.act(raw[:, c0:c0 + n], ps, AF.Copy, [pk], [rk])

        for kv in range(2):
            wq, wqk = self.wchunk(l, self.CH_AQ + kv)
            self.proj_rows(wq, wqk, dstq)
            rope_raw()
            self.cp("pool", Qb, raw, [rk], [qbk])
            wk_, wkk = self.wchunk(l, self.CH_AK + kv)
            self.proj_rows(wk_, wkk, dstq)
            rope_raw()
            for hl in range(2):
                kmv, kmk = Km[hl]
                self.ts("dve", kmv, raw, hm[:, hl:hl + 1], None, ALU.mult, None, [rk, hmk], [kmk])
            for b0 in range(0, 34, 8):
                nb = min(8, 34 - b0)
                pq, pqk = self.q[3], self.qk[3]
                for bi in range(nb):
                    blk = b0 + bi
                    pc = pcol(blk * 128)
                    self.mm(pq[:, bi * 64:(bi + 1) * 64],
                            [(self.xn[:, k, pc:pc + 128], wtv[:, k, kv * 64:(kv + 1) * 64]) for k in range(8)],
                            ["xn", wtvk], [pqk])
                src = pq[:, 0:nb * 64].rearrange("p (b d) -> p b d", d=64)
                self.act(Vp[:, b0:b0 + nb, 0, 0:64], src, AF.Copy, [pqk], [vpk])
                self.cp("dve", Vp[:, b0:b0 + nb, 1, 64:128], src, [pqk], [vpk])
            for qb in range(34):
                q0 = qb * 128
                kblocks = [(0, None), (1, None)]
                if qb >= 2:
                    if qb - 1 >= 2:
                        kblocks.append((qb - 1, 0))
                    kblocks.append((qb, None))
                    if qb + 1 < 34:
                        kblocks.append((qb + 1, 1))
                nkb = len(kblocks)
                po, pok = self.q[2], self.qk[2]
                nmm = 2 * nkb
                imm = 0
                for hl in range(2):
                    ps, psk = self.q[hl], self.qk[hl]
                    kmv, kmk = Km[hl]
                    ptv, ptk = PT[hl]
                    for bi, (kb, mk) in enumerate(kblocks):
                        self.mm(ps[:, bi * 128:(bi + 1) * 128],
                                [(kmv[:, kb * 128:(kb + 1) * 128], Qb[:, q0:q0 + 128])], [kmk, qbk], [psk])
                    self.act(ptv[:, 0:nkb * 128], ps[:, 0:nkb * 128], AF.Exp, [psk], [ptk], scale=0.125)
                    for bi, (kb, mk) in enumerate(kblocks):
                        if mk is not None:
                            self.tt("pool", ptv[:, bi * 128:(bi + 1) * 128], ptv[:, bi * 128:(bi + 1) * 128],
                                    cmask[:, mk, :], ALU.mult, [ptk, cmbk], [ptk])
                    for bi, (kb, mk) in enumerate(kblocks):
                        P.op("pe", lambda e, kb=kb, bi=bi, hl=hl, ptv=ptv, imm=imm, po=po, nmm=nmm: e.matmul(
                            po[:, 0:128], Vp[:, kb, hl, :], ptv[:, bi * 128:(bi + 1) * 128],
                            start=(imm == 0), stop=(imm == nmm - 1)), [vpk, ptk], [pok])
                        P.op("pe", lambda e, kb=kb, bi=bi, hl=hl, ptv=ptv, imm=imm, po=po, nmm=nmm: e.matmul(
                            po[:, 512:640], onesp[:, hl, :], ptv[:, bi * 128:(bi + 1) * 128],
                            start=(imm == 0), stop=(imm == nmm - 1)), [onk, ptk], [pok])
                        imm += 1
                av, avk = atb[qb % 2]
                self.ts("dve", dn, po[:, 512:640], esink[:, kv:kv + 1], None, ALU.add, None, [pok, esk], [dnk])
                P.op("dve", lambda e: e.reciprocal(dn, dn), [dnk], [dnk])
                self.tt("dve", av, po[:, 0:128], dn, ALU.mult, [pok, dnk], [avk])
                tix = 0 if qb < 2 else 1 + (qb - 2) // 4
                self.dma(self.MIX[1024 + kv * 128:1024 + (kv + 1) * 128, q0:q0 + 128], av, [avk], [("MIX", tix)])

    def mixer_s5(self, l, allk):
        P = self.P
        TWO_PI = 2.0 * math.pi
        self.wrot_init(1)
        SEG = 512
        segs = [(0, LC)] + [(LC + SEG * i, SEG) for i in range(LL // SEG)]
        def small(name, shape=(2, 8)):
            return self.f32(list(shape), "s5" + name)
        are, k_are = small("are"); aim, k_aim = small("aim"); ls, k_ls = small("ls")
        self.dma(are, self.s5_are[l], [], [k_are])
        self.dma(aim, self.s5_aim[l], [], [k_aim])
        self.dma(ls, self.s5_ls[l], [], [k_ls])
        dt, k_dt = small("dt"); mag, k_mag = small("mag"); th, k_th = small("th")
        tmp, k_tmp = small("tmp"); tmp2, k_tmp2 = small("tmp2")
        ti_v, k_ti = self.alloc(16, "s5ti")
        ti = ti_v.bitcast(I32).rearrange("p (a b) -> p a b", a=2)
        cosv, k_cos = small("cosv"); sinv, k_sin = small("sinv")
        fr, k_fr = small("fr"); fi, k_fi = small("fi")
        self.act(dt, ls, AF.Exp, [k_ls], [k_dt])
        self.tt("dve", tmp, are, dt, ALU.mult, [k_are, k_dt], [k_tmp])
        self.act(mag, tmp, AF.Exp, [k_tmp], [k_mag])
        self.tt("dve", th, aim, dt, ALU.mult, [k_aim, k_dt], [k_th])
        self.ts("dve", th, th, 1.0 / TWO_PI, None, ALU.mult, None, [k_th], [k_th])

        def sin_turns(dst, dkey, src, skey, shift, shp_tmp, k_t, tint, k_i, shp_tmp2, k_t2):
            self.ts("dve", shp_tmp, src, shift, None, ALU.add, None, [skey], [k_t])
            self.cp("dve", tint, shp_tmp, [k_t], [k_i])
            self.cp("dve", shp_tmp2, tint, [k_i], [k_t2])
            self.tt("dve", shp_tmp, shp_tmp, shp_tmp2, ALU.subtract, [k_t, k_t2], [k_t])
            self.act(dst, shp_tmp, AF.Sin, [k_t], [dkey], scale=TWO_PI)

        sin_turns(sinv, k_sin, th, k_th, 0.0, tmp, k_tmp, ti, k_ti, tmp2, k_tmp2)
        sin_turns(cosv, k_cos, th, k_th, 0.25, tmp, k_tmp, ti, k_ti, tmp2, k_tmp2)
        abre, k_abre = small("abre"); abim, k_abim = small("abim")
        self.tt("dve", abre, mag, cosv, ALU.mult, [k_mag, k_cos], [k_abre])
        self.tt("dve", abim, mag, sinv, ALU.mult, [k_mag, k_sin], [k_abim])
        den, k_den = small("den")
        self.tt("dve", den, are, are, ALU.mult, [k_are], [k_den])
        self.tt("dve", tmp, aim, aim, ALU.mult, [k_aim], [k_tmp])
        self.tt("dve", den, den, tmp, ALU.add, [k_den, k_tmp], [k_den])
        P.op("dve", lambda e: e.reciprocal(den, den), [k_den], [k_den])
        nr, k_nr = small("nr")
        self.ts("dve", nr, abre, -1.0, None, ALU.add, None, [k_abre], [k_nr])
        self.tt("dve", fr, nr, are, ALU.mult, [k_nr, k_are], [k_fr])
        self.tt("dve", tmp, abim, aim, ALU.mult, [k_abim, k_aim], [k_tmp])
        self.tt("dve", fr, fr, tmp, ALU.add, [k_fr, k_tmp], [k_fr])
        self.tt("dve", fr, fr, den, ALU.mult, [k_fr, k_den], [k_fr])
        self.tt("dve", fi, abim, are, ALU.mult, [k_abim, k_are], [k_fi])
        self.tt("dve", tmp, nr, aim, ALU.mult, [k_nr, k_aim], [k_tmp])
        self.tt("dve", fi, fi, tmp, ALU.subtract, [k_fi, k_tmp], [k_fi])
        self.tt("dve", fi, fi, den, ALU.mult, [k_fi, k_den], [k_fi])
        bre, k_bre = self.f32([8, 16], "s5bre"); bim, k_bim = self.f32([8, 16], "s5bim")
        self.dma(bre, self.s5_bre[l], [], [k_bre])
        self.dma(bim, self.s5_bim[l], [], [k_bim])
        dv, k_dv = self.alloc(2, "s5dv")
        self.dma(dv, self.s5_dv[l], [], [k_dv])
        ident, k_id = self.alloc(128, "ident")
        self.dma(ident, self.ident_in, [], [k_id])
        iota, k_io = self.alloc(SEG, "iota")
        self.dma(iota, self.iota_in[:, 0:SEG], [], [k_io])
        glu32 = self.f32([2, 256], "glu32")
        glub, k_glub = self.bf([2, 256], "glub")
        self.dma(glu32[0], self.s5_glu[l].rearrange("(k p) n -> p k n", p=128), [], [glu32[1]])
        self.cp("dve", glub, glu32[0], [glu32[1]], [k_glub])
        U, k_u = self.f32([2, T], "s5U")
        Y, k_y = self.f32([2, T], "s5Y")
        P.op("pool", lambda e: e.memset(Y, 0.0), [], [k_y])
        for c in range(2):
            wv, wk = self.wchunk(l, self.CH_U + c)
            def dst(ti_, c0, n, isc, ps, pk, c=c):
                self.act(U[:, c, c0:c0 + n], ps, AF.Copy, [pk], [k_u])
            self.proj_rows(wv, wk, dst)
        bb1, k_bb1 = self.alloc(16, "bb1"); bb2, k_bb2 = self.alloc(16, "bb2")
        Bpad, k_bp = self.alloc(128, "Bpad")
        BBts = [self.f32([2, 128], "BBt%d" % i) for i in range(2)]
        Cres = [self.alloc(128, "Cre%d" % i) for i in range(2)]
        Cims = [self.alloc(128, "Cim%d" % i) for i in range(2)]
        RBs = [self.alloc(SEG, "RB%d" % i) for i in range(2)]
        sts = [self.alloc(2, "s5st%d" % i) for i in range(2)]
        T1, k1 = self.alloc(SEG, "s5T1"); T2, k2 = self.alloc(SEG, "s5T2"); T3, k3 = self.alloc(SEG, "s5T3")
        TF, ktf = self.alloc(SEG, "s5TF")
        TIv, k_TI = self.alloc(SEG, "s5TI")
        TI = TIv.bitcast(I32)
        CSs = [self.alloc(SEG, "s5CS%d" % i) for i in range(2)]
        SNs = [self.alloc(SEG, "s5SN%d" % i) for i in range(2)]
        ZRZ, _ = self.alloc(2 * SEG, "s5ZRZ")
        ZRs = [(ZRZ[:, 0:SEG], "s5ZR0k"), (ZRZ[:, SEG:2 * SEG], "s5ZR1k")]
        ZIs = [self.alloc(SEG, "s5ZI%d" % i) for i in range(2)]
        U1, ku1 = self.alloc(SEG, "s5U1"); U2, ku2 = self.alloc(SEG, "s5U2"); U3, ku3 = self.alloc(SEG, "s5U3")

        def V(ap, c0, n, rev):
            a_ = ap[:, c0:c0 + n]
            return a_[:, ::-1] if rev else a_

        its = []
        for d in range(2):
            order = list(range(len(segs))) if d == 0 else [0] + list(range(len(segs) - 1, 0, -1))
            for j in range(8):
                for oi, si in enumerate(order):
                    its.append((d, j, oi, si))

        def prep_dj(d, j):
            pb = (d * 8 + j) % 2
            BBt, k_bbt = BBts[pb]
            m0 = 32 * (j % 4)
            for ri in range(2):
                x1, kx1 = (bre, k_bre) if ri == 0 else (bim, k_bim)
                x2, kx2 = (bim, k_bim) if ri == 0 else (bre, k_bre)
                self.ts("dve", bb1, x1[:, j, :], fr[:, d, j:j + 1], None, ALU.mult, None, [kx1, k_fr], [k_bb1])
                self.ts("dve", bb2, x2[:, j, :], fi[:, d, j:j + 1], None, ALU.mult, None, [kx2, k_fi], [k_bb2])
                self.tt("dve", bb1, bb1, bb2, ALU.subtract if ri == 0 else ALU.add, [k_bb1, k_bb2], [k_bb1])
                P.op("dve", lambda e: e.memset(Bpad, 0.0), [], [k_bp])
                self.cp("dve", Bpad[0:64, m0:m0 + 16], bb1[0:64, :], [k_bb1], [k_bp])
                self.cp("dve", Bpad[64:128, m0 + 16:m0 + 32], bb1[64:128, :], [k_bb1], [k_bp])
                pq, pqk = self.q[3], self.qk[3]
                P.op("pe", lambda e, pq=pq: e.transpose(pq[:, 512:640], Bpad, ident), [k_bp, k_id], [(pqk, "tr")])
                self.act(BBt[:, ri, :], pq[:, 512:640], AF.Copy, [(pqk, "tr")], [k_bbt])
            Cre, k_cre = Cres[pb]
            Cim, k_cim = Cims[pb]
            self.dma(Cre, self.s5_cre[l][:, d, j, :], [], [k_cre])
            self.dma(Cim, self.s5_cim[l][:, d, j, :], [], [k_cim])
            RB, k_rb = RBs[pb]
            self.act(RB, iota, AF.Identity, [k_io, k_mag], [k_rb], scale=0.0, bias=mag[:, d, j:j + 1])
            st, k_st = sts[pb]
            P.op("dve", lambda e, st=st: e.memset(st, 0.0), [], [k_st])

        def stage_a(idx):
            d, j, oi, si = its[idx]
            if oi == 0:
                prep_dj(d, j)
            pb = (d * 8 + j) % 2
            b = idx % 2
            rev = (d == 1)
            c0, n = segs[si]
            n0 = c0 if d == 0 else (0 if si == 0 else LC + (T - c0 - n))
            BBt, k_bbt = BBts[pb]
            urhs = V(U[:, j // 4, :], c0, n, rev)
            pp, kpp = self.q[b], self.qk[b]
            self.mm(pp[:, 0:n], [(BBt[:, 0, :], urhs)], [k_bbt, k_u], [kpp])
            self.mm(pp[:, 512:512 + n], [(BBt[:, 1, :], urhs)], [k_bbt, k_u], [kpp])
            CS, kc = CSs[b]
            SN, ks = SNs[b]
            P.op("dve", lambda e, n=n, n0=n0, d=d, j=j: e.tensor_scalar(
                T1[:, 0:n], iota[:, 0:n], float(n0), th[:, d, j:j + 1], ALU.add, ALU.mult), [k_io, k_th], [k1])
            for (dst_, kd, shift) in ((SN, ks, 0.0), (CS, kc, 0.25)):
                if shift:
                    self.ts("dve", T1[:, 0:n], T1[:, 0:n], shift, None, ALU.add, None, [k1], [k1])
                self.cp("dve", TI[:, 0:n], T1[:, 0:n], [k1], [k_TI])
                self.cp("dve", TF[:, 0:n], TI[:, 0:n], [k_TI], [ktf])
                self.tt("dve", T2[:, 0:n], T1[:, 0:n], TF[:, 0:n], ALU.subtract, [k1, ktf], [k2])
                self.act(dst_[:, 0:n], T2[:, 0:n], AF.Sin, [k2], [kd], scale=TWO_PI)

        def stage_b(idx):
            d, j, oi, si = its[idx]
            pb = (d * 8 + j) % 2
            b = idx % 2
            rev = (d == 1)
            c0, n = segs[si]
            pp, kpp = self.q[b], self.qk[b]
            pre = pp[:, 0:n]
            pim = pp[:, 512:512 + n]
            CS, kc = CSs[b]
            SN, ks = SNs[b]
            ZR, kzr = ZRs[b]
            ZI, kzi = ZIs[b]
            RB, k_rb = RBs[pb]
            st, k_st = sts[pb]
            Cre, k_cre = Cres[pb]
            Cim, k_cim = Cims[pb]
            self.tt("dve", T1[:, 0:n], pre, CS[:, 0:n], ALU.mult, [kpp, kc], [k1])
            self.tt("dve", T2[:, 0:n], pim, SN[:, 0:n], ALU.mult, [kpp, ks], [k2])
            self.tt("dve", T1[:, 0:n], T1[:, 0:n], T2[:, 0:n], ALU.add, [k1, k2], [k1])
            self.tt("dve", T3[:, 0:n], pim, CS[:, 0:n], ALU.mult, [kpp, kc], [k3])
            self.tt("dve", T2[:, 0:n], pre, SN[:, 0:n], ALU.mult, [kpp, ks], [k2])
            self.tt("dve", T3[:, 0:n], T3[:, 0:n], T2[:, 0:n], ALU.subtract, [k3, k2], [k3])
            P.op("dve", lambda e, n=n: e.tensor_tensor_scan(ZR[:, 0:n], RB[:, 0:n], T1[:, 0:n], st[:, 0:1],
                                                            ALU.mult, ALU.add), [k_rb, k1, k_st], [kzr])
            P.op("dve", lambda e, n=n: e.tensor_tensor_scan(ZI[:, 0:n], RB[:, 0:n], T3[:, 0:n], st[:, 1:2],
                                                            ALU.mult, ALU.add), [k_rb, k3, k_st], [kzi])
            self.act(st[:, 0:1], ZR[:, n - 1:n], AF.Copy, [kzr], [k_st])
            self.act(st[:, 1:2], ZI[:, n - 1:n], AF.Copy, [kzi], [k_st])
            self.tt("dve", U1[:, 0:n], ZR[:, 0:n], CS[:, 0:n], ALU.mult, [kzr, kc], [ku1])
            self.tt("dve", U2[:, 0:n], ZI[:, 0:n], SN[:, 0:n], ALU.mult, [kzi, ks], [ku2])
            self.tt("dve", U1[:, 0:n], U1[:, 0:n], U2[:, 0:n], ALU.subtract, [ku1, ku2], [ku1])
            self.tt("dve", U3[:, 0:n], ZR[:, 0:n], SN[:, 0:n], ALU.mult, [kzr, ks], [ku3])
            self.tt("dve", U2[:, 0:n], ZI[:, 0:n], CS[:, 0:n], ALU.mult, [kzi, kc], [ku2])
            self.stt("dve", U3[:, 0:n], U2[:, 0:n], -1.0, U3[:, 0:n], ALU.mult, ALU.subtract, [ku2, ku3], [ku3])
            pyv = self.q[2 + idx % 2][:, 0:n]
            kpyv = self.qk[2 + idx % 2]
            self.mm(pyv, [(Cre, U1[:, 0:n]), (Cim, U3[:, 0:n])], [k_cre, k_cim, ku1, ku3], [kpyv])
            if idx > 0:
                stage_c(idx - 1)

        def stage_c(idx):
            d, j, oi, si = its[idx]
            c0, n = segs[si]
            pyv = self.q[2 + idx % 2][:, 0:n]
            kpyv = self.qk[2 + idx % 2]
            yv = V(Y[:, j // 4, :], c0, n, (d == 1))
            self.tt("dve", yv, yv, pyv, ALU.add, [k_y, kpyv], [k_y])

        stage_a(0)
        for idx in range(len(its)):
            if idx + 1 < len(its):
                stage_a(idx + 1)
            stage_b(idx)
        stage_c(len(its) - 1)
        _barrier(P)
        zt, kz = ZRZ.rearrange("p (a b) -> p a b", a=2), "s5ztk"
        zbv, kzb = SNs[0]
        zb = zbv.bitcast(BF16).rearrange("p (a b) -> p a b", a=2)
        sg, ksg = SNs[1]
        ob = [CSs[0], CSs[1]]
        for ti_, (c0, n, isc) in enumerate(TILES):
            for c in range(2):
                self.stt("dve", zt[:, c, 0:n], U[:, c, c0:c0 + n], dv[:, c:c + 1], Y[:, c, c0:c0 + n], ALU.mult, ALU.add,
                         [k_u, k_y, k_dv], [kz])
            self.act(zt[:, :, 0:n], zt[:, :, 0:n], AF.Gelu, [kz], [kz])
            self.cp("pool", zb[:, :, 0:n], zt[:, :, 0:n], [kz], [kzb])
            for oc in range(2):
                pq, pqk = self.q[oc], self.qk[oc]
                self.mm(pq[:, 0:n], [(glub[:, k, oc * 128:(oc + 1) * 128], zb[:, k, 0:n]) for k in range(2)],
                        [k_glub, kzb], [pqk])
                self.act(sg[:, 0:n], pq[:, 0:n], AF.Sigmoid, [pqk], [ksg])
                ov, ok = ob[oc]
                self.tt("dve", ov[:, 0:n], zt[:, oc, 0:n], sg[:, 0:n], ALU.mult, [kz, ksg], [ok])
                self.dma(self.MIX[512 + oc * 128:512 + (oc + 1) * 128, c0:c0 + n], ov[:, 0:n], [ok], [("MIX", ti_)])

    CHK = 16

    def mixer_hg(self, l, allk):
        P = self.P
        C = self.CHK
        self.wrot_init(2)
        NB_ = T // 128
        m32, k_m32 = self.f32([2, 128], "g2m32")
        cmk, k_cmk = self.bf([2, 128], "g2cm")
        self.dma(m32, self.hg2_mask, [], [k_m32])
        self.cp("dve", cmk, m32, [k_m32], [k_cmk])
        rmask, k_rm = self.alloc(8, "g2rm")
        self.dma(rmask, self.hg2_rowmask, [], [k_rm])
        ident, k_id = self.alloc(128, "g2id")
        self.dma(ident, self.ident_in, [], [k_id])
        RST, k_rst = self.bf([512], "g2rst")
        P.op("pool", lambda e: e.memset(RST, 1.0), [], [k_rst])
        P.op("pool", lambda e: e.memset(RST.rearrange("p (a c) -> p a c", c=C)[:, :, 0:1], 0.0), [], [k_rst])
        Qp, k_q = self.alloc(T, "g2Q")
        Fb, k_f = self.alloc(T, "g2F")
        EC, k_ec = self.alloc(T, "g2EC")
        Bm, k_bm = self.alloc(T, "g2Bm")
        Ab, k_a = self.bf([T], "g2A")
        Bb, k_bb = self.bf([T], "g2Bb")
        Vt, k_v = self.bf([NB_, 128], "g2V")
        Vpad, k_vp = self.bf([2, 128], "g2Vpad")
        Vm = [self.bf([8, 2, 128], "g2Vm%d" % i) for i in range(2)]
        BmT = [self.bf([2, 128], "g2BmT%d" % i) for i in range(2)]
        PT = [self.bf([2, 128], "g2PT%d" % i) for i in range(2)]
        dsd = [self.f32([8, 128], "g2dsd%d" % i) for i in range(2)]
        S, k_s = self.alloc(128, "g2S")
        SPb = [self.bf([128], "g2SP%d" % i) for i in range(2)]
        OB = [self.alloc(128, "g2OB%d" % i) for i in range(2)]
        for t_, k_ in (Vm + BmT):
            P.op("pool", lambda e, t_=t_: e.memset(t_, 0.0), [], [k_])
        P.op("pool", lambda e: e.memset(Vpad, 0.0), [], [k_vp])

        def V(ap, c0, n, rev):
            a_ = ap[:, c0:c0 + n]
            return a_[:, ::-1] if rev else a_

        for hp in range(2):
            wq, wqk = self.wchunk(l, self.CH_N + 0 + hp)
            def dq(ti, c0, n, isc, ps, pk):
                self.act(Qp[:, c0:c0 + n], ps, AF.Silu, [pk], [k_q])
            self.proj_rows(wq, wqk, dq)
            wv, wvk = self.wchunk(l, self.CH_N + 6 + hp)
            for b0 in range(0, NB_, 4):
                nb = min(4, NB_ - b0)
                pq, pqk = self.q[3], self.qk[3]
                for bi in range(nb):
                    pc = pcol((b0 + bi) * 128)
                    self.mm(pq[:, bi * 128:(bi + 1) * 128],
                            [(self.xn[:, k, pc:pc + 128], wv[:, k, :]) for k in range(8)], ["xn", wvk], [pqk])
                self.act(Vt[:, b0:b0 + nb, :], pq[:, 0:nb * 128].rearrange("p (b d) -> p b d", d=128), AF.Copy,
                         [pqk], [k_v])
            for d in range(2):
                rev = (d == 1)
                wf, wfk = self.wchunk(l, self.CH_N + 2 + 2 * d + hp)
                def df(ti, c0, n, isc, ps, pk):
                    self.act(Fb[:, c0:c0 + n], ps, AF.Sigmoid, [pk], [k_f])
                self.proj_rows(wf, wfk, df)
                self.ts("dve", Fb, Fb, self.oml2[:, d, hp, l:l + 1], self.lbt2[:, d, hp, l:l + 1], ALU.mult, ALU.add,
                        [k_f, "oml2", "lbt2"], [k_f])
                self.act(EC, Fb, AF.Ln, [k_f], [k_ec])
                for (c0, n) in [(0, LC)] + [(LC + 512 * i, 512) for i in range(LL // 512)]:
                    P.op("dve", lambda e, c0=c0, n=n, rev=rev: e.tensor_tensor_scan(
                        V(EC, c0, n, rev), RST[:, 0:n], V(EC, c0, n, rev), 0.0, ALU.mult, ALU.add), [k_rst, k_ec], [k_ec])
                self.act(Bm, EC, AF.Exp, [k_ec], [k_bm], scale=-1.0)
                self.act(EC, EC, AF.Exp, [k_ec], [k_ec])
                self.tt("dve", Ab, Qp, EC, ALU.mult, [k_q, k_ec], [k_a])
                self.ts("dve", Fb, Fb, -1.0, 1.0, ALU.mult, ALU.add, [k_f], [k_f])
                self.tt("dve", Bm, Bm, Fb, ALU.mult, [k_bm, k_f], [k_bm])
                self.cp("pool", Bb, Bm, [k_bm], [k_bb])
                _barrier(P)
                P.op("dve", lambda e: e.memset(S, 0.0), [], [k_s])
                P.op("pool", lambda e: e.memset(SPb[0][0], 0.0), [], [SPb[0][1]])
                order = list(range(NB_)) if d == 0 else [1, 0] + list(range(NB_ - 1, 1, -1))
                spi = [0]

                def front(bi_):
                    blk = order[bi_]
                    b = bi_ % 2
                    cs = slice(blk * 128, (blk + 1) * 128)
                    pt_, kpt = self.q[3], (self.qk[3], "tr", b)
                    ptv = pt_[:, b * 128:(b + 1) * 128]
                    P.op("pe", lambda e, ptv=ptv, cs=cs: e.transpose(ptv, Bm[:, cs], ident), [k_bm, k_id], [kpt])
                    bt, kbt = BmT[b]
                    self.act(bt[:, 0, 0:64], ptv[:, 0:64], AF.Copy, [kpt], [kbt])
                    self.act(bt[:, 1, 64:128], ptv[:, 64:128], AF.Copy, [kpt], [kbt])
                    vm, kvm = Vm[b]
                    for a in range(2):
                        self.tt("dve", vm[:, :, a, a * 64:(a + 1) * 64],
                                Vt[:, blk, a * 64:(a + 1) * 64].unsqueeze(1).to_broadcast([128, 8, 64]),
                                rmask.unsqueeze(2).to_broadcast([128, 8, 64]), ALU.mult, [k_v, k_rm], [kvm])
                    pd, kpd = self.q[b], self.qk[b]
                    for c8 in range(8):
                        self.mm(pd[:, c8 * 128:(c8 + 1) * 128], [(bt[:, a, :], vm[:, c8, a, :]) for a in range(2)],
                                [kbt, kvm], [kpd])
                    dv_, kdv = dsd[b]
                    for c8 in range(8):
                        col = blk * 128 + c8 * C + (C - 1 if d == 0 else 0)
                        self.act(dv_[:, c8, :], pd[:, c8 * 128:(c8 + 1) * 128], AF.Identity, [kpd, k_ec], [kdv],
                                 scale=EC[:, col:col + 1])
                    ps_, kps = self.q[2], (self.qk[2], b)
                    pv_, kpv = PT[b]
                    for a in range(2):
                        sl = slice(a * 64, (a + 1) * 64)
                        psv = ps_[:, (2 * b + a) * 128:(2 * b + a + 1) * 128]
                        self.mm(psv, [(Bb[sl, cs], Ab[sl, cs])], [k_bb, k_a], [kps])
                        self.tt("dve", pv_[:, a, :], psv, cmk[:, d, :], ALU.mult, [kps, k_cmk], [kpv])

                def back(bi_):
                    blk = order[bi_]
                    b = bi_ % 2
                    cs0 = blk * 128
                    pv_, kpv = PT[b]
                    dv_, kdv = dsd[b]
                    po, kpo = self.q[3], (self.qk[3], "o", b)
                    pov = po[:, 512 + b * 128:512 + (b + 1) * 128]
                    P.op("pool", lambda e, blk=blk: e.tensor_copy(Vpad[:, 0, 0:64], Vt[:, blk, 0:64]), [k_v], [k_vp])
                    P.op("pool", lambda e, blk=blk: e.tensor_copy(Vpad[:, 1, 64:128], Vt[:, blk, 64:128]), [k_v], [k_vp])
                    nmm = 2 + 8
                    im = 0
                    for a in range(2):
                        P.op("pe", lambda e, a=a, im=im, pov=pov, pv_=pv_: e.matmul(
                            pov, Vpad[:, a, :], pv_[:, a, :], start=(im == 0), stop=False), [k_vp, kpv], [kpo])
                        im += 1
                    cl = list(range(8)) if d == 0 else list(range(7, -1, -1))
                    for ci, c8 in enumerate(cl):
                        sp, ksp = SPb[spi[0] % 2]
                        P.op("pe", lambda e, c8=c8, sp=sp, pov=pov, ci=ci: e.matmul(
                            pov[:, c8 * C:(c8 + 1) * C], sp, Ab[:, cs0 + c8 * C:cs0 + (c8 + 1) * C],
                            start=False, stop=(ci == 7), skip_group_check=True), [ksp, k_a], [kpo])
                        col = cs0 + c8 * C + (C - 1 if d == 0 else 0)
                        self.stt("dve", S, S, EC[:, col:col + 1], dv_[:, c8, :], ALU.mult, ALU.add, [k_s, k_ec, kdv], [k_s])
                        spi[0] += 1
                        sp2, ksp2 = SPb[spi[0] % 2]
                        self.act(sp2, S, AF.Copy, [k_s], [ksp2])
                    ob, kob = OB[b]
                    self.cp("pool", ob, pov, [kpo], [kob]) if False else self.act(ob, pov, AF.Copy, [kpo], [kob])
                    tix = 0 if blk < 2 else 1 + (blk - 2) // 4
                    self.dma(self.MIX[d * 256 + hp * 128:d * 256 + (hp + 1) * 128, cs0:cs0 + 128], ob, [kob], [("MIX", tix)])

                front(0)
                for bi_ in range(NB_):
                    if bi_ + 1 < NB_:
                        front(bi_ + 1)
                    back(bi_)
                _barrier(P)

    def declare_hyena_inputs(self):
        inp = self.inp
        self.hy_zemb = {LL: inp("hy_zemb_l", [33, LL]), LC: inp("hy_zemb_c", [33, LC])}
        self.hy_win = {LL: inp("hy_win_l", [256, LL]), LC: inp("hy_win_c", [256, LC])}
        self.hy_w1 = inp("hy_w1", [DEPTH, 33, 64])
        self.hy_b1 = inp("hy_b1", [DEPTH, 64, 1])
        self.hy_fr = inp("hy_fr", [DEPTH, 64, 2])
        self.hy_w2 = inp("hy_w2", [DEPTH, 64, 64])
        self.hy_b2 = inp("hy_b2", [DEPTH, 64, 1])
        self.hy_w3 = inp("hy_w3", [DEPTH, 64, 1024])
        self.hy_cw = inp("hy_cw", [DEPTH, 3, 768])
        self.hy_cb = inp("hy_cb", [DEPTH, 768])
        self.hy_bias = inp("hy_bias", [DEPTH, 2, 256])
        self.jmat = inp("jmat", [128, 128])
        self.GK = {LL: self.dram("GKl", [2, 256, 2 * LL], BF16), LC: self.dram("GKc", [2, 256, 2 * LC], BF16)}

    def hy_filters(self, l):
        P = self.P
        TWO_PI = 2.0 * math.pi
        self.phase()
        w1, k_w1 = self.alloc(64, "hw1"); w2, k_w2 = self.alloc(64, "hw2"); w3, k_w3 = self.alloc(1024, "hw3")
        b1, k_b1 = self.alloc(1, "hb1"); b2, k_b2 = self.alloc(1, "hb2"); fr, k_fr = self.alloc(2, "hfr")
        self.dma(w1[0:33, :], self.hy_w1[l], [], [k_w1])
        self.dma(w2[0:64, :], self.hy_w2[l], [], [k_w2])
        self.dma(w3[0:64, :], self.hy_w3[l], [], [k_w3])
        self.dma(b1[0:64, :], self.hy_b1[l], [], [k_b1])
        self.dma(b2[0:64, :], self.hy_b2[l], [], [k_b2])
        self.dma(fr[0:64, :], self.hy_fr[l], [], [k_fr])
        self.ts("dve", fr[0:64, :], fr[0:64, :], 1.0 / TWO_PI, None, ALU.mult, None, [k_fr], [k_fr])
        ze, k_ze = self.alloc(512, "hze")
        t1, k_t1 = self.alloc(512, "ht1"); t2, k_t2 = self.alloc(512, "ht2")
        tiv, k_ti = self.alloc(512, "hti")
        tint = tiv.bitcast(I32)
        h1, k_h1 = self.alloc(512, "hh1")
        h2, k_h2 = self.alloc(LL, "hh2")
        hf, k_hf = self.alloc(LL, "hhf"); hb, k_hb = self.alloc(LL, "hhb")
        win, k_win = self.alloc(LL, "hwin")
        Gt, k_gt = self.bf([2 * LL], "hGt")

        def sin_layer(dst, kd, ps, kps, bias, kb, frs, n):
            P.op("dve", lambda e: e.tensor_scalar(t1[0:64, 0:n], ps, bias, frs, ALU.add, ALU.mult), [kps, kb, k_fr], [k_t1])
            self.cp("dve", tint[0:64, 0:n], t1[0:64, 0:n], [k_t1], [k_ti])
            self.cp("dve", t2[0:64, 0:n], tint[0:64, 0:n], [k_ti], [k_t2])
            self.tt("dve", t1[0:64, 0:n], t1[0:64, 0:n], t2[0:64, 0:n], ALU.subtract, [k_t1, k_t2], [k_t1])
            self.act(dst, t1[0:64, 0:n], AF.Sin, [k_t1], [kd], scale=TWO_PI)

        for L in (LL, LC):
            for c0 in range(0, L, 512):
                n = min(512, L - c0)
                self.dma(ze[0:33, 0:n], self.hy_zemb[L][:, c0:c0 + n], [], [k_ze])
                pq, pqk = self.q[0], self.qk[0]
                self.mm(pq[0:64, 0:n], [(w1[0:33, :], ze[0:33, 0:n])], [k_w1, k_ze], [pqk])
                sin_layer(h1[0:64, 0:n], k_h1, pq[0:64, 0:n], pqk, b1[0:64, 0:1], k_b1, fr[0:64, 0:1], n)
                pq2, pqk2 = self.q[1], self.qk[1]
                self.mm(pq2[0:64, 0:n], [(w2[0:64, :], h1[0:64, 0:n])], [k_w2, k_h1], [pqk2])
                sin_layer(h2[0:64, c0:c0 + n], k_h2, pq2[0:64, 0:n], pqk2, b2[0:64, 0:1], k_b2, fr[0:64, 1:2], n)
            for o in range(2):
                for chalf in range(2):
                    self.dma(win[:, 0:L], self.hy_win[L][chalf * 128:(chalf + 1) * 128, :], [], [k_win])
                    for di, (dst, kd) in enumerate(((hf, k_hf), (hb, k_hb))):
                        cc = o * 512 + di * 256 + chalf * 128
                        for c0 in range(0, L, 512):
                            n = min(512, L - c0)
                            pq, pqk = self.q[2 + (c0 // 512) % 2], self.qk[2 + (c0 // 512) % 2]
                            self.mm(pq[:, 0:n], [(w3[0:64, cc:cc + 128], h2[0:64, c0:c0 + n])], [k_w3, k_h2], [pqk])
                            self.tt("dve", dst[:, c0:c0 + n], pq[:, 0:n], win[:, c0:c0 + n], ALU.mult, [pqk, k_win], [kd])
                    self.cp("pool", Gt[:, 0:L - 1], hb[:, 1:L][:, ::-1], [k_hb], [k_gt])
                    self.tt("pool", Gt[:, L - 1:L], hf[:, 0:1], hb[:, 0:1], ALU.add, [k_hf, k_hb], [k_gt])
                    self.cp("pool", Gt[:, L:2 * L - 1], hf[:, 1:L], [k_hf], [k_gt])
                    P.op("pool", lambda e, L=L: e.memset(Gt[:, 2 * L - 1:2 * L], 0.0), [], [k_gt])
                    self.dma(self.GK[L][o, chalf * 128:(chalf + 1) * 128, :], Gt[:, 0:2 * L], [k_gt], [("GK", L)])

    def mixer_hy(self, l, allk):
        P = self.P
        self.hy_filters(l)
        self.phase()
        NBT = 34
        Wg, k_wg = self.bf([8, 3, 192], "yWg")
        stg, k_stg = self.f32([8, 192], "ystg")
        cwb, k_cwb = self.f32([3, 192], "ycwb")
        cbb, k_cbb = self.alloc(192, "ycbb")
        bsb, k_bsb = self.f32([2, 64], "ybsb")
        PJ, k_pj = self.f32([NBT, 192], "yPJ")
        Zl, k_zl = self.bf([94, 64], "yZl")
        Zc, k_zc = self.bf([4, 64], "yZc")
        zb, k_zb = self.bf([32, 64], "yzb")
        z1, k_z1 = self.f32([32, 64], "yz1")
        tmp, k_tmp = self.f32([32, 64], "ytmp")
        strips = [self.bf([2 * LL - 128], "ystrip%d" % i) for i in range(2)]
        j32, k_j32 = self.alloc(128, "yj32")
        Jb, k_jb = self.bf([128], "yJb")
        ident, k_id = self.alloc(128, "yident")
        OT = [self.alloc(512, "yOT%d" % i) for i in range(2)]
        self.dma(j32, self.jmat, [], [k_j32])
        self.cp("dve", Jb, j32, [k_j32], [k_jb])
        self.dma(ident, self.ident_in, [], [k_id])
        P.op("pool", lambda e: e.memset(Zl, 0.0), [], [k_zl])
        P.op("pool", lambda e: e.memset(Zc, 0.0), [], [k_zc])
        wv = self.w_mx[l][:, self.CH_HY * 128:(self.CH_HY + 6) * 128].rearrange("(k p) n -> p k n", p=128)
        si = 0
        for cg in range(4):
            for part in range(3):
                cs = part * 256 + cg * 64
                self.dma(stg[:, :, part * 64:(part + 1) * 64], wv[:, :, cs:cs + 64], [], [k_stg])
                self.dma(cwb[:, :, part * 64:(part + 1) * 64], self.hy_cw[l][:, cs:cs + 64].partition_broadcast(128), [], [k_cwb])
                self.dma(cbb[:, part * 64:(part + 1) * 64], self.hy_cb[l][cs:cs + 64].partition_broadcast(128), [], [k_cbb])
            self.dma(bsb, self.hy_bias[l][:, cg * 64:(cg + 1) * 64].partition_broadcast(128), [], [k_bsb])
            for tap in range(3):
                self.tt("dve", Wg[:, :, tap, :], stg, cwb[:, tap, :].unsqueeze(1).to_broadcast([128, 8, 192]), ALU.mult,
                        [k_stg, k_cwb], [k_wg])
            for a in range(NBT):
                pc = pcol(a * 128)
                pq, pqk = self.q[a % 2], self.qk[a % 2]
                self.mm(pq[:, 0:192], [(self.xn[:, kk, pc + tap - 1:pc + tap - 1 + 128], Wg[:, kk, tap, :])
                                       for tap in range(3) for kk in range(8)], ["xn", k_wg], [pqk])
                self.tt("dve", PJ[:, a, :], pq[:, 0:192], cbb, ALU.add, [pqk, k_cbb], [k_pj])
            for (blk0, nb, L, Zp, k_zp) in ((0, 2, LC, Zc, k_zc), (2, 32, LL, Zl, k_zl)):
                X1 = PJ[:, blk0:blk0 + nb, 0:64]
                X2 = PJ[:, blk0:blk0 + nb, 64:128]
                Z0 = PJ[:, blk0:blk0 + nb, 128:192]
                for o in range(2):
                    zprev = Z0 if o == 0 else z1[:, 0:nb, :]
                    kprev = k_pj if o == 0 else k_z1
                    gate = X1 if o == 0 else X2
                    self.cp("pool", zb[:, 0:nb, :], zprev, [kprev], [k_zb])
                    for a0 in range(0, nb, 8):
                        an = min(8, nb - a0)
                        pq, pqk = self.q[2], self.qk[2]
                        self.mm(pq[:, 0:an * 64], [(Jb, zb[:, a0:a0 + an, :])], [k_jb, k_zb], [pqk])
                        self.act(Zp[:, nb - 1 + a0:nb - 1 + a0 + an, :],
                                 pq[:, 0:an * 64].rearrange("p (a c) -> p a c", c=64), AF.Copy, [pqk], [k_zp])
                    for c16 in range(4):
                        pq, pqk = self.q[c16 % 2], self.qk[c16 % 2]
                        for ci in range(16):
                            c = c16 * 16 + ci
                            ch = cg * 64 + c
                            sv, k_sv = strips[si % 2]
                            si += 1
                            W_ = 2 * L - 128
                            gk = self.GK[L]
                            src = bass.AP(tensor=gk.tensor, offset=gk[o, ch, 0:1].offset, ap=[[1, 128], [1, W_]])
                            self.dma(sv[:, 0:W_], src, [("GK", L)], [k_sv])
                            nl = 2 * nb - 1
                            for di in range(nl):
                                d = di - (nb - 1)
                                P.op("pe", lambda e, pq=pq, ci=ci, nb=nb, sv=sv, di=di, d=d, Zp=Zp, c=c, nl=nl: e.matmul(
                                    pq[:, ci * nb:(ci + 1) * nb], sv[:, 128 * di:128 * di + 128],
                                    Zp[:, nb - 1 - d:nb - 1 - d + nb, c], start=(di == 0), stop=(di == nl - 1)),
                                    [k_sv, k_zp], [pqk])
                        cs_ = slice(c16 * 16, (c16 + 1) * 16)
                        tv = tmp[:, 0:nb, cs_]
                        self.tt("dve", tv, zprev[:, :, cs_], bsb[:, o, cs_].unsqueeze(1).to_broadcast([128, nb, 16]), ALU.mult,
                                [kprev, k_bsb], [k_tmp])
                        self.tt("dve", tv, tv, pq[:, 0:16 * nb].rearrange("p (c a) -> p a c", a=nb), ALU.add,
                                [k_tmp, pqk], [k_tmp])
                    if o == 0:
                        self.tt("dve", z1[:, 0:nb, :], tmp[:, 0:nb, :], gate, ALU.mult, [k_tmp, k_pj], [k_z1])
                    else:
                        self.tt("dve", tmp[:, 0:nb, :], tmp[:, 0:nb, :], gate, ALU.mult, [k_tmp, k_pj], [k_tmp])
                for a0 in range(0, nb, 4):
                    an = min(4, nb - a0)
                    pq, pqk = self.q[3], self.qk[3]
                    for ai in range(an):
                        P.op("pe", lambda e, pq=pq, ai=ai, a0=a0: e.transpose(
                            pq[0:64, ai * 128:(ai + 1) * 128], tmp[:, a0 + ai, :], ident), [k_tmp, k_id], [pqk])
                    ov, k_ov = OT[(a0 // 4) % 2]
                    self.act(ov[0:64, 0:an * 128], pq[0:64, 0:an * 128], AF.Copy, [pqk], [k_ov])
                    col = (blk0 + a0) * 128
                    self.dma(self.MIX[768 + cg * 64:768 + (cg + 1) * 64, col:col + an * 128], ov[0:64, 0:an * 128],
                             [k_ov], allk)


def _pk(v, k):
    return np.ascontiguousarray(v.reshape(k, 128).T)


def prep_shared(inp):
    f = lambda a: np.ascontiguousarray(a, dtype=np.float32)
    sh = {}
    sh["ada_w"] = f(inp["ada_w"])
    sh["ada_b"] = f(inp["ada_b"].reshape(DEPTH, 48, 128).transpose(0, 2, 1))
    sh["nmw"] = f(inp["norm_mix_w"].reshape(DEPTH, 8, 128).transpose(0, 2, 1))
    sh["nfw"] = f(inp["norm_ffn_w"].reshape(DEPTH, 8, 128).transpose(0, 2, 1))
    sh["fnw"] = f(_pk(inp["final_norm_w"], 8))
    sh["w_in"] = f(inp["w_in"])
    sh["w_out"] = f(inp["w_out"])
    sh["w_up"] = f(inp["ffn_w_up"])
    sh["w_dn"] = f(inp["ffn_w_down"])
    sh["fcw"] = f(inp["ffn_conv_w"].reshape(DEPTH, 3, 22, 128).transpose(0, 3, 1, 2))
    sh["fcb"] = f(inp["ffn_conv_b"].reshape(DEPTH, 22, 128).transpose(0, 2, 1))
    sh["mnw"] = f(inp["merge_norm_w"].reshape(DEPTH, 8, 128).transpose(0, 2, 1))
    return sh


def prep_core(inp, b):
    d = {}
    d["xin"] = np.ascontiguousarray(np.concatenate([inp["ctx"][b].T, inp["x"][b].T], axis=1), dtype=np.float32)
    cv = np.zeros((128, 8, 2), np.float32)
    cv[:, :, 0] = _pk(inp["c"][b], 8)
    cv[:, :, 1] = _pk(inp["c_ctx"], 8)
    d["cvec"] = cv
    return d


def kernel(**inputs):
    inp = {k: np.asarray(v) for k, v in inputs.items()}
    bld = Builder()
    nc = bld.build()
    sh = prep_shared(inp)
    sh.update(prep_mixer_shared(inp))
    in_maps = []
    for b in range(NCORES):
        d = dict(sh)
        d.update(prep_core(inp, b))
        in_maps.append({k: d[k] for k in bld.din})
    res = run_bass_kernel_spmd(nc, in_maps, core_ids=list(range(NCORES)))
    outs = [np.asarray(r["out"]).T for r in res.results]
    return np.ascontiguousarray(np.stack(outs, axis=0).astype(np.float32))


def prep_mixer_shared(inp):
    f = lambda a: np.ascontiguousarray(a, dtype=np.float32)
    sh = {}
    q0, ff0, fb0, v0, g0, u0, hy0, tq0, tk0, tv0 = 0, 256, 512, 768, 1024, 1280, 1536, 2304, 2560, 2688
    cols = []
    for h in range(4):
        for base in (q0, ff0, fb0, v0):
            c = [base + 64 * h + k for k in range(64)]
            cols += c + c
    cols += list(range(g0, g0 + 256)) + list(range(u0, u0 + 256))
    for kv in range(2):
        for par in range(2):
            cols += [tq0 + (2 * kv + hl) * 64 + 2 * i + par for hl in range(2) for i in range(32)]
    for kv in range(2):
        for par in range(2):
            cols += [tk0 + kv * 64 + 2 * i + par for hl in range(2) for i in range(32)]
    cols += list(range(hy0, hy0 + 768)) + list(range(tv0, tv0 + 128))
    cols += list(range(q0, q0 + 256)) + list(range(ff0, ff0 + 256)) + list(range(fb0, fb0 + 256)) + list(range(v0, v0 + 256))
    assert len(cols) == 39 * 128
    sh["w_mx"] = f(inp["w_in"][:, :, np.array(cols)])
    L = LL
    row = np.repeat(np.arange(L // 64), 64).astype(np.float32)
    col = np.tile(np.arange(64), L // 64).astype(np.float32)
    inv = (1.0 / (10000.0 ** (np.arange(0, 32, 2, dtype=np.float32) / 32.0))).astype(np.float32)
    ang = np.concatenate([row[:, None] * inv, col[:, None] * inv], axis=-1).astype(np.float32)
    cs, sn = np.cos(ang).T, np.sin(ang).T
    sh["ropeC"] = f(np.tile(cs, (4, 1)))
    sh["ropeS"] = f(np.concatenate([np.tile(sn, (2, 1)), -np.tile(sn, (2, 1))], axis=0))
    sk = np.zeros((DEPTH, 128, 2), np.float32)
    for kv in range(2):
        for r in range(128):
            sk[:, r, kv] = inp["att_sink"][:, 2 * kv + r // 64]
    sh["sinkp"] = sk
    s_ = np.arange(128)[:, None]
    t_ = np.arange(128)[None, :]
    sh["cmask"] = f(np.stack([(s_ >= t_), (s_ <= t_)], axis=1))
    r = np.arange(128)
    sh["hmask"] = f(np.stack([((r % 64) // 32 == 0), ((r % 64) // 32 == 1)], axis=1))
    op = np.zeros((128, 2, 128), np.float32)
    op[:, 0, 0:64] = 1.0
    op[:, 1, 64:128] = 1.0
    sh["onesp"] = op
    def gp(a):
        return f(a.reshape(DEPTH, 2, 8, 2, 64).transpose(0, 3, 4, 1, 2).reshape(DEPTH, 128, 2, 8))
    sh["s5_are"] = gp(inp["s5_a_re"])
    sh["s5_aim"] = gp(inp["s5_a_im"])
    sh["s5_ls"] = gp(np.broadcast_to(inp["s5_log_step"][:, :, :, None], (DEPTH, 2, 16, 64)))
    def bp(a):
        return f(a.reshape(DEPTH, 8, 2, 64, 16).transpose(0, 2, 3, 1, 4).reshape(DEPTH, 128, 8, 16))
    sh["s5_bre"] = bp(inp["s5_b_re"])
    sh["s5_bim"] = bp(inp["s5_b_im"])
    def cpad(a):
        o = np.zeros((DEPTH, 2, 64, 2, 8, 128), np.float32)
        for j in range(8):
            for gl in range(2):
                m0 = 32 * (j % 4) + 16 * gl
                o[:, gl, :, :, j, m0:m0 + 16] = a[:, :, 2 * j + gl, :, :].transpose(0, 3, 1, 2)
        return f(o.reshape(DEPTH, 128, 2, 8, 128))
    sh["s5_cre"] = cpad(inp["s5_c_re"])
    sh["s5_cim"] = cpad(inp["s5_c_im"])
    sh["s5_dv"] = f(inp["s5_d"].reshape(DEPTH, 2, 128).transpose(0, 2, 1))
    sh["s5_glu"] = f(inp["s5_glu_w"])
    sh["ident"] = f(np.eye(128))
    sh["iota"] = f(np.broadcast_to(np.arange(1024, dtype=np.float32)[None, :], (128, 1024)))
    lg = inp["hg_lb_logits"].reshape(DEPTH, 2, 4, 64)
    lg = lg.transpose(3, 1, 2, 0)
    sh["hg_lbl"] = f(np.concatenate([lg, lg], axis=0))
    E = np.zeros((128, 4096), np.float32)
    for k in range(64):
        E[k, 64 * k:64 * k + 64] = 1.0
    sh["hg_E"] = E
    S2 = np.zeros((128, 32, 64), np.float32)
    for j in range(32):
        S2[0:64, j, 2 * j] = 1.0
        S2[64:128, j, 2 * j + 1] = 1.0
    sh["hg_S2"] = S2.reshape(128, 2048)
    lg2 = inp["hg_lb_logits"].reshape(DEPTH, 2, 2, 128)
    sh["hg2_lbl"] = f(lg2.transpose(3, 1, 2, 0))
    s_ = np.arange(128)[:, None]
    t_ = np.arange(128)[None, :]
    same = (s_ // 16) == (t_ // 16)
    sh["hg2_mask"] = f(np.stack([same & (s_ <= t_), same & (s_ >= t_)], axis=1))
    sh["hg2_rowmask"] = f((np.arange(128)[:, None] // 16) == np.arange(8)[None, :])
    sh.update(prep_hyena_shared(inp))
    return sh


def prep_hyena_shared(inp):
    f = lambda a: np.ascontiguousarray(a, dtype=np.float32)
    sh = {}
    for L, tag in ((LL, "l"), (LC, "c")):
        t01 = np.linspace(0.0, 1.0, L, dtype=np.float32)[:, None]
        w = (2.0 * np.float32(math.pi) * np.arange(L, dtype=np.float32)[:, None] / np.float32(L)).astype(np.float32)
        fr = np.linspace(1e-4, 15.0, 16, dtype=np.float32)[None, :]
        z = np.concatenate([t01, np.cos(fr * w), -np.sin(fr * w)], axis=-1).astype(np.float32)
        sh["hy_zemb_" + tag] = f(z.T)
        hmin = math.log(1e-2) / 1.5
        hmax = math.log(1e-2) / 0.3
        deltas = np.linspace(hmin, hmax, 256, dtype=np.float32)
        win = np.exp(-t01 * np.abs(deltas)[None, :]).astype(np.float32)
        sh["hy_win_" + tag] = f(win.T)
    sh["hy_w1"] = f(inp["hy_w1"])
    sh["hy_b1"] = f(inp["hy_b1"][:, :, None])
    sh["hy_fr"] = f(inp["hy_freq"].transpose(0, 2, 1))
    sh["hy_w2"] = f(inp["hy_w2"])
    sh["hy_b2"] = f(inp["hy_b2"][:, :, None])
    sh["hy_w3"] = f(inp["hy_w3"])
    sh["hy_cw"] = f(inp["hy_conv_w"])
    sh["hy_cb"] = f(inp["hy_conv_b"])
    sh["hy_bias"] = f(inp["hy_bias"])
    sh["jmat"] = f(np.eye(128)[::-1])
    return sh
```

```python
import contextlib
import math
import numpy as np
import concourse.bass as bass
import concourse.mybir as mybir
from concourse.bass_utils import run_bass_kernel_spmd

F32 = mybir.dt.float32
BF16 = mybir.dt.bfloat16
I32 = mybir.dt.int32
ALU = mybir.AluOpType
AF = mybir.ActivationFunctionType

D = 1024
DEPTH = 4
LC = 256
LL = 4096
T = LC + LL
DFF = 2816
EPS = 1e-6
NCORES = 8
TILES = [(0, 256, 1)] + [(256 + 512 * i, 512, 0) for i in range(8)]
TP = T + 3


def pcol(c):
    return c + 1 if c < LC else c + 2


ENGS = ("pe", "act", "dve", "pool", "sp")
NDMA = 12


class Prog:
    def __init__(self, nc):
        self.nc = nc
        self.ops = {e: [] for e in ENGS}
        self.cnt = {e: 0 for e in ENGS}
        self.seen = {e: {} for e in ENGS}
        self.lastw = {}
        self.readers = {}
        self.dma_i = {e: 0 for e in ENGS}
        self.dma_val = {}
        self.sems = {}

    def _wait(self, eng, tok):
        sk, v = tok
        if sk == "pe" and eng == "pe":
            return
        if self.seen[eng].get(sk, 0) >= v:
            return
        self.seen[eng][sk] = v
        self.ops[eng].append(("wait", sk, v))

    def _deps(self, eng, reads, writes):
        for k in reads:
            if k in self.lastw:
                self._wait(eng, self.lastw[k])
        for k in writes:
            if k in self.lastw:
                self._wait(eng, self.lastw[k])
            for t in self.readers.get(k, ()):
                self._wait(eng, t)

    def _commit(self, tok, reads, writes):
        for k in reads:
            lst = self.readers.setdefault(k, [])
            lst.append(tok)
            if len(lst) > 32:
                d = {}
                for sk, v in lst:
                    d[sk] = max(d.get(sk, 0), v)
                self.readers[k] = list(d.items())
        for k in writes:
            self.lastw[k] = tok
            self.readers[k] = []

    def op(self, eng, fn, reads=(), writes=()):
        self._deps(eng, reads, writes)
        self.cnt[eng] += 1
        tok = (eng, self.cnt[eng])
        self.ops[eng].append(("op", fn))
        self._commit(tok, reads, writes)
        return tok

    def dma(self, eng, out, in_, reads=(), writes=(), **kw):
        self._deps(eng, reads, writes)
        i = self.dma_i[eng]
        self.dma_i[eng] += 1
        slot = (eng, i % NDMA)
        prev = self.dma_val.get(slot, 0)
        if prev:
            self._wait(eng, (slot, prev))
        val = prev + 16
        self.dma_val[slot] = val
        self.ops[eng].append(("dma", out, in_, slot, kw))
        tok = (slot, val)
        self._commit(tok, reads, writes)
        return tok

    def finish(self, toks):
        for t in toks:
            self._wait("sp", t)

    def emit(self):
        nc = self.nc
        with contextlib.ExitStack() as st:
            semkeys = list(ENGS)
            for e in ENGS:
                for s in range(NDMA):
                    if (e, s) in self.dma_val:
                        semkeys.append((e, s))
            for sk in semkeys:
                nm = sk if isinstance(sk, str) else "d_%s_%d" % sk
                self.sems[sk] = st.enter_context(nc.semaphore("s_" + nm))
            block = st.enter_context(nc.Block())

            def run(eng_obj, lst, ename):
                sem_own = self.sems[ename]
                for o in lst:
                    if o[0] == "wait":
                        eng_obj.wait_ge(self.sems[o[1]], o[2])
                    elif o[0] == "op":
                        o[1](eng_obj).then_inc(sem_own, 1)
                    else:
                        _, out, in_, slot, kw = o
                        eng_obj.dma_start(out=out, in_=in_, **kw).then_inc(self.sems[slot], 16)

            @block.tensor
            def _(e):
                run(e, self.ops["pe"], "pe")

            @block.scalar
            def _(e):
                run(e, self.ops["act"], "act")

            @block.vector
            def _(e):
                run(e, self.ops["dve"], "dve")

            @block.gpsimd
            def _(e):
                run(e, self.ops["pool"], "pool")

            @block.sync
            def _(e):
                run(e, self.ops["sp"], "sp")


def _barrier(P):
    toks = [(e, P.cnt[e]) for e in ENGS if P.cnt[e] > 0]
    toks += [(slot, v) for slot, v in P.dma_val.items()]
    for e in ENGS:
        for t in toks:
            P._wait(e, t)


ARENA_WORDS = 51200
XN_WORDS = 8 * TP // 2 + 4


class Builder:
    def __init__(self, depth=DEPTH, mixers=("hg", "s5", "hy", "at"), debug=()):
        self.depth = depth
        self.mixers = mixers
        self.debug = debug
        self.nc = bass.Bass("TRN2", target_bir_lowering=False)
        self.P = Prog(self.nc)
        self.st = contextlib.ExitStack()
        self.din = {}
        self.out_toks = []

    def inp(self, name, shape, dt=F32):
        self.din[name] = self.nc.dram_tensor(name, list(shape), dt, kind="ExternalInput").ap()
        return self.din[name]

    def dram(self, name, shape, dt=F32, kind="Internal"):
        return self.nc.dram_tensor(name, list(shape), dt, kind=kind).ap()

    def phase(self):
        _barrier(self.P)
        self.ptr = self.base
        self.nphase += 1

    def alloc(self, words, name):
        a = self.ptr
        self.ptr += (words + 1) // 2 * 2
        assert self.ptr <= ARENA_WORDS, (name, self.ptr)
        key = "%s@%d" % (name, self.nphase)
        return self.arena[:, a:a + words], key

    def f32(self, shape, name):
        n = int(np.prod(shape))
        v, k = self.alloc(n, name)
        if len(shape) == 2:
            v = v.rearrange("p (a b) -> p a b", a=shape[0])
        elif len(shape) == 3:
            v = v.rearrange("p (a b c) -> p a b c", a=shape[0], b=shape[1])
        return v, k

    def bf(self, shape, name):
        n = int(np.prod(shape))
        v, k = self.alloc((n + 1) // 2, name)
        v = v.bitcast(BF16)[:, 0:n]
        if len(shape) == 2:
            v = v.rearrange("p (a b) -> p a b", a=shape[0])
        elif len(shape) == 3:
            v = v.rearrange("p (a b c) -> p a b c", a=shape[0], b=shape[1])
        return v, k

    def act(self, out, in_, func, reads, writes, **kw):
        return self.P.op("act", lambda e: e.activation(out, in_, func, **kw), reads, writes)

    def tt(self, eng, out, a, b, op, reads, writes):
        return self.P.op(eng, lambda e: e.tensor_tensor(out, a, b, op), reads, writes)

    def ts(self, eng, out, a, s1, s2, op0, op1, reads, writes):
        if s2 is None:
            return self.P.op(eng, lambda e: e.tensor_scalar(out, a, s1, None, op0), reads, writes)
        return self.P.op(eng, lambda e: e.tensor_scalar(out, a, s1, s2, op0, op1), reads, writes)

    def stt(self, eng, out, a, s, b, op0, op1, reads, writes):
        return self.P.op(eng, lambda e: e.scalar_tensor_tensor(out, a, s, b, op0, op1), reads, writes)

    def cp(self, eng, out, in_, reads, writes):
        return self.P.op(eng, lambda e: e.tensor_copy(out, in_), reads, writes)

    def mm(self, out, pairs, reads, writes):
        n = len(pairs)
        for i, (l, r) in enumerate(pairs):
            self.P.op("pe", lambda e, l=l, r=r, i=i: e.matmul(out, l, r, start=(i == 0), stop=(i == n - 1)),
                      reads, writes)

    def dma(self, out, in_, reads, writes, eng="sp"):
        return self.P.dma(eng, out, in_, reads, writes)

    def build(self):
        nc, P = self.nc, self.P
        inp = self.inp
        xin = inp("xin", [D, T])
        cvec = inp("cvec", [128, 8, 2])
        ada_w = inp("ada_w", [DEPTH, D, 6 * D])
        ada_b = inp("ada_b", [DEPTH, 128, 48])
        nmw = inp("nmw", [DEPTH, 128, 8])
        nfw = inp("nfw", [DEPTH, 128, 8])
        fnw = inp("fnw", [128, 8])
        self.w_in = inp("w_in", [DEPTH, D, 2816])
        self.w_out = inp("w_out", [DEPTH, D, D])
        self.w_up = inp("w_up", [DEPTH, D, 2 * DFF])
        self.w_dn = inp("w_dn", [DEPTH, DFF, D])
        fcw = inp("fcw", [DEPTH, 128, 3, 22])
        fcb = inp("fcb", [DEPTH, 128, 22])
        mnw = inp("mnw", [DEPTH, 128, 8])
        self.declare_mixer_inputs()
        out = self.dram("out", [D, LL], kind="ExternalOutput")
        X = self.dram("Xs", [D, T])
        self.X = X
        self.MIX = self.dram("MIXs", [1536, T], kind=("ExternalOutput" if getattr(self, "mix_debug", False) else "Internal"))
        self.H = self.dram("Hs", [DFF, T], BF16)
        self.dbg = {}
        for name, shape in self.debug:
            self.dbg[name] = self.dram("dbg_" + name, shape, kind="ExternalOutput")

        self.arena = self.st.enter_context(nc.sbuf_tensor("arena", [128, ARENA_WORDS], F32))
        self.q = [self.st.enter_context(nc.psum_tensor("q%d" % i, [128, 1024], F32)) for i in range(4)]
        self.qk = ["q%d" % i for i in range(4)]
        self.ptr = 0
        self.nphase = 0
        xnv, _ = self.alloc(XN_WORDS, "xn")
        self.xn = xnv.bitcast(BF16)[:, 0:8 * TP].rearrange("p (k n) -> p k n", k=8)
        ob, _ = self.alloc(64, "ones")
        self.ones_bf = ob.bitcast(BF16)
        self.mods, _ = self.f32([48, 2], "mods")
        self.gain1, _ = self.f32([8, 2], "gain1")
        self.gain2, _ = self.f32([8, 2], "gain2")
        cc, _ = self.f32([8, 2], "cc")
        scc, _ = self.f32([8, 2], "scc")
        adab, _ = self.alloc(48, "adab")
        nmw_t, _ = self.alloc(8, "nmw_t")
        nfw_t, _ = self.alloc(8, "nfw_t")
        fnw_t, _ = self.alloc(8, "fnw_t")
        self.mnw_t, _ = self.alloc(8, "mnw_t")
        self.fcw_t, _ = self.f32([3, 22], "fcw_t")
        self.fcb_t, _ = self.alloc(22, "fcb_t")
        self.alloc_mixer_persistent()
        self.base = self.ptr
        mods, gain1, gain2 = self.mods, self.gain1, self.gain2
        q, qk = self.q, self.qk

        P.op("pool", lambda e: e.memset(self.ones_bf, 1.0), [], ["ones_bf"])
        P.op("pool", lambda e: e.memset(self.xn, 0.0), [], ["xn"])
        self.dma(cc, cvec, [], ["cc"])
        self.dma(fnw_t, fnw, [], ["fnw_t"])
        self.act(scc, cc, AF.Silu, ["cc"], ["scc"])

        src = xin
        for l in range(self.depth):
            self.phase()
            self.dma(adab, ada_b[l], [], ["adab"])
            self.dma(nmw_t, nmw[l], [], ["nmw_t"])
            self.dma(nfw_t, nfw[l], [], ["nfw_t"])
            self.dma(self.mnw_t, mnw[l], [], ["mnw_t"])
            self.dma(self.fcw_t, fcw[l], [], ["fcw_t"])
            self.dma(self.fcb_t, fcb[l], [], ["fcb_t"])
            stg = [self.f32([8, 512], "astage%d" % i) for i in range(2)]
            awv = ada_w[l].rearrange("(k p) n -> p k n", p=128)
            for og in range(12):
                sv, sk = stg[og % 2]
                self.dma(sv, awv[:, :, og * 512:(og + 1) * 512], [], [sk])
                pa, pk = q[og % 2], qk[og % 2]
                for j in range(4):
                    self.mm(pa[:, 2 * j:2 * j + 2],
                            [(sv[:, k, 128 * j:128 * j + 128], scc[:, k, :]) for k in range(8)], [sk, "scc"], [pk])
                self.tt("dve", mods[:, og * 4:(og + 1) * 4, :], pa[:, 0:8].rearrange("p (j v) -> p j v", v=2),
                        adab[:, og * 4:(og + 1) * 4].unsqueeze(2).to_broadcast([128, 4, 2]), ALU.add,
                        [pk, "adab"], ["mods"])
            self.stt("dve", gain1, mods[:, 8:16, :], 1.0, nmw_t.unsqueeze(2).to_broadcast([128, 8, 2]),
                     ALU.add, ALU.mult, ["mods", "nmw_t"], ["gain1"])
            self.stt("dve", gain2, mods[:, 32:40, :], 1.0, nfw_t.unsqueeze(2).to_broadcast([128, 8, 2]),
                     ALU.add, ALU.mult, ["mods", "nfw_t"], ["gain2"])
            self.norm_phase(src, gain1, "gain1", 0)
            self.mixers_phase(l)
            self.phase_o(l, src)
            src = X
            self.norm_phase(X, gain2, "gain2", 24)
            self.ffn_up_phase(l)
            self.ffn_down_phase(l)
        self.final_norm(fnw_t, out)
        P.finish(self.out_toks)
        P.emit()
        self.st.close()
        return nc

    def rms_rstd(self, xtile, xkey, n, nchunks, rstd, rt, sqb, pq, pqk, eps=EPS):
        self.act(sqb[:, 0:nchunks, 0:n], xtile[:, 0:nchunks, 0:n], AF.Square, [xkey], ["sqb"])
        self.mm(pq[:, 0:n], [(self.ones_bf, sqb[:, k, 0:n]) for k in range(nchunks)], ["sqb", "ones_bf"], [pqk])
        self.act(rt[:, 0:n], pq[:, 0:n], AF.Sqrt, [pqk], ["rt"], bias=eps, scale=1.0 / (128 * nchunks))
        self.P.op("dve", lambda e: e.reciprocal(rstd[:, 0:n], rt[:, 0:n]), ["rt"], ["rstd"])

    def norm_phase(self, src, gain, gkey, shift_base):
        self.phase()
        xts = [self.f32([8, 512], "xt%d" % i) for i in range(2)]
        sqb, _ = self.bf([8, 512], "sqb")
        rstd, _ = self.alloc(512, "rstd")
        rt, _ = self.alloc(512, "rt")
        for pc_ in (0, LC + 1, TP - 1):
            self.P.op("pool", lambda e, pc_=pc_: e.memset(self.xn[:, :, pc_:pc_ + 1], 0.0), [], ["xn"])
        for ti, (c0, n, isc) in enumerate(TILES):
            xtile, xk = xts[ti % 2]
            self.dma(xtile[:, :, 0:n], src[:, c0:c0 + n].rearrange("(k p) n -> p k n", p=128), [("X", ti)], [xk])
            pq, pqk = self.q[ti % 2], self.qk[ti % 2]
            self.rms_rstd(xtile, xk, n, 8, rstd, rt, sqb, pq, pqk)
            self.tt("dve", xtile[:, :, 0:n], xtile[:, :, 0:n], rstd[:, 0:n].unsqueeze(1).to_broadcast([128, 8, n]),
                    ALU.mult, [xk, "rstd"], [xk])
            pc = pcol(c0)
            for k in range(8):
                self.act(self.xn[:, k, pc:pc + n], xtile[:, k, 0:n], AF.Identity, [xk, gkey, "mods"], ["xn"],
                         scale=gain[:, k, isc:isc + 1], bias=self.mods[:, shift_base + k, isc:isc + 1])

    def load_w_chunk(self, wsrc_cols, stg, wbf, use_act):
        sv, sk = stg
        wv, wk = wbf
        self.dma(sv, wsrc_cols.rearrange("(k p) n -> p k n", p=128), [], [sk])
        if use_act:
            self.act(wv, sv, AF.Copy, [sk], [wk])
        else:
            self.cp("pool", wv, sv, [sk], [wk])

    def proj_rows(self, wv, wk, dst_fn, m=128):
        for ti, (c0, n, isc) in enumerate(TILES):
            pq, pqk = self.q[2 + ti % 2], self.qk[2 + ti % 2]
            pc = pcol(c0)
            self.mm(pq[0:m, 0:n], [(wv[:, k, 0:m], self.xn[:, k, pc:pc + n]) for k in range(8)], [wk, "xn"], [pqk])
            dst_fn(ti, c0, n, isc, pq[0:m, 0:n], pqk)

    def phase_o(self, l, src):
        self.phase()
        stg = [self.f32([8, 512], "ostage%d" % i) for i in range(2)]
        wo, wok = self.bf([8, 1024], "wo")
        for i in range(2):
            sv, sk = stg[i]
            self.dma(sv, self.w_out[l][:, i * 512:(i + 1) * 512].rearrange("(k p) n -> p k n", p=128), [], [sk])
            self.act(wo[:, :, i * 512:(i + 1) * 512], sv, AF.Copy, [sk], [wok])
        mixt, mk = self.f32([12, 512], "mixt")
        mixb, mbk = self.bf([8, 512], "mixb")
        sqb, _ = self.bf([2, 512], "sqb")
        rstd, _ = self.alloc(512, "rstd")
        rt, _ = self.alloc(512, "rt")
        sg, sgk = self.f32([2, 512], "silug")
        xts = [self.f32([8, 512], "oxt%d" % i) for i in range(2)]
        MIXv = self.MIX.rearrange("(k p) n -> p k n", p=128)
        for ti, (c0, n, isc) in enumerate(TILES):
            xtile, xk = xts[ti % 2]
            self.dma(xtile[:, :, 0:n], src[:, c0:c0 + n].rearrange("(k p) n -> p k n", p=128), [("X", ti)], [xk])
            self.dma(mixt[:, :, 0:n], MIXv[:, :, c0:c0 + n], [("MIX", ti)], [mk])
            self.tt("dve", mixt[:, 0:2, 0:n], mixt[:, 0:2, 0:n], mixt[:, 2:4, 0:n], ALU.add, [mk], [mk])
            self.act(sg[:, :, 0:n], mixt[:, 10:12, 0:n], AF.Silu, [mk], [sgk])
            for g, c in enumerate((0, 4, 6, 8)):
                pq, pqk = self.q[g % 2], self.qk[g % 2]
                self.rms_rstd(mixt[:, c:c + 2, :], mk, n, 2, rstd, rt, sqb, pq, pqk)
                for j in range(2):
                    self.stt("dve", mixt[:, c + j, 0:n], mixt[:, c + j, 0:n], self.mnw_t[:, 2 * g + j:2 * g + j + 1],
                             rstd[:, 0:n], ALU.mult, ALU.mult, [mk, "rstd", "mnw_t"], [mk])
                    if g == 0:
                        self.tt("dve", mixb[:, j, 0:n], mixt[:, j, 0:n], sg[:, j, 0:n], ALU.mult, [mk, sgk], [mbk])
                    else:
                        self.cp("pool", mixb[:, 2 * g + j, 0:n], mixt[:, c + j, 0:n], [mk], [mbk])
            for oc in range(8):
                pq, pqk = self.q[2 + oc % 2], self.qk[2 + oc % 2]
                self.mm(pq[:, 0:n], [(wo[:, k, oc * 128:(oc + 1) * 128], mixb[:, k, 0:n]) for k in range(8)],
                        [wok, mbk], [pqk])
                self.stt("dve", xtile[:, oc, 0:n], pq[:, 0:n], self.mods[:, 16 + oc, isc:isc + 1], xtile[:, oc, 0:n],
                         ALU.mult, ALU.add, [pqk, xk, "mods"], [xk])
            self.dma(self.X[:, c0:c0 + n].rearrange("(k p) n -> p k n", p=128), xtile[:, :, 0:n], [xk], [("X", ti)])

    def ffn_up_phase(self, l):
        self.phase()
        stg = [[self.f32([8, 128], "ustage%d%d" % (i, j)) for j in range(2)] for i in range(2)]
        wbf = [[self.bf([8, 128], "uw%d%d" % (i, j)) for j in range(2)] for i in range(2)]
        afs = [self.alloc(T, "a_full%d" % i) for i in range(2)]
        cfs = [self.alloc(T, "c_full%d" % i) for i in range(2)]
        hb = [self.alloc(T // 2, "hb%d" % i) for i in range(2)]
        Hv = self.H
        segs = [(0, LC), (LC, T)]
        def part_a(oc):
            pb = oc % 2
            a_full, ak = afs[pb]
            c_full, ck = cfs[pb]
            self.load_w_chunk(self.w_up[l][:, oc * 128:(oc + 1) * 128], stg[pb][0], wbf[pb][0], True)
            self.load_w_chunk(self.w_up[l][:, DFF + oc * 128:DFF + (oc + 1) * 128], stg[pb][1], wbf[pb][1], False)
            wa, wak = wbf[pb][0]
            w0 = self.fcw_t[:, 0, oc:oc + 1]
            w1 = self.fcw_t[:, 1, oc:oc + 1]
            w2 = self.fcw_t[:, 2, oc:oc + 1]
            def dst_a(ti, c0, n, isc, ps, pk, a_full=a_full, ak=ak):
                self.act(a_full[:, c0:c0 + n], ps, AF.Copy, [pk], [ak])
            self.proj_rows(wa, wak, dst_a)
            self.act(c_full, a_full, AF.Identity, [ak, "fcw_t", "fcb_t"], [ck], scale=w1, bias=self.fcb_t[:, oc:oc + 1])
            for (s_, e_) in segs:
                self.stt("dve", c_full[:, s_ + 1:e_], a_full[:, s_:e_ - 1], w0, c_full[:, s_ + 1:e_], ALU.mult, ALU.add,
                         [ak, ck, "fcw_t"], [ck])
                self.stt("dve", c_full[:, s_:e_ - 1], a_full[:, s_ + 1:e_], w2, c_full[:, s_:e_ - 1], ALU.mult, ALU.add,
                         [ak, ck, "fcw_t"], [ck])
            self.act(c_full, c_full, AF.Silu, [ck], [ck])

        def part_v(oc):
            pb = oc % 2
            c_full, ck = cfs[pb]
            hv = hb[pb][0].bitcast(BF16)
            hk = hb[pb][1]
            wv_, wvk = wbf[pb][1]
            def dst_v(ti, c0, n, isc, ps, pk, c_full=c_full, ck=ck, hv=hv, hk=hk):
                self.tt("dve", hv[:, c0:c0 + n], ps, c_full[:, c0:c0 + n], ALU.mult, [pk, ck], [hk])
            self.proj_rows(wv_, wvk, dst_v)
            self.dma(Hv[oc * 128:(oc + 1) * 128, :], hv, [hk], [("H", oc)])

        part_a(0)
        for oc in range(22):
            if oc + 1 < 22:
                part_a(oc + 1)
            part_v(oc)

    def ffn_down_phase(self, l):
        self.phase()
        stg = [self.f32([8, 512], "dstage%d" % i) for i in range(2)]
        wd = self.arena[:, 0:11264].bitcast(BF16).rearrange("p (k n) -> p k n", k=22)
        wdk = "xn"
        i = 0
        for k0, kn in ((0, 8), (8, 8), (16, 6)):
            for c in range(2):
                sv, sk = stg[i % 2]
                self.dma(sv[:, 0:kn, :], self.w_dn[l][k0 * 128:(k0 + kn) * 128, c * 512:(c + 1) * 512]
                         .rearrange("(k p) n -> p k n", p=128), [], [sk])
                if i % 2:
                    self.act(wd[:, k0:k0 + kn, c * 512:(c + 1) * 512], sv[:, 0:kn, :], AF.Copy, [sk], [wdk])
                else:
                    self.cp("pool", wd[:, k0:k0 + kn, c * 512:(c + 1) * 512], sv[:, 0:kn, :], [sk], [wdk])
                i += 1
        hts = [self.bf([22, 512], "ht%d" % i) for i in range(2)]
        xts = [self.f32([8, 512], "dxt%d" % i) for i in range(2)]
        Hv = self.H.rearrange("(k p) n -> p k n", p=128)
        for ti, (c0, n, isc) in enumerate(TILES):
            xtile, xk = xts[ti % 2]
            ht, hk = hts[ti % 2]
            self.dma(xtile[:, :, 0:n], self.X[:, c0:c0 + n].rearrange("(k p) n -> p k n", p=128), [("X", ti)], [xk])
            self.dma(ht[:, :, 0:n], Hv[:, :, c0:c0 + n], [("H", oc) for oc in range(22)], [hk])
            for oc in range(8):
                pq, pqk = self.q[oc % 4], self.qk[oc % 4]
                self.mm(pq[:, 0:n], [(wd[:, k, oc * 128:(oc + 1) * 128], ht[:, k, 0:n]) for k in range(22)],
                        [wdk, hk], [pqk])
                self.stt("dve", xtile[:, oc, 0:n], pq[:, 0:n], self.mods[:, 40 + oc, isc:isc + 1], xtile[:, oc, 0:n],
                         ALU.mult, ALU.add, [pqk, xk, "mods"], [xk])
            self.dma(self.X[:, c0:c0 + n].rearrange("(k p) n -> p k n", p=128), xtile[:, :, 0:n], [xk], [("X", ti)])

    def final_norm(self, fnw_t, out):
        self.phase()
        xts = [self.f32([8, 512], "fxt%d" % i) for i in range(2)]
        sqb, _ = self.bf([8, 512], "sqb")
        rstd, _ = self.alloc(512, "rstd")
        rt, _ = self.alloc(512, "rt")
        for ti, (c0, n, isc) in enumerate(TILES):
            if isc:
                continue
            xtile, xk = xts[ti % 2]
            self.dma(xtile[:, :, 0:n], self.X[:, c0:c0 + n].rearrange("(k p) n -> p k n", p=128), [("X", ti)], [xk])
            pq, pqk = self.q[ti % 2], self.qk[ti % 2]
            self.rms_rstd(xtile, xk, n, 8, rstd, rt, sqb, pq, pqk)
            for k in range(8):
                self.stt("dve", xtile[:, k, 0:n], xtile[:, k, 0:n], fnw_t[:, k:k + 1], rstd[:, 0:n], ALU.mult, ALU.mult,
                         [xk, "rstd", "fnw_t"], [xk])
            tok = self.dma(out[:, c0 - LC:c0 - LC + n].rearrange("(k p) n -> p k n", p=128), xtile[:, :, 0:n], [xk], [])
            self.out_toks.append(tok)

    CH_HG = 0
    CH_G = 16
    CH_U = 18
    CH_AQ = 20
    CH_AK = 22
    CH_HY = 24
    CH_TV = 30
    CH_N = 31
    NCH = 39

    def declare_mixer_inputs(self):
        inp = self.inp
        self.w_mx = inp("w_mx", [DEPTH, D, self.NCH * 128])
        self.ropeC = inp("ropeC", [128, LL])
        self.ropeS = inp("ropeS", [128, LL])
        self.sinkp = inp("sinkp", [DEPTH, 128, 2])
        self.cmask = inp("cmask", [128, 2, 128])
        self.hmask = inp("hmask", [128, 2])
        self.onesp = inp("onesp", [128, 2, 128])
        self.s5_are = inp("s5_are", [DEPTH, 128, 2, 8])
        self.s5_aim = inp("s5_aim", [DEPTH, 128, 2, 8])
        self.s5_ls = inp("s5_ls", [DEPTH, 128, 2, 8])
        self.s5_bre = inp("s5_bre", [DEPTH, 128, 8, 16])
        self.s5_bim = inp("s5_bim", [DEPTH, 128, 8, 16])
        self.s5_cre = inp("s5_cre", [DEPTH, 128, 2, 8, 128])
        self.s5_cim = inp("s5_cim", [DEPTH, 128, 2, 8, 128])
        self.s5_dv = inp("s5_dv", [DEPTH, 128, 2])
        self.s5_glu = inp("s5_glu", [DEPTH, 256, 256])
        self.ident_in = inp("ident", [128, 128])
        self.iota_in = inp("iota", [128, 1024])
        self.hg_lbl = inp("hg_lbl", [128, 2, 4, DEPTH])
        self.hg_E = inp("hg_E", [128, 4096])
        self.hg_S2 = inp("hg_S2", [128, 2048])
        self.hg2_lbl = inp("hg2_lbl", [128, 2, 2, DEPTH])
        self.hg2_mask = inp("hg2_mask", [128, 2, 128])
        self.hg2_rowmask = inp("hg2_rowmask", [128, 8])
        self.declare_hyena_inputs()

    def alloc_mixer_persistent(self):
        self.lbt, _ = self.f32([2, 4, DEPTH], "lbt")
        self.oml, _ = self.f32([2, 4, DEPTH], "oml")
        self.lbt2, _ = self.f32([2, 2, DEPTH], "lbt2")
        self.oml2, _ = self.f32([2, 2, DEPTH], "oml2")

    def mixer_setup(self):
        lg, k = self.f32([2, 4, DEPTH], "lbl")
        sm, sk = self.f32([2, 4], "lbsum")
        self.dma(lg, self.hg_lbl, [], [k])
        self.act(lg, lg, AF.Exp, [k], [k])
        self.P.op("dve", lambda e: e.tensor_reduce(sm, lg, mybir.AxisListType.X, ALU.add), [k], [sk])
        self.P.op("dve", lambda e: e.reciprocal(sm, sm), [sk], [sk])
        self.tt("dve", lg, lg, sm.unsqueeze(3).to_broadcast([128, 2, 4, DEPTH]), ALU.mult, [k, sk], [k])
        self.P.op("dve", lambda e: e.memset(self.lbt[:, :, :, 0:1], 0.0), [], ["lbt"])
        for l in range(1, DEPTH):
            self.tt("dve", self.lbt[:, :, :, l:l + 1], self.lbt[:, :, :, l - 1:l], lg[:, :, :, l:l + 1], ALU.add,
                    [k, "lbt"], ["lbt"])
        self.ts("dve", self.oml, self.lbt, -1.0, 1.0, ALU.mult, ALU.add, ["lbt"], ["oml"])
        lg2, k2 = self.f32([2, 2, DEPTH], "lbl2")
        sm2, sk2 = self.f32([2, 2], "lbsum2")
        self.dma(lg2, self.hg2_lbl, [], [k2])
        self.act(lg2, lg2, AF.Exp, [k2], [k2])
        self.P.op("dve", lambda e: e.tensor_reduce(sm2, lg2, mybir.AxisListType.X, ALU.add), [k2], [sk2])
        self.P.op("dve", lambda e: e.reciprocal(sm2, sm2), [sk2], [sk2])
        self.tt("dve", lg2, lg2, sm2.unsqueeze(3).to_broadcast([128, 2, 2, DEPTH]), ALU.mult, [k2, sk2], [k2])
        self.P.op("dve", lambda e: e.memset(self.lbt2[:, :, :, 0:1], 0.0), [], ["lbt2"])
        for l in range(1, DEPTH):
            self.tt("dve", self.lbt2[:, :, :, l:l + 1], self.lbt2[:, :, :, l - 1:l], lg2[:, :, :, l:l + 1], ALU.add,
                    [k2, "lbt2"], ["lbt2"])
        self.ts("dve", self.oml2, self.lbt2, -1.0, 1.0, ALU.mult, ALU.add, ["lbt2"], ["oml2"])

    def wrot_init(self, nbuf=3):
        self.wrot = [(self.f32([8, 128], "wst%d" % i), self.bf([8, 128], "wbf%d" % i)) for i in range(nbuf)]
        self.wrot_i = 0

    def wchunk(self, l, ci):
        stg, wbf = self.wrot[self.wrot_i % len(self.wrot)]
        self.wrot_i += 1
        self.load_w_chunk(self.w_mx[l][:, ci * 128:(ci + 1) * 128], stg, wbf, self.wrot_i % 2 == 0)
        return wbf

    def wchunk_own(self, l, ci, name):
        stg = self.f32([8, 128], name + "_st")
        wbf = self.bf([8, 128], name + "_bf")
        self.load_w_chunk(self.w_mx[l][:, ci * 128:(ci + 1) * 128], stg, wbf, True)
        return wbf

    def zero_mix_rows(self, r0, r1):
        z, zk = self.alloc(T, "zero")
        self.P.op("pool", lambda e: e.memset(z, 0.0), [], [zk])
        for r in range(r0, r1):
            self.dma(self.MIX[r * 128:(r + 1) * 128, :], z, [zk], [("MIX", ti) for ti in range(len(TILES))])

    def mixers_phase(self, l):
        if l == 0:
            self.phase()
            self.mixer_setup()
        allk = [("MIX", ti) for ti in range(len(TILES))]
        self.phase()
        if "hg" in self.mixers:
            self.mixer_hg(l, allk)
        else:
            self.zero_mix_rows(0, 4)
        self.phase()
        self.wrot_init()
        for c in range(2):
            wv, wk = self.wchunk(l, self.CH_G + c)
            gb, gk = self.alloc(T, "gfull%d" % c)
            def dst(ti, c0, n, isc, ps, pk, gb=gb, gk=gk):
                self.act(gb[:, c0:c0 + n], ps, AF.Copy, [pk], [gk])
            self.proj_rows(wv, wk, dst)
            self.dma(self.MIX[1280 + c * 128:1280 + (c + 1) * 128, :], gb, [gk], allk)
        self.phase()
        if "s5" in self.mixers:
            self.mixer_s5(l, allk)
        else:
            self.zero_mix_rows(4, 6)
        self.phase()
        if "hy" in self.mixers:
            self.mixer_hy(l, allk)
        else:
            self.zero_mix_rows(6, 8)
        self.phase()
        if "at" in self.mixers:
            self.mixer_at(l, allk)
        else:
            self.zero_mix_rows(8, 10)

    def mixer_at(self, l, allk):
        P = self.P
        self.wrot_init()
        HW = 2048
        Ch, ck = self.alloc(HW, "ropeC")
        Sh, sk = self.alloc(HW, "ropeS")
        cm32, cmk = self.f32([2, 128], "cm32")
        cmask, cmbk = self.bf([2, 128], "cmask")
        self.dma(cm32, self.cmask, [], [cmk])
        self.cp("dve", cmask, cm32, [cmk], [cmbk])
        hm, hmk = self.alloc(2, "hmask")
        self.dma(hm, self.hmask, [], [hmk])
        op32, opk = self.f32([2, 128], "op32")
        onesp, onk = self.bf([2, 128], "onesp")
        self.dma(op32, self.onesp, [], [opk])
        self.cp("dve", onesp, op32, [opk], [onk])
        esink, esk = self.alloc(2, "esink")
        self.dma(esink, self.sinkp[l], [], [esk])
        self.act(esink, esink, AF.Exp, [esk], [esk])
        raw, rk = self.alloc(T, "raw")
        tb, tbk = self.alloc(HW, "ropeB")
        tsw, tswk = self.alloc(HW, "ropeSw")
        Qb, qbk = self.bf([T], "Qb")
        Km = [self.bf([T], "Km%d" % i) for i in range(2)]
        Vp, vpk = self.bf([34, 2, 128], "Vp")
        PT = [self.bf([640], "PT%d" % i) for i in range(2)]
        atb = [self.alloc(128, "atb%d" % i) for i in range(2)]
        dn, dnk = self.alloc(128, "den")
        wtv, wtvk = self.wchunk_own(l, self.CH_TV, "wtv")
        P.op("pool", lambda e: e.memset(Vp, 0.0), [], [vpk])

        def rope_raw():
            for half in range(2):
                c0 = LC + half * HW
                self.dma(Ch, self.ropeC[:, half * HW:(half + 1) * HW], [], [ck])
                self.dma(Sh, self.ropeS[:, half * HW:(half + 1) * HW], [], [sk])
                self.tt("pool", tb, raw[:, c0:c0 + HW], Sh, ALU.mult, [rk, sk], [tbk])
                self.cp("dve", tsw[0:64, :], tb[64:128, :], [tbk], [tswk])
                self.cp("dve", tsw[64:128, :], tb[0:64, :], [tbk], [tswk])
                self.tt("dve", raw[:, c0:c0 + HW], raw[:, c0:c0 + HW], Ch, ALU.mult, [rk, ck], [rk])
                self.tt("dve", raw[:, c0:c0 + HW], raw[:, c0:c0 + HW], tsw, ALU.add, [rk, tswk], [rk])

        def dstq(ti, c0, n, isc, ps, pk):
            self.act(raw[:, c0:c0 + n], ps, AF.Copy, [pk], [rk])

        for kv in range(2):
            wq, wqk = self.wchunk(l, self.CH_AQ + kv)
            self.proj_rows(wq, wqk, dstq)
            rope_raw()
            self.cp("pool", Qb, raw, [rk], [qbk])
            wk_, wkk = self.wchunk(l, self.CH_AK + kv)
            self.proj_rows(wk_, wkk, dstq)
            rope_raw()
            for hl in range(2):
                kmv, kmk = Km[hl]
                self.ts("dve", kmv, raw, hm[:, hl:hl + 1], None, ALU.mult, None, [rk, hmk], [kmk])
            for b0 in range(0, 34, 8):
                nb = min(8, 34 - b0)
                pq, pqk = self.q[3], self.qk[3]
                for bi in range(nb):
                    blk = b0 + bi
                    pc = pcol(blk * 128)
                    self.mm(pq[:, bi * 64:(bi + 1) * 64],
                            [(self.xn[:, k, pc:pc + 128], wtv[:, k, kv * 64:(kv + 1) * 64]) for k in range(8)],
                            ["xn", wtvk], [pqk])
                src = pq[:, 0:nb * 64].rearrange("p (b d) -> p b d", d=64)
                self.act(Vp[:, b0:b0 + nb, 0, 0:64], src, AF.Copy, [pqk], [vpk])
                self.cp("dve", Vp[:, b0:b0 + nb, 1, 64:128], src, [pqk], [vpk])
            for qb in range(34):
                q0 = qb * 128
                kblocks = [(0, None), (1, None)]
                if qb >= 2:
                    if qb - 1 >= 2:
                        kblocks.append((qb - 1, 0))
                    kblocks.append((qb, None))
                    if qb + 1 < 34:
                        kblocks.append((qb + 1, 1))
                nkb = len(kblocks)
                po, pok = self.q[2], self.qk[2]
                nmm = 2 * nkb
                imm = 0
                for hl in range(2):
                    ps, psk = self.q[hl], self.qk[hl]
                    kmv, kmk = Km[hl]
                    ptv, ptk = PT[hl]
                    for bi, (kb, mk) in enumerate(kblocks):
                        self.mm(ps[:, bi * 128:(bi + 1) * 128],
                                [(kmv[:, kb * 128:(kb + 1) * 128], Qb[:, q0:q0 + 128])], [kmk, qbk], [psk])
                    self.act(ptv[:, 0:nkb * 128], ps[:, 0:nkb * 128], AF.Exp, [psk], [ptk], scale=0.125)
                    for bi, (kb, mk) in enumerate(kblocks):
                        if mk is not None:
                            self.tt("pool", ptv[:, bi * 128:(bi + 1) * 128], ptv[:, bi * 128:(bi + 1) * 128],
                                    cmask[:, mk, :], ALU.mult, [ptk, cmbk], [ptk])
                    for bi, (kb, mk) in enumerate(kblocks):
                        P.op("pe", lambda e, kb=kb, bi=bi, hl=hl, ptv=ptv, imm=imm, po=po, nmm=nmm: e.matmul(
                            po[:, 0:128], Vp[:, kb, hl, :], ptv[:, bi * 128:(bi + 1) * 128],
                            start=(imm == 0), stop=(imm == nmm - 1)), [vpk, ptk], [pok])
                        P.op("pe", lambda e, kb=kb, bi=bi, hl=hl, ptv=ptv, imm=imm, po=po, nmm=nmm: e.matmul(
                            po[:, 512:640], onesp[:, hl, :], ptv[:, bi * 128:(bi + 1) * 128],
                            start=(imm == 0), stop=(imm == nmm - 1)), [onk, ptk], [pok])
                        imm += 1
                av, avk = atb[qb % 2]
                self.ts("dve", dn, po[:, 512:640], esink[:, kv:kv + 1], None, ALU.add, None, [pok, esk], [dnk])
                P.op("dve", lambda e: e.reciprocal(dn, dn), [dnk], [dnk])
                self.tt("dve", av, po[:, 0:128], dn, ALU.mult, [pok, dnk], [avk])
                tix = 0 if qb < 2 else 1 + (qb - 2) // 4
                self.dma(self.MIX[1024 + kv * 128:1024 + (kv + 1) * 128, q0:q0 + 128], av, [avk], [("MIX", tix)])

    def mixer_s5(self, l, allk):
        P = self.P
        TWO_PI = 2.0 * math.pi
        self.wrot_init(1)
        SEG = 512
        segs = [(0, LC)] + [(LC + SEG * i, SEG) for i in range(LL // SEG)]
        def small(name, shape=(2, 8)):
            return self.f32(list(shape), "s5" + name)
        are, k_are = small("are"); aim, k_aim = small("aim"); ls, k_ls = small("ls")
        self.dma(are, self.s5_are[l], [], [k_are])
        self.dma(aim, self.s5_aim[l], [], [k_aim])
        self.dma(ls, self.s5_ls[l], [], [k_ls])
        dt, k_dt = small("dt"); mag, k_mag = small("mag"); th, k_th = small("th")
        tmp, k_tmp = small("tmp"); tmp2, k_tmp2 = small("tmp2")
        ti_v, k_ti = self.alloc(16, "s5ti")
        ti = ti_v.bitcast(I32).rearrange("p (a b) -> p a b", a=2)
        cosv, k_cos = small("cosv"); sinv, k_sin = small("sinv")
        fr, k_fr = small("fr"); fi, k_fi = small("fi")
        self.act(dt, ls, AF.Exp, [k_ls], [k_dt])
        self.tt("dve", tmp, are, dt, ALU.mult, [k_are, k_dt], [k_tmp])
        self.act(mag, tmp, AF.Exp, [k_tmp], [k_mag])
        self.tt("dve", th, aim, dt, ALU.mult, [k_aim, k_dt], [k_th])
        self.ts("dve", th, th, 1.0 / TWO_PI, None, ALU.mult, None, [k_th], [k_th])

        def sin_turns(dst, dkey, src, skey, shift, shp_tmp, k_t, tint, k_i, shp_tmp2, k_t2):
            self.ts("dve", shp_tmp, src, shift, None, ALU.add, None, [skey], [k_t])
            self.cp("dve", tint, shp_tmp, [k_t], [k_i])
            self.cp("dve", shp_tmp2, tint, [k_i], [k_t2])
            self.tt("dve", shp_tmp, shp_tmp, shp_tmp2, ALU.subtract, [k_t, k_t2], [k_t])
            self.act(dst, shp_tmp, AF.Sin, [k_t], [dkey], scale=TWO_PI)

        sin_turns(sinv, k_sin, th, k_th, 0.0, tmp, k_tmp, ti, k_ti, tmp2, k_tmp2)
        sin_turns(cosv, k_cos, th, k_th, 0.25, tmp, k_tmp, ti, k_ti, tmp2, k_tmp2)
        abre, k_abre = small("abre"); abim, k_abim = small("abim")
        self.tt("dve", abre, mag, cosv, ALU.mult, [k_mag, k_cos], [k_abre])
        self.tt("dve", abim, mag, sinv, ALU.mult, [k_mag, k_sin], [k_abim])
        den, k_den = small("den")
        self.tt("dve", den, are, are, ALU.mult, [k_are], [k_den])
        self.tt("dve", tmp, aim, aim, ALU.mult, [k_aim], [k_tmp])
        self.tt("dve", den, den, tmp, ALU.add, [k_den, k_tmp], [k_den])
        P.op("dve", lambda e: e.reciprocal(den, den), [k_den], [k_den])
        nr, k_nr = small("nr")
        self.ts("dve", nr, abre, -1.0, None, ALU.add, None, [k_abre], [k_nr])
        self.tt("dve", fr, nr, are, ALU.mult, [k_nr, k_are], [k_fr])
        self.tt("dve", tmp, abim, aim, ALU.mult, [k_abim, k_aim], [k_tmp])
        self.tt("dve", fr, fr, tmp, ALU.add, [k_fr, k_tmp], [k_fr])
        self.tt("dve", fr, fr, den, ALU.mult, [k_fr, k_den], [k_fr])
        self.tt("dve", fi, abim, are, ALU.mult, [k_abim, k_are], [k_fi])
        self.tt("dve", tmp, nr, aim, ALU.mult, [k_nr, k_aim], [k_tmp])
        self.tt("dve", fi, fi, tmp, ALU.subtract, [k_fi, k_tmp], [k_fi])
        self.tt("dve", fi, fi, den, ALU.mult, [k_fi, k_den], [k_fi])
        bre, k_bre = self.f32([8, 16], "s5bre"); bim, k_bim = self.f32([8, 16], "s5bim")
        self.dma(bre, self.s5_bre[l], [], [k_bre])
        self.dma(bim, self.s5_bim[l], [], [k_bim])
        dv, k_dv = self.alloc(2, "s5dv")
        self.dma(dv, self.s5_dv[l], [], [k_dv])
        ident, k_id = self.alloc(128, "ident")
        self.dma(ident, self.ident_in, [], [k_id])
        iota, k_io = self.alloc(SEG, "iota")
        self.dma(iota, self.iota_in[:, 0:SEG], [], [k_io])
        glu32 = self.f32([2, 256], "glu32")
        glub, k_glub = self.bf([2, 256], "glub")
        self.dma(glu32[0], self.s5_glu[l].rearrange("(k p) n -> p k n", p=128), [], [glu32[1]])
        self.cp("dve", glub, glu32[0], [glu32[1]], [k_glub])
        U, k_u = self.f32([2, T], "s5U")
        Y, k_y = self.f32([2, T], "s5Y")
        P.op("pool", lambda e: e.memset(Y, 0.0), [], [k_y])
        for c in range(2):
            wv, wk = self.wchunk(l, self.CH_U + c)
            def dst(ti_, c0, n, isc, ps, pk, c=c):
                self.act(U[:, c, c0:c0 + n], ps, AF.Copy, [pk], [k_u])
            self.proj_rows(wv, wk, dst)
        bb1, k_bb1 = self.alloc(16, "bb1"); bb2, k_bb2 = self.alloc(16, "bb2")
        Bpad, k_bp = self.alloc(128, "Bpad")
        BBts = [self.f32([2, 128], "BBt%d" % i) for i in range(2)]
        Cres = [self.alloc(128, "Cre%d" % i) for i in range(2)]
        Cims = [self.alloc(128, "Cim%d" % i) for i in range(2)]
        RBs = [self.alloc(SEG, "RB%d" % i) for i in range(2)]
        sts = [self.alloc(2, "s5st%d" % i) for i in range(2)]
        T1, k1 = self.alloc(SEG, "s5T1"); T2, k2 = self.alloc(SEG, "s5T2"); T3, k3 = self.alloc(SEG, "s5T3")
        TF, ktf = self.alloc(SEG, "s5TF")
        TIv, k_TI = self.alloc(SEG, "s5TI")
        TI = TIv.bitcast(I32)
        CSs = [self.alloc(SEG, "s5CS%d" % i) for i in range(2)]
        SNs = [self.alloc(SEG, "s5SN%d" % i) for i in range(2)]
        ZRZ, _ = self.alloc(2 * SEG, "s5ZRZ")
        ZRs = [(ZRZ[:, 0:SEG], "s5ZR0k"), (ZRZ[:, SEG:2 * SEG], "s5ZR1k")]
        ZIs = [self.alloc(SEG, "s5ZI%d" % i) for i in range(2)]
        U1, ku1 = self.alloc(SEG, "s5U1"); U2, ku2 = self.alloc(SEG, "s5U2"); U3, ku3 = self.alloc(SEG, "s5U3")

        def V(ap, c0, n, rev):
            a_ = ap[:, c0:c0 + n]
            return a_[:, ::-1] if rev else a_

        its = []
        for d in range(2):
            order = list(range(len(segs))) if d == 0 else [0] + list(range(len(segs) - 1, 0, -1))
            for j in range(8):
                for oi, si in enumerate(order):
                    its.append((d, j, oi, si))

        def prep_dj(d, j):
            pb = (d * 8 + j) % 2
            BBt, k_bbt = BBts[pb]
            m0 = 32 * (j % 4)
            for ri in range(2):
                x1, kx1 = (bre, k_bre) if ri == 0 else (bim, k_bim)
                x2, kx2 = (bim, k_bim) if ri == 0 else (bre, k_bre)
                self.ts("dve", bb1, x1[:, j, :], fr[:, d, j:j + 1], None, ALU.mult, None, [kx1, k_fr], [k_bb1])
                self.ts("dve", bb2, x2[:, j, :], fi[:, d, j:j + 1], None, ALU.mult, None, [kx2, k_fi], [k_bb2])
                self.tt("dve", bb1, bb1, bb2, ALU.subtract if ri == 0 else ALU.add, [k_bb1, k_bb2], [k_bb1])
                P.op("dve", lambda e: e.memset(Bpad, 0.0), [], [k_bp])
                self.cp("dve", Bpad[0:64, m0:m0 + 16], bb1[0:64, :], [k_bb1], [k_bp])
                self.cp("dve", Bpad[64:128, m0 + 16:m0 + 32], bb1[64:128, :], [k_bb1], [k_bp])
                pq, pqk = self.q[3], self.qk[3]
                P.op("pe", lambda e, pq=pq: e.transpose(pq[:, 512:640], Bpad, ident), [k_bp, k_id], [(pqk, "tr")])
                self.act(BBt[:, ri, :], pq[:, 512:640], AF.Copy, [(pqk, "tr")], [k_bbt])
            Cre, k_cre = Cres[pb]
            Cim, k_cim = Cims[pb]
            self.dma(Cre, self.s5_cre[l][:, d, j, :], [], [k_cre])
            self.dma(Cim, self.s5_cim[l][:, d, j, :], [], [k_cim])
            RB, k_rb = RBs[pb]
            self.act(RB, iota, AF.Identity, [k_io, k_mag], [k_rb], scale=0.0, bias=mag[:, d, j:j + 1])
            st, k_st = sts[pb]
            P.op("dve", lambda e, st=st: e.memset(st, 0.0), [], [k_st])

        def stage_a(idx):
            d, j, oi, si = its[idx]
            if oi == 0:
                prep_dj(d, j)
            pb = (d * 8 + j) % 2
            b = idx % 2
            rev = (d == 1)
            c0, n = segs[si]
            n0 = c0 if d == 0 else (0 if si == 0 else LC + (T - c0 - n))
            BBt, k_bbt = BBts[pb]
            urhs = V(U[:, j // 4, :], c0, n, rev)
            pp, kpp = self.q[b], self.qk[b]
            self.mm(pp[:, 0:n], [(BBt[:, 0, :], urhs)], [k_bbt, k_u], [kpp])
            self.mm(pp[:, 512:512 + n], [(BBt[:, 1, :], urhs)], [k_bbt, k_u], [kpp])
            CS, kc = CSs[b]
            SN, ks = SNs[b]
            P.op("dve", lambda e, n=n, n0=n0, d=d, j=j: e.tensor_scalar(
                T1[:, 0:n], iota[:, 0:n], float(n0), th[:, d, j:j + 1], ALU.add, ALU.mult), [k_io, k_th], [k1])
            for (dst_, kd, shift) in ((SN, ks, 0.0), (CS, kc, 0.25)):
                if shift:
                    self.ts("dve", T1[:, 0:n], T1[:, 0:n], shift, None, ALU.add, None, [k1], [k1])
                self.cp("dve", TI[:, 0:n], T1[:, 0:n], [k1], [k_TI])
                self.cp("dve", TF[:, 0:n], TI[:, 0:n], [k_TI], [ktf])
                self.tt("dve", T2[:, 0:n], T1[:, 0:n], TF[:, 0:n], ALU.subtract, [k1, ktf], [k2])
                self.act(dst_[:, 0:n], T2[:, 0:n], AF.Sin, [k2], [kd], scale=TWO_PI)

        def stage_b(idx):
            d, j, oi, si = its[idx]
            pb = (d * 8 + j) % 2
            b = idx % 2
            rev = (d == 1)
            c0, n = segs[si]
            pp, kpp = self.q[b], self.qk[b]
            pre = pp[:, 0:n]
            pim = pp[:, 512:512 + n]
            CS, kc = CSs[b]
            SN, ks = SNs[b]
            ZR, kzr = ZRs[b]
            ZI, kzi = ZIs[b]
            RB, k_rb = RBs[pb]
            st, k_st = sts[pb]
            Cre, k_cre = Cres[pb]
            Cim, k_cim = Cims[pb]
            self.tt("dve", T1[:, 0:n], pre, CS[:, 0:n], ALU.mult, [kpp, kc], [k1])
            self.tt("dve", T2[:, 0:n], pim, SN[:, 0:n], ALU.mult, [kpp, ks], [k2])
            self.tt("dve", T1[:, 0:n], T1[:, 0:n], T2[:, 0:n], ALU.add, [k1, k2], [k1])
            self.tt("dve", T3[:, 0:n], pim, CS[:, 0:n], ALU.mult, [kpp, kc], [k3])
            self.tt("dve", T2[:, 0:n], pre, SN[:, 0:n], ALU.mult, [kpp, ks], [k2])
            self.tt("dve", T3[:, 0:n], T3[:, 0:n], T2[:, 0:n], ALU.subtract, [k3, k2], [k3])
            P.op("dve", lambda e, n=n: e.tensor_tensor_scan(ZR[:, 0:n], RB[:, 0:n], T1[:, 0:n], st[:, 0:1],
                                                            ALU.mult, ALU.add), [k_rb, k1, k_st], [kzr])
            P.op("dve", lambda e, n=n: e.tensor_tensor_scan(ZI[:, 0:n], RB[:, 0:n], T3[:, 0:n], st[:, 1:2],
                                                            ALU.mult, ALU.add), [k_rb, k3, k_st], [kzi])
            self.act(st[:, 0:1], ZR[:, n - 1:n], AF.Copy, [kzr], [k_st])
            self.act(st[:, 1:2], ZI[:, n - 1:n], AF.Copy, [kzi], [k_st])
            self.tt("dve", U1[:, 0:n], ZR[:, 0:n], CS[:, 0:n], ALU.mult, [kzr, kc], [ku1])
            self.tt("dve", U2[:, 0:n], ZI[:, 0:n], SN[:, 0:n], ALU.mult, [kzi, ks], [ku2])
            self.tt("dve", U1[:, 0:n], U1[:, 0:n], U2[:, 0:n], ALU.subtract, [ku1, ku2], [ku1])
            self.tt("dve", U3[:, 0:n], ZR[:, 0:n], SN[:, 0:n], ALU.mult, [kzr, ks], [ku3])
            self.tt("dve", U2[:, 0:n], ZI[:, 0:n], CS[:, 0:n], ALU.mult, [kzi, kc], [ku2])
            self.stt("dve", U3[:, 0:n], U2[:, 0:n], -1.0, U3[:, 0:n], ALU.mult, ALU.subtract, [ku2, ku3], [ku3])
            pyv = self.q[2 + idx % 2][:, 0:n]
            kpyv = self.qk[2 + idx % 2]
            self.mm(pyv, [(Cre, U1[:, 0:n]), (Cim, U3[:, 0:n])], [k_cre, k_cim, ku1, ku3], [kpyv])
            if idx > 0:
                stage_c(idx - 1)

        def stage_c(idx):
            d, j, oi, si = its[idx]
            c0, n = segs[si]
            pyv = self.q[2 + idx % 2][:, 0:n]
            kpyv = self.qk[2 + idx % 2]
            yv = V(Y[:, j // 4, :], c0, n, (d == 1))
            self.tt("dve", yv, yv, pyv, ALU.add, [k_y, kpyv], [k_y])

        stage_a(0)
        for idx in range(len(its)):
            if idx + 1 < len(its):
                stage_a(idx + 1)
            stage_b(idx)
        stage_c(len(its) - 1)
        _barrier(P)
        zt, kz = ZRZ.rearrange("p (a b) -> p a b", a=2), "s5ztk"
        zbv, kzb = SNs[0]
        zb = zbv.bitcast(BF16).rearrange("p (a b) -> p a b", a=2)
        sg, ksg = SNs[1]
        ob = [CSs[0], CSs[1]]
        for ti_, (c0, n, isc) in enumerate(TILES):
            for c in range(2):
                self.stt("dve", zt[:, c, 0:n], U[:, c, c0:c0 + n], dv[:, c:c + 1], Y[:, c, c0:c0 + n], ALU.mult, ALU.add,
                         [k_u, k_y, k_dv], [kz])
            self.act(zt[:, :, 0:n], zt[:, :, 0:n], AF.Gelu, [kz], [kz])
            self.cp("pool", zb[:, :, 0:n], zt[:, :, 0:n], [kz], [kzb])
            for oc in range(2):
                pq, pqk = self.q[oc], self.qk[oc]
                self.mm(pq[:, 0:n], [(glub[:, k, oc * 128:(oc + 1) * 128], zb[:, k, 0:n]) for k in range(2)],
                        [k_glub, kzb], [pqk])
                self.act(sg[:, 0:n], pq[:, 0:n], AF.Sigmoid, [pqk], [ksg])
                ov, ok = ob[oc]
                self.tt("dve", ov[:, 0:n], zt[:, oc, 0:n], sg[:, 0:n], ALU.mult, [kz, ksg], [ok])
                self.dma(self.MIX[512 + oc * 128:512 + (oc + 1) * 128, c0:c0 + n], ov[:, 0:n], [ok], [("MIX", ti_)])

    CHK = 16

    def mixer_hg(self, l, allk):
        P = self.P
        C = self.CHK
        self.wrot_init(2)
        NB_ = T // 128
        m32, k_m32 = self.f32([2, 128], "g2m32")
        cmk, k_cmk = self.bf([2, 128], "g2cm")
        self.dma(m32, self.hg2_mask, [], [k_m32])
        self.cp("dve", cmk, m32, [k_m32], [k_cmk])
        rmask, k_rm = self.alloc(8, "g2rm")
        self.dma(rmask, self.hg2_rowmask, [], [k_rm])
        ident, k_id = self.alloc(128, "g2id")
        self.dma(ident, self.ident_in, [], [k_id])
        RST, k_rst = self.bf([512], "g2rst")
        P.op("pool", lambda e: e.memset(RST, 1.0), [], [k_rst])
        P.op("pool", lambda e: e.memset(RST.rearrange("p (a c) -> p a c", c=C)[:, :, 0:1], 0.0), [], [k_rst])
        Qp, k_q = self.alloc(T, "g2Q")
        Fb, k_f = self.alloc(T, "g2F")
        EC, k_ec = self.alloc(T, "g2EC")
        Bm, k_bm = self.alloc(T, "g2Bm")
        Ab, k_a = self.bf([T], "g2A")
        Bb, k_bb = self.bf([T], "g2Bb")
        Vt, k_v = self.bf([NB_, 128], "g2V")
        Vpad, k_vp = self.bf([2, 128], "g2Vpad")
        Vm = [self.bf([8, 2, 128], "g2Vm%d" % i) for i in range(2)]
        BmT = [self.bf([2, 128], "g2BmT%d" % i) for i in range(2)]
        PT = [self.bf([2, 128], "g2PT%d" % i) for i in range(2)]
        dsd = [self.f32([8, 128], "g2dsd%d" % i) for i in range(2)]
        S, k_s = self.alloc(128, "g2S")
        SPb = [self.bf([128], "g2SP%d" % i) for i in range(2)]
        OB = [self.alloc(128, "g2OB%d" % i) for i in range(2)]
        for t_, k_ in (Vm + BmT):
            P.op("pool", lambda e, t_=t_: e.memset(t_, 0.0), [], [k_])
        P.op("pool", lambda e: e.memset(Vpad, 0.0), [], [k_vp])

        def V(ap, c0, n, rev):
            a_ = ap[:, c0:c0 + n]
            return a_[:, ::-1] if rev else a_

        for hp in range(2):
            wq, wqk = self.wchunk(l, self.CH_N + 0 + hp)
            def dq(ti, c0, n, isc, ps, pk):
                self.act(Qp[:, c0:c0 + n], ps, AF.Silu, [pk], [k_q])
            self.proj_rows(wq, wqk, dq)
            wv, wvk = self.wchunk(l, self.CH_N + 6 + hp)
            for b0 in range(0, NB_, 4):
                nb = min(4, NB_ - b0)
                pq, pqk = self.q[3], self.qk[3]
                for bi in range(nb):
                    pc = pcol((b0 + bi) * 128)
                    self.mm(pq[:, bi * 128:(bi + 1) * 128],
                            [(self.xn[:, k, pc:pc + 128], wv[:, k, :]) for k in range(8)], ["xn", wvk], [pqk])
                self.act(Vt[:, b0:b0 + nb, :], pq[:, 0:nb * 128].rearrange("p (b d) -> p b d", d=128), AF.Copy,
                         [pqk], [k_v])
            for d in range(2):
                rev = (d == 1)
                wf, wfk = self.wchunk(l, self.CH_N + 2 + 2 * d + hp)
                def df(ti, c0, n, isc, ps, pk):
                    self.act(Fb[:, c0:c0 + n], ps, AF.Sigmoid, [pk], [k_f])
                self.proj_rows(wf, wfk, df)
                self.ts("dve", Fb, Fb, self.oml2[:, d, hp, l:l + 1], self.lbt2[:, d, hp, l:l + 1], ALU.mult, ALU.add,
                        [k_f, "oml2", "lbt2"], [k_f])
                self.act(EC, Fb, AF.Ln, [k_f], [k_ec])
                for (c0, n) in [(0, LC)] + [(LC + 512 * i, 512) for i in range(LL // 512)]:
                    P.op("dve", lambda e, c0=c0, n=n, rev=rev: e.tensor_tensor_scan(
                        V(EC, c0, n, rev), RST[:, 0:n], V(EC, c0, n, rev), 0.0, ALU.mult, ALU.add), [k_rst, k_ec], [k_ec])
                self.act(Bm, EC, AF.Exp, [k_ec], [k_bm], scale=-1.0)
                self.act(EC, EC, AF.Exp, [k_ec], [k_ec])
                self.tt("dve", Ab, Qp, EC, ALU.mult, [k_q, k_ec], [k_a])
                self.ts("dve", Fb, Fb, -1.0, 1.0, ALU.mult, ALU.add, [k_f], [k_f])
                self.tt("dve", Bm, Bm, Fb, ALU.mult, [k_bm, k_f], [k_bm])
                self.cp("pool", Bb, Bm, [k_bm], [k_bb])
                _barrier(P)
                P.op("dve", lambda e: e.memset(S, 0.0), [], [k_s])
                P.op("pool", lambda e: e.memset(SPb[0][0], 0.0), [], [SPb[0][1]])
                order = list(range(NB_)) if d == 0 else [1, 0] + list(range(NB_ - 1, 1, -1))
                spi = [0]

                def front(bi_):
                    blk = order[bi_]
                    b = bi_ % 2
                    cs = slice(blk * 128, (blk + 1) * 128)
                    pt_, kpt = self.q[3], (self.qk[3], "tr", b)
                    ptv = pt_[:, b * 128:(b + 1) * 128]
                    P.op("pe", lambda e, ptv=ptv, cs=cs: e.transpose(ptv, Bm[:, cs], ident), [k_bm, k_id], [kpt])
                    bt, kbt = BmT[b]
                    self.act(bt[:, 0, 0:64], ptv[:, 0:64], AF.Copy, [kpt], [kbt])
                    self.act(bt[:, 1, 64:128], ptv[:, 64:128], AF.Copy, [kpt], [kbt])
                    vm, kvm = Vm[b]
                    for a in range(2):
                        self.tt("dve", vm[:, :, a, a * 64:(a + 1) * 64],
                                Vt[:, blk, a * 64:(a + 1) * 64].unsqueeze(1).to_broadcast([128, 8, 64]),
                                rmask.unsqueeze(2).to_broadcast([128, 8, 64]), ALU.mult, [k_v, k_rm], [kvm])
                    pd, kpd = self.q[b], self.qk[b]
                    for hf_ in range(2):
                        self.mm(pd[:, hf_ * 512:(hf_ + 1) * 512].rearrange("p (c v) -> p c v", v=128),
                                [(bt[:, a, :], vm[:, hf_ * 4:(hf_ + 1) * 4, a, :]) for a in range(2)], [kbt, kvm], [kpd])
                    dv_, kdv = dsd[b]
                    for c8 in range(8):
                        col = blk * 128 + c8 * C + (C - 1 if d == 0 else 0)
                        self.act(dv_[:, c8, :], pd[:, c8 * 128:(c8 + 1) * 128], AF.Identity, [kpd, k_ec], [kdv],
                                 scale=EC[:, col:col + 1])
                    ps_, kps = self.q[2], (self.qk[2], b)
                    pv_, kpv = PT[b]
                    for a in range(2):
                        sl = slice(a * 64, (a + 1) * 64)
                        psv = ps_[:, (2 * b + a) * 128:(2 * b + a + 1) * 128]
                        self.mm(psv, [(Bb[sl, cs], Ab[sl, cs])], [k_bb, k_a], [kps])
                        self.tt("dve", pv_[:, a, :], psv, cmk[:, d, :], ALU.mult, [kps, k_cmk], [kpv])

                def back(bi_):
                    blk = order[bi_]
                    b = bi_ % 2
                    cs0 = blk * 128
                    pv_, kpv = PT[b]
                    dv_, kdv = dsd[b]
                    po, kpo = self.q[3], (self.qk[3], "o", b)
                    pov = po[:, 512 + b * 128:512 + (b + 1) * 128]
                    P.op("pool", lambda e, blk=blk: e.tensor_copy(Vpad[:, 0, 0:64], Vt[:, blk, 0:64]), [k_v], [k_vp])
                    P.op("pool", lambda e, blk=blk: e.tensor_copy(Vpad[:, 1, 64:128], Vt[:, blk, 64:128]), [k_v], [k_vp])
                    nmm = 2 + 8
                    im = 0
                    for a in range(2):
                        P.op("pe", lambda e, a=a, im=im, pov=pov, pv_=pv_: e.matmul(
                            pov, Vpad[:, a, :], pv_[:, a, :], start=(im == 0), stop=False), [k_vp, kpv], [kpo])
                        im += 1
                    cl = list(range(8)) if d == 0 else list(range(7, -1, -1))
                    for ci, c8 in enumerate(cl):
                        sp, ksp = SPb[spi[0] % 2]
                        P.op("pe", lambda e, c8=c8, sp=sp, pov=pov, ci=ci: e.matmul(
                            pov[:, c8 * C:(c8 + 1) * C], sp, Ab[:, cs0 + c8 * C:cs0 + (c8 + 1) * C],
                            start=False, stop=(ci == 7), skip_group_check=True), [ksp, k_a], [kpo])
                        col = cs0 + c8 * C + (C - 1 if d == 0 else 0)
                        self.stt("dve", S, S, EC[:, col:col + 1], dv_[:, c8, :], ALU.mult, ALU.add, [k_s, k_ec, kdv], [k_s])
                        spi[0] += 1
                        sp2, ksp2 = SPb[spi[0] % 2]
                        self.act(sp2, S, AF.Copy, [k_s], [ksp2])
                    ob, kob = OB[b]
                    self.cp("pool", ob, pov, [kpo], [kob]) if False else self.act(ob, pov, AF.Copy, [kpo], [kob])
                    tix = 0 if blk < 2 else 1 + (blk - 2) // 4
                    self.dma(self.MIX[d * 256 + hp * 128:d * 256 + (hp + 1) * 128, cs0:cs0 + 128], ob, [kob], [("MIX", tix)])

                front(0)
                for bi_ in range(NB_):
                    if bi_ + 1 < NB_:
                        front(bi_ + 1)
                    back(bi_)
                _barrier(P)

    def declare_hyena_inputs(self):
        inp = self.inp
        self.hy_zemb = {LL: inp("hy_zemb_l", [33, LL]), LC: inp("hy_zemb_c", [33, LC])}
        self.hy_win = {LL: inp("hy_win_l", [256, LL]), LC: inp("hy_win_c", [256, LC])}
        self.hy_w1 = inp("hy_w1", [DEPTH, 33, 64])
        self.hy_b1 = inp("hy_b1", [DEPTH, 64, 1])
        self.hy_fr = inp("hy_fr", [DEPTH, 64, 2])
        self.hy_w2 = inp("hy_w2", [DEPTH, 64, 64])
        self.hy_b2 = inp("hy_b2", [DEPTH, 64, 1])
        self.hy_w3 = inp("hy_w3", [DEPTH, 64, 1024])
        self.hy_cw = inp("hy_cw", [DEPTH, 3, 768])
        self.hy_cb = inp("hy_cb", [DEPTH, 768])
        self.hy_bias = inp("hy_bias", [DEPTH, 2, 256])
        self.jmat = inp("jmat", [128, 128])
        self.GK = {LL: self.dram("GKl", [2, 256, 2 * LL], BF16), LC: self.dram("GKc", [2, 256, 2 * LC], BF16)}

    def hy_filters(self, l):
        P = self.P
        TWO_PI = 2.0 * math.pi
        self.phase()
        w1, k_w1 = self.alloc(64, "hw1"); w2, k_w2 = self.alloc(64, "hw2"); w3, k_w3 = self.alloc(1024, "hw3")
        b1, k_b1 = self.alloc(1, "hb1"); b2, k_b2 = self.alloc(1, "hb2"); fr, k_fr = self.alloc(2, "hfr")
        self.dma(w1[0:33, :], self.hy_w1[l], [], [k_w1])
        self.dma(w2[0:64, :], self.hy_w2[l], [], [k_w2])
        self.dma(w3[0:64, :], self.hy_w3[l], [], [k_w3])
        self.dma(b1[0:64, :], self.hy_b1[l], [], [k_b1])
        self.dma(b2[0:64, :], self.hy_b2[l], [], [k_b2])
        self.dma(fr[0:64, :], self.hy_fr[l], [], [k_fr])
        self.ts("dve", fr[0:64, :], fr[0:64, :], 1.0 / TWO_PI, None, ALU.mult, None, [k_fr], [k_fr])
        ze, k_ze = self.alloc(512, "hze")
        t1, k_t1 = self.alloc(512, "ht1"); t2, k_t2 = self.alloc(512, "ht2")
        tiv, k_ti = self.alloc(512, "hti")
        tint = tiv.bitcast(I32)
        h1, k_h1 = self.alloc(512, "hh1")
        h2, k_h2 = self.alloc(LL, "hh2")
        hf, k_hf = self.alloc(LL, "hhf"); hb, k_hb = self.alloc(LL, "hhb")
        win, k_win = self.alloc(LL, "hwin")
        Gt, k_gt = self.bf([2 * LL], "hGt")

        def sin_layer(dst, kd, ps, kps, bias, kb, frs, n):
            P.op("dve", lambda e: e.tensor_scalar(t1[0:64, 0:n], ps, bias, frs, ALU.add, ALU.mult), [kps, kb, k_fr], [k_t1])
            self.cp("dve", tint[0:64, 0:n], t1[0:64, 0:n], [k_t1], [k_ti])
            self.cp("dve", t2[0:64, 0:n], tint[0:64, 0:n], [k_ti], [k_t2])
            self.tt("dve", t1[0:64, 0:n], t1[0:64, 0:n], t2[0:64, 0:n], ALU.subtract, [k_t1, k_t2], [k_t1])
            self.act(dst, t1[0:64, 0:n], AF.Sin, [k_t1], [kd], scale=TWO_PI)

        for L in (LL, LC):
            for c0 in range(0, L, 512):
                n = min(512, L - c0)
                self.dma(ze[0:33, 0:n], self.hy_zemb[L][:, c0:c0 + n], [], [k_ze])
                pq, pqk = self.q[0], self.qk[0]
                self.mm(pq[0:64, 0:n], [(w1[0:33, :], ze[0:33, 0:n])], [k_w1, k_ze], [pqk])
                sin_layer(h1[0:64, 0:n], k_h1, pq[0:64, 0:n], pqk, b1[0:64, 0:1], k_b1, fr[0:64, 0:1], n)
                pq2, pqk2 = self.q[1], self.qk[1]
                self.mm(pq2[0:64, 0:n], [(w2[0:64, :], h1[0:64, 0:n])], [k_w2, k_h1], [pqk2])
                sin_layer(h2[0:64, c0:c0 + n], k_h2, pq2[0:64, 0:n], pqk2, b2[0:64, 0:1], k_b2, fr[0:64, 1:2], n)
            for o in range(2):
                for chalf in range(2):
                    self.dma(win[:, 0:L], self.hy_win[L][chalf * 128:(chalf + 1) * 128, :], [], [k_win])
                    for di, (dst, kd) in enumerate(((hf, k_hf), (hb, k_hb))):
                        cc = o * 512 + di * 256 + chalf * 128
                        for c0 in range(0, L, 512):
                            n = min(512, L - c0)
                            pq, pqk = self.q[2 + (c0 // 512) % 2], self.qk[2 + (c0 // 512) % 2]
                            self.mm(pq[:, 0:n], [(w3[0:64, cc:cc + 128], h2[0:64, c0:c0 + n])], [k_w3, k_h2], [pqk])
                            self.tt("dve", dst[:, c0:c0 + n], pq[:, 0:n], win[:, c0:c0 + n], ALU.mult, [pqk, k_win], [kd])
                    self.cp("pool", Gt[:, 0:L - 1], hb[:, 1:L][:, ::-1], [k_hb], [k_gt])
                    self.tt("pool", Gt[:, L - 1:L], hf[:, 0:1], hb[:, 0:1], ALU.add, [k_hf, k_hb], [k_gt])
                    self.cp("pool", Gt[:, L:2 * L - 1], hf[:, 1:L], [k_hf], [k_gt])
                    P.op("pool", lambda e, L=L: e.memset(Gt[:, 2 * L - 1:2 * L], 0.0), [], [k_gt])
                    self.dma(self.GK[L][o, chalf * 128:(chalf + 1) * 128, :], Gt[:, 0:2 * L], [k_gt], [("GK", L)])

    def mixer_hy(self, l, allk):
        P = self.P
        self.hy_filters(l)
        self.phase()
        NBT = 34
        Wg, k_wg = self.bf([8, 3, 192], "yWg")
        stg, k_stg = self.f32([8, 192], "ystg")
        cwb, k_cwb = self.f32([3, 192], "ycwb")
        cbb, k_cbb = self.alloc(192, "ycbb")
        bsb, k_bsb = self.f32([2, 64], "ybsb")
        PJ, k_pj = self.f32([NBT, 192], "yPJ")
        Zl, k_zl = self.bf([94, 64], "yZl")
        Zc, k_zc = self.bf([4, 64], "yZc")
        zb, k_zb = self.bf([32, 64], "yzb")
        z1, k_z1 = self.f32([32, 64], "yz1")
        tmp, k_tmp = self.f32([32, 64], "ytmp")
        strips = [self.bf([2 * LL - 128], "ystrip%d" % i) for i in range(2)]
        j32, k_j32 = self.alloc(128, "yj32")
        Jb, k_jb = self.bf([128], "yJb")
        ident, k_id = self.alloc(128, "yident")
        OT = [self.alloc(512, "yOT%d" % i) for i in range(2)]
        self.dma(j32, self.jmat, [], [k_j32])
        self.cp("dve", Jb, j32, [k_j32], [k_jb])
        self.dma(ident, self.ident_in, [], [k_id])
        P.op("pool", lambda e: e.memset(Zl, 0.0), [], [k_zl])
        P.op("pool", lambda e: e.memset(Zc, 0.0), [], [k_zc])
        wv = self.w_mx[l][:, self.CH_HY * 128:(self.CH_HY + 6) * 128].rearrange("(k p) n -> p k n", p=128)
        si = 0
        for cg in range(4):
            for part in range(3):
                cs = part * 256 + cg * 64
                self.dma(stg[:, :, part * 64:(part + 1) * 64], wv[:, :, cs:cs + 64], [], [k_stg])
                self.dma(cwb[:, :, part * 64:(part + 1) * 64], self.hy_cw[l][:, cs:cs + 64].partition_broadcast(128), [], [k_cwb])
                self.dma(cbb[:, part * 64:(part + 1) * 64], self.hy_cb[l][cs:cs + 64].partition_broadcast(128), [], [k_cbb])
            self.dma(bsb, self.hy_bias[l][:, cg * 64:(cg + 1) * 64].partition_broadcast(128), [], [k_bsb])
            for tap in range(3):
                self.tt("dve", Wg[:, :, tap, :], stg, cwb[:, tap, :].unsqueeze(1).to_broadcast([128, 8, 192]), ALU.mult,
                        [k_stg, k_cwb], [k_wg])
            for a in range(NBT):
                pc = pcol(a * 128)
                pq, pqk = self.q[a % 2], self.qk[a % 2]
                self.mm(pq[:, 0:192], [(self.xn[:, kk, pc + tap - 1:pc + tap - 1 + 128], Wg[:, kk, tap, :])
                                       for tap in range(3) for kk in range(8)], ["xn", k_wg], [pqk])
                self.tt("dve", PJ[:, a, :], pq[:, 0:192], cbb, ALU.add, [pqk, k_cbb], [k_pj])
            for (blk0, nb, L, Zp, k_zp) in ((0, 2, LC, Zc, k_zc), (2, 32, LL, Zl, k_zl)):
                X1 = PJ[:, blk0:blk0 + nb, 0:64]
                X2 = PJ[:, blk0:blk0 + nb, 64:128]
                Z0 = PJ[:, blk0:blk0 + nb, 128:192]
                for o in range(2):
                    zprev = Z0 if o == 0 else z1[:, 0:nb, :]
                    kprev = k_pj if o == 0 else k_z1
                    gate = X1 if o == 0 else X2
                    self.cp("pool", zb[:, 0:nb, :], zprev, [kprev], [k_zb])
                    for a0 in range(0, nb, 8):
                        an = min(8, nb - a0)
                        pq, pqk = self.q[2], self.qk[2]
                        self.mm(pq[:, 0:an * 64], [(Jb, zb[:, a0:a0 + an, :])], [k_jb, k_zb], [pqk])
                        self.act(Zp[:, nb - 1 + a0:nb - 1 + a0 + an, :],
                                 pq[:, 0:an * 64].rearrange("p (a c) -> p a c", c=64), AF.Copy, [pqk], [k_zp])
                    for c16 in range(4):
                        pq, pqk = self.q[c16 % 2], self.qk[c16 % 2]
                        for ci in range(16):
                            c = c16 * 16 + ci
                            ch = cg * 64 + c
                            sv, k_sv = strips[si % 2]
                            si += 1
                            W_ = 2 * L - 128
                            gk = self.GK[L]
                            src = bass.AP(tensor=gk.tensor, offset=gk[o, ch, 0:1].offset, ap=[[1, 128], [1, W_]])
                            self.dma(sv[:, 0:W_], src, [("GK", L)], [k_sv])
                            nl = 2 * nb - 1
                            for di in range(nl):
                                d = di - (nb - 1)
                                P.op("pe", lambda e, pq=pq, ci=ci, nb=nb, sv=sv, di=di, d=d, Zp=Zp, c=c, nl=nl: e.matmul(
                                    pq[:, ci * nb:(ci + 1) * nb], sv[:, 128 * di:128 * di + 128],
                                    Zp[:, nb - 1 - d:nb - 1 - d + nb, c], start=(di == 0), stop=(di == nl - 1)),
                                    [k_sv, k_zp], [pqk])
                        cs_ = slice(c16 * 16, (c16 + 1) * 16)
                        tv = tmp[:, 0:nb, cs_]
                        self.tt("dve", tv, zprev[:, :, cs_], bsb[:, o, cs_].unsqueeze(1).to_broadcast([128, nb, 16]), ALU.mult,
                                [kprev, k_bsb], [k_tmp])
                        self.tt("dve", tv, tv, pq[:, 0:16 * nb].rearrange("p (c a) -> p a c", a=nb), ALU.add,
                                [k_tmp, pqk], [k_tmp])
                    if o == 0:
                        self.tt("dve", z1[:, 0:nb, :], tmp[:, 0:nb, :], gate, ALU.mult, [k_tmp, k_pj], [k_z1])
                    else:
                        self.tt("dve", tmp[:, 0:nb, :], tmp[:, 0:nb, :], gate, ALU.mult, [k_tmp, k_pj], [k_tmp])
                for a0 in range(0, nb, 4):
                    an = min(4, nb - a0)
                    pq, pqk = self.q[3], self.qk[3]
                    for ai in range(an):
                        P.op("pe", lambda e, pq=pq, ai=ai, a0=a0: e.transpose(
                            pq[0:64, ai * 128:(ai + 1) * 128], tmp[:, a0 + ai, :], ident), [k_tmp, k_id], [pqk])
                    ov, k_ov = OT[(a0 // 4) % 2]
                    self.act(ov[0:64, 0:an * 128], pq[0:64, 0:an * 128], AF.Copy, [pqk], [k_ov])
                    col = (blk0 + a0) * 128
                    self.dma(self.MIX[768 + cg * 64:768 + (cg + 1) * 64, col:col + an * 128], ov[0:64, 0:an * 128],
                             [k_ov], allk)


def _pk(v, k):
    return np.ascontiguousarray(v.reshape(k, 128).T)


def prep_shared(inp):
    f = lambda a: np.ascontiguousarray(a, dtype=np.float32)
    sh = {}
    sh["ada_w"] = f(inp["ada_w"])
    sh["ada_b"] = f(inp["ada_b"].reshape(DEPTH, 48, 128).transpose(0, 2, 1))
    sh["nmw"] = f(inp["norm_mix_w"].reshape(DEPTH, 8, 128).transpose(0, 2, 1))
    sh["nfw"] = f(inp["norm_ffn_w"].reshape(DEPTH, 8, 128).transpose(0, 2, 1))
    sh["fnw"] = f(_pk(inp["final_norm_w"], 8))
    sh["w_in"] = f(inp["w_in"])
    sh["w_out"] = f(inp["w_out"])
    sh["w_up"] = f(inp["ffn_w_up"])
    sh["w_dn"] = f(inp["ffn_w_down"])
    sh["fcw"] = f(inp["ffn_conv_w"].reshape(DEPTH, 3, 22, 128).transpose(0, 3, 1, 2))
    sh["fcb"] = f(inp["ffn_conv_b"].reshape(DEPTH, 22, 128).transpose(0, 2, 1))
    sh["mnw"] = f(inp["merge_norm_w"].reshape(DEPTH, 8, 128).transpose(0, 2, 1))
    return sh


def prep_core(inp, b):
    d = {}
    d["xin"] = np.ascontiguousarray(np.concatenate([inp["ctx"][b].T, inp["x"][b].T], axis=1), dtype=np.float32)
    cv = np.zeros((128, 8, 2), np.float32)
    cv[:, :, 0] = _pk(inp["c"][b], 8)
    cv[:, :, 1] = _pk(inp["c_ctx"], 8)
    d["cvec"] = cv
    return d


def kernel(**inputs):
    inp = {k: np.asarray(v) for k, v in inputs.items()}
    bld = Builder()
    nc = bld.build()
    sh = prep_shared(inp)
    sh.update(prep_mixer_shared(inp))
    in_maps = []
    for b in range(NCORES):
        d = dict(sh)
        d.update(prep_core(inp, b))
        in_maps.append({k: d[k] for k in bld.din})
    res = run_bass_kernel_spmd(nc, in_maps, core_ids=list(range(NCORES)))
    outs = [np.asarray(r["out"]).T for r in res.results]
    return np.ascontiguousarray(np.stack(outs, axis=0).astype(np.float32))


def prep_mixer_shared(inp):
    f = lambda a: np.ascontiguousarray(a, dtype=np.float32)
    sh = {}
    q0, ff0, fb0, v0, g0, u0, hy0, tq0, tk0, tv0 = 0, 256, 512, 768, 1024, 1280, 1536, 2304, 2560, 2688
    cols = []
    for h in range(4):
        for base in (q0, ff0, fb0, v0):
            c = [base + 64 * h + k for k in range(64)]
            cols += c + c
    cols += list(range(g0, g0 + 256)) + list(range(u0, u0 + 256))
    for kv in range(2):
        for par in range(2):
            cols += [tq0 + (2 * kv + hl) * 64 + 2 * i + par for hl in range(2) for i in range(32)]
    for kv in range(2):
        for par in range(2):
            cols += [tk0 + kv * 64 + 2 * i + par for hl in range(2) for i in range(32)]
    cols += list(range(hy0, hy0 + 768)) + list(range(tv0, tv0 + 128))
    cols += list(range(q0, q0 + 256)) + list(range(ff0, ff0 + 256)) + list(range(fb0, fb0 + 256)) + list(range(v0, v0 + 256))
    assert len(cols) == 39 * 128
    sh["w_mx"] = f(inp["w_in"][:, :, np.array(cols)])
    L = LL
    row = np.repeat(np.arange(L // 64), 64).astype(np.float32)
    col = np.tile(np.arange(64), L // 64).astype(np.float32)
    inv = (1.0 / (10000.0 ** (np.arange(0, 32, 2, dtype=np.float32) / 32.0))).astype(np.float32)
    ang = np.concatenate([row[:, None] * inv, col[:, None] * inv], axis=-1).astype(np.float32)
    cs, sn = np.cos(ang).T, np.sin(ang).T
    sh["ropeC"] = f(np.tile(cs, (4, 1)))
    sh["ropeS"] = f(np.concatenate([np.tile(sn, (2, 1)), -np.tile(sn, (2, 1))], axis=0))
    sk = np.zeros((DEPTH, 128, 2), np.float32)
    for kv in range(2):
        for r in range(128):
            sk[:, r, kv] = inp["att_sink"][:, 2 * kv + r // 64]
    sh["sinkp"] = sk
    s_ = np.arange(128)[:, None]
    t_ = np.arange(128)[None, :]
    sh["cmask"] = f(np.stack([(s_ >= t_), (s_ <= t_)], axis=1))
    r = np.arange(128)
    sh["hmask"] = f(np.stack([((r % 64) // 32 == 0), ((r % 64) // 32 == 1)], axis=1))
    op = np.zeros((128, 2, 128), np.float32)
    op[:, 0, 0:64] = 1.0
    op[:, 1, 64:128] = 1.0
    sh["onesp"] = op
    def gp(a):
        return f(a.reshape(DEPTH, 2, 8, 2, 64).transpose(0, 3, 4, 1, 2).reshape(DEPTH, 128, 2, 8))
    sh["s5_are"] = gp(inp["s5_a_re"])
    sh["s5_aim"] = gp(inp["s5_a_im"])
    sh["s5_ls"] = gp(np.broadcast_to(inp["s5_log_step"][:, :, :, None], (DEPTH, 2, 16, 64)))
    def bp(a):
        return f(a.reshape(DEPTH, 8, 2, 64, 16).transpose(0, 2, 3, 1, 4).reshape(DEPTH, 128, 8, 16))
    sh["s5_bre"] = bp(inp["s5_b_re"])
    sh["s5_bim"] = bp(inp["s5_b_im"])
    def cpad(a):
        o = np.zeros((DEPTH, 2, 64, 2, 8, 128), np.float32)
        for j in range(8):
            for gl in range(2):
                m0 = 32 * (j % 4) + 16 * gl
                o[:, gl, :, :, j, m0:m0 + 16] = a[:, :, 2 * j + gl, :, :].transpose(0, 3, 1, 2)
        return f(o.reshape(DEPTH, 128, 2, 8, 128))
    sh["s5_cre"] = cpad(inp["s5_c_re"])
    sh["s5_cim"] = cpad(inp["s5_c_im"])
    sh["s5_dv"] = f(inp["s5_d"].reshape(DEPTH, 2, 128).transpose(0, 2, 1))
    sh["s5_glu"] = f(inp["s5_glu_w"])
    sh["ident"] = f(np.eye(128))
    sh["iota"] = f(np.broadcast_to(np.arange(1024, dtype=np.float32)[None, :], (128, 1024)))
    lg = inp["hg_lb_logits"].reshape(DEPTH, 2, 4, 64)
    lg = lg.transpose(3, 1, 2, 0)
    sh["hg_lbl"] = f(np.concatenate([lg, lg], axis=0))
    E = np.zeros((128, 4096), np.float32)
    for k in range(64):
        E[k, 64 * k:64 * k + 64] = 1.0
    sh["hg_E"] = E
    S2 = np.zeros((128, 32, 64), np.float32)
    for j in range(32):
        S2[0:64, j, 2 * j] = 1.0
        S2[64:128, j, 2 * j + 1] = 1.0
    sh["hg_S2"] = S2.reshape(128, 2048)
    lg2 = inp["hg_lb_logits"].reshape(DEPTH, 2, 2, 128)
    sh["hg2_lbl"] = f(lg2.transpose(3, 1, 2, 0))
    s_ = np.arange(128)[:, None]
    t_ = np.arange(128)[None, :]
    same = (s_ // 16) == (t_ // 16)
    sh["hg2_mask"] = f(np.stack([same & (s_ <= t_), same & (s_ >= t_)], axis=1))
    sh["hg2_rowmask"] = f((np.arange(128)[:, None] // 16) == np.arange(8)[None, :])
    sh.update(prep_hyena_shared(inp))
    return sh


def prep_hyena_shared(inp):
    f = lambda a: np.ascontiguousarray(a, dtype=np.float32)
    sh = {}
    for L, tag in ((LL, "l"), (LC, "c")):
        t01 = np.linspace(0.0, 1.0, L, dtype=np.float32)[:, None]
        w = (2.0 * np.float32(math.pi) * np.arange(L, dtype=np.float32)[:, None] / np.float32(L)).astype(np.float32)
        fr = np.linspace(1e-4, 15.0, 16, dtype=np.float32)[None, :]
        z = np.concatenate([t01, np.cos(fr * w), -np.sin(fr * w)], axis=-1).astype(np.float32)
        sh["hy_zemb_" + tag] = f(z.T)
        hmin = math.log(1e-2) / 1.5
        hmax = math.log(1e-2) / 0.3
        deltas = np.linspace(hmin, hmax, 256, dtype=np.float32)
        win = np.exp(-t01 * np.abs(deltas)[None, :]).astype(np.float32)
        sh["hy_win_" + tag] = f(win.T)
    sh["hy_w1"] = f(inp["hy_w1"])
    sh["hy_b1"] = f(inp["hy_b1"][:, :, None])
    sh["hy_fr"] = f(inp["hy_freq"].transpose(0, 2, 1))
    sh["hy_w2"] = f(inp["hy_w2"])
    sh["hy_b2"] = f(inp["hy_b2"][:, :, None])
    sh["hy_w3"] = f(inp["hy_w3"])
    sh["hy_cw"] = f(inp["hy_conv_w"])
    sh["hy_cb"] = f(inp["hy_conv_b"])
    sh["hy_bias"] = f(inp["hy_bias"])
    sh["jmat"] = f(np.eye(128)[::-1])
    return sh
```
